# Optimizing a Trainium2 kernel written in Bass

```python
import jax, jax.numpy as jnp
from jax import lax
import numpy as np

D_MODEL = 1024
BATCH = 8
SEQ = 2048
DEPTH = 1

EPS = 1e-6
MLA_HEADS = 8
MLA_NOPE = 64
MLA_ROPE = 32
MLA_VDIM = 64
MLA_Q_RANK = 384
MLA_KV_RANK = 256
MLA_QK = MLA_NOPE + MLA_ROPE
MLA_WIDTH = MLA_HEADS * MLA_VDIM
ROPE_THETA = 10000.0
Q_BLOCK = 128
GLA_HEADS = 4
GLA_DK = D_MODEL // 2
GLA_DV = D_MODEL
GLA_HK = GLA_DK // GLA_HEADS
GLA_HV = GLA_DV // GLA_HEADS
GLA_GATE_RANK = 16
GLA_GATE_NORM = 16.0
GLA_CHUNK = 64
SPLITS = (MLA_Q_RANK, MLA_KV_RANK, MLA_ROPE, MLA_WIDTH,
          GLA_DK, GLA_DK, GLA_DV, GLA_GATE_RANK, GLA_DV,
          D_MODEL, D_MODEL)
D_IN = sum(SPLITS)

kernel_name = 'hybrid_mla_gla_block'


def rmsnorm(x, g):
    xf = x.astype(jnp.float32)
    y = xf * lax.rsqrt(jnp.mean(xf * xf, axis=-1, keepdims=True) + EPS)
    return (y * g.astype(jnp.float32)).astype(x.dtype)


def rope_tables(positions):
    half = MLA_ROPE // 2
    freqs = ROPE_THETA ** (-jnp.arange(half, dtype=jnp.float32) / half)
    ang = positions.astype(jnp.float32)[..., None] * freqs
    return jnp.cos(ang)[:, :, None, :], jnp.sin(ang)[:, :, None, :]


def apply_rope(x, cos, sin):
    half = MLA_ROPE // 2
    x1 = x[..., :half].astype(jnp.float32)
    x2 = x[..., half:].astype(jnp.float32)
    return jnp.concatenate([x1 * cos - x2 * sin, x2 * cos + x1 * sin], axis=-1).astype(x.dtype)


def mla_attention(q, k, v):
    B, S, H, _ = q.shape
    nb = S // Q_BLOCK
    scale = MLA_QK ** -0.5
    qb = q.reshape(B, nb, Q_BLOCK, H, MLA_QK).transpose(1, 0, 2, 3, 4)
    kpos = jnp.arange(S)

    def block(args):
        i, qi = args
        s = jnp.einsum('bqhd,bkhd->bhqk', qi, k, preferred_element_type=jnp.float32) * scale
        qpos = i * Q_BLOCK + jnp.arange(Q_BLOCK)
        s = jnp.where(kpos[None, :] <= qpos[:, None], s, -jnp.inf)
        p = jax.nn.softmax(s, axis=-1)
        return jnp.einsum('bhqk,bkhd->bqhd', p.astype(v.dtype), v)

    o = lax.map(block, (jnp.arange(nb), qb))
    return o.transpose(1, 0, 2, 3, 4).reshape(B, S, H, MLA_VDIM)


def gla_chunked(q, k, v, log_a):
    B, S, H, DK = q.shape
    DV = v.shape[-1]
    C = GLA_CHUNK
    N = S // C

    def chunks(t):
        return t.reshape(B, N, C, H, t.shape[-1]).transpose(1, 0, 3, 2, 4).astype(jnp.float32)

    qc = chunks(q) * (DK ** -0.5)
    kc, vc, gc = chunks(k), chunks(v), chunks(log_a)
    b = jnp.cumsum(gc, axis=3)
    b_last = b[:, :, :, -1:, :]
    q_in = qc * jnp.exp(b)
    k_in = kc * jnp.exp(-b)
    k_st = kc * jnp.exp(b_last - b)
    causal = jnp.tril(jnp.ones((C, C), jnp.float32))
    attn = jnp.einsum('nbhid,nbhjd->nbhij', q_in, k_in) * causal
    o_intra = jnp.einsum('nbhij,nbhjv->nbhiv', attn, vc)

    def step(state, inp):
        q_i, k_i, v_i, dec = inp
        o = jnp.einsum('bhid,bhdv->bhiv', q_i, state)
        state = dec[:, :, 0, :, None] * state + jnp.einsum('bhjd,bhjv->bhdv', k_i, v_i)
        return state, o

    s0 = jnp.zeros((B, H, DK, DV), jnp.float32)
    _, o_inter = lax.scan(step, s0, (q_in, k_st, vc, jnp.exp(b_last)))
    o = o_intra + o_inter
    return o.transpose(1, 0, 3, 2, 4).reshape(B, S, H, DV)


def setup_inputs(seed: int = 0) -> dict:
    key = jax.random.key(seed)
    ks = jax.random.split(key, 20)

    def w(k, shape, fan_in):
        return jax.random.normal(k, shape, jnp.float32) * fan_in ** -0.5

    def gain(k, shape):
        return 1.0 + 0.02 * jax.random.normal(k, shape, jnp.float32)

    x = jax.random.normal(ks[0], (BATCH, SEQ, D_MODEL), jnp.float32)
    positions = (jnp.arange(SEQ, dtype=jnp.int32)[None, :]
                 + jax.random.randint(ks[1], (BATCH, 1), 0, 64, dtype=jnp.int32))
    return {
        'x': x,
        'positions': positions,
        'g_in': gain(ks[2], (DEPTH, D_MODEL)),
        'w_in': w(ks[3], (DEPTH, D_MODEL, D_IN), D_MODEL),
        'g_q': gain(ks[4], (DEPTH, MLA_Q_RANK)),
        'w_uq': w(ks[5], (DEPTH, MLA_Q_RANK, MLA_HEADS * MLA_QK), MLA_Q_RANK),
        'g_kv': gain(ks[6], (DEPTH, MLA_KV_RANK)),
        'w_ukv': w(ks[7], (DEPTH, MLA_KV_RANK, MLA_HEADS * (MLA_NOPE + MLA_VDIM)), MLA_KV_RANK),
        'w_gla_gate': w(ks[8], (DEPTH, GLA_GATE_RANK, GLA_DK), GLA_GATE_RANK),
        'b_gla_gate': 0.01 * jax.random.normal(ks[9], (DEPTH, GLA_DK), jnp.float32),
        'g_gla': gain(ks[10], (DEPTH, GLA_HV)),
        'w_proj_mla': w(ks[11], (DEPTH, MLA_WIDTH, D_MODEL), MLA_WIDTH),
        'w_proj_gla': w(ks[12], (DEPTH, GLA_DV, D_MODEL), GLA_DV),
        'w_out': w(ks[13], (DEPTH, D_MODEL, D_MODEL), D_MODEL),
        'g_final': gain(ks[14], (D_MODEL,)),
    }


def reference(x, positions, g_in, w_in, g_q, w_uq, g_kv, w_ukv, w_gla_gate, b_gla_gate,
              g_gla, w_proj_mla, w_proj_gla, w_out, g_final):
    B, S, _ = x.shape
    cos, sin = rope_tables(positions)
    split_points = [int(p) for p in np.cumsum(SPLITS)[:-1]]
    for l in range(DEPTH):
        h = rmsnorm(x, g_in[l])
        proj = h @ w_in[l]
        (c_q, c_kv, k_r, z_mla, q_g, k_g, v_g, a_lr, z_gla,
         gate_mla, gate_gla) = jnp.split(proj, split_points, axis=-1)

        q = (rmsnorm(c_q, g_q[l]) @ w_uq[l]).reshape(B, S, MLA_HEADS, MLA_QK)
        q = jnp.concatenate([q[..., :MLA_NOPE], apply_rope(q[..., MLA_NOPE:], cos, sin)], axis=-1)
        kv = (rmsnorm(c_kv, g_kv[l]) @ w_ukv[l]).reshape(B, S, MLA_HEADS, MLA_NOPE + MLA_VDIM)
        k_nope, v_mla = kv[..., :MLA_NOPE], kv[..., MLA_NOPE:]
        k_rope = apply_rope(k_r[:, :, None, :], cos, sin)
        k = jnp.concatenate([k_nope, jnp.broadcast_to(k_rope, (B, S, MLA_HEADS, MLA_ROPE))], axis=-1)
        o_mla = mla_attention(q, k, v_mla).reshape(B, S, MLA_WIDTH)
        y_mla = (o_mla * jax.nn.silu(z_mla)) @ w_proj_mla[l]

        log_a = jax.nn.log_sigmoid((a_lr @ w_gla_gate[l] + b_gla_gate[l]).astype(jnp.float32)) / GLA_GATE_NORM
        o_gla = gla_chunked(q_g.reshape(B, S, GLA_HEADS, GLA_HK),
                            k_g.reshape(B, S, GLA_HEADS, GLA_HK),
                            v_g.reshape(B, S, GLA_HEADS, GLA_HV),
                            log_a.reshape(B, S, GLA_HEADS, GLA_HK))
        o_gla = rmsnorm(o_gla, g_gla[l]).astype(x.dtype).reshape(B, S, GLA_DV)
        y_gla = (o_gla * jax.nn.silu(z_gla)) @ w_proj_gla[l]

        merged = jax.nn.sigmoid(gate_mla) * y_mla + jax.nn.sigmoid(gate_gla) * y_gla
        x = x + merged @ w_out[l]
    return rmsnorm(x, g_final)
```

```python
import math
import numpy as np
import concourse.bass as bass
import concourse.mybir as mybir
from concourse.bass_utils import run_bass_kernel_spmd

F32 = mybir.dt.float32
BF16 = mybir.dt.bfloat16
I32 = mybir.dt.int32
AF = mybir.ActivationFunctionType
ALU = mybir.AluOpType

ENGS = ("pe", "act", "dve", "pool", "sp")
DSZ = {F32: 4, BF16: 2, I32: 4}

S = 2048
D = 1024
NT = 16
NC4 = 4
EPS = 1e-6
DIN = 6320
C_CQ, C_CKV, C_KR, C_ZM, C_QG, C_KG, C_VG, C_ALR, C_ZG, C_GM, C_GG = (
    0, 384, 640, 672, 1184, 1696, 2208, 3232, 3248, 4272, 5296)

DEBUG_TAPS = None
STOP_AFTER = None


ALLBUFS = []


def inherit(new_bufs, old_bufs):
    toks = set()
    for o in old_bufs:
        if o.w is not None:
            toks.add(o.w)
        toks.update(o.r)
    for n in new_bufs:
        n.r = list(set(n.r) | toks)


class Buf:
    __slots__ = ("name", "w", "r", "excl")

    def __init__(self, name, excl=False):
        self.name = name
        self.w = None
        self.r = []
        self.excl = excl
        ALLBUFS.append(self)


class Prog:
    def __init__(self, nc):
        self.nc = nc
        self.q = {e: [] for e in ENGS}
        self.cnt = {}
        self.waited = {e: {} for e in ENGS}
        self.semh = {}
        self.gfence = []
        for e in ENGS:
            self._sem("E_" + e)

    def barrier(self):
        self.gfence = self.fence()

    def _sem(self, name):
        if name not in self.semh:
            self.semh[name] = self.nc.alloc_semaphore(name)
            self.cnt[name] = 0
        return self.semh[name]

    def _waits(self, eng, deps):
        w = self.waited[eng]
        best = {}
        for d in deps:
            if d is None:
                continue
            sn, v = d
            if eng == "pe" and sn == "E_pe":
                continue
            if w.get(sn, 0) >= v:
                continue
            if best.get(sn, 0) < v:
                best[sn] = v
        for sn, v in best.items():
            w[sn] = v
        return list(best.items())

    def _deps(self, reads, writes, extra, eng=None):
        deps = []
        for b in reads:
            if b.w is not None:
                deps.append(b.w)
            if b.excl:
                own = "E_" + str(eng)
                deps.extend(t for t in b.r if t[0] != own)
        for b in writes:
            if b.w is not None:
                deps.append(b.w)
            deps.extend(b.r)
        deps.extend(extra)
        deps.extend(self.gfence)
        return deps

    def op(self, eng, fns, reads=(), writes=(), extra=()):
        if callable(fns):
            fns = [fns]
        waits = self._waits(eng, self._deps(reads, writes, extra, eng))
        sn = "E_" + eng
        self.cnt[sn] += 1
        tok = (sn, self.cnt[sn])
        self.q[eng].append((waits, fns, (sn, 1)))
        for b in reads:
            b.r.append(tok)
        for b in writes:
            b.w = tok
            b.r = []
        return tok

    def dma(self, queue, out, in_, sem, reads=(), writes=(), extra=(), **kw):
        self._sem(sem)
        waits = self._waits(queue, self._deps(reads, writes, extra, queue))
        self.cnt[sem] += 16
        tok = (sem, self.cnt[sem])
        self.q[queue].append((waits, [lambda e: e.dma_start(out=out, in_=in_, **kw)], (sem, 16)))
        for b in reads:
            b.r.append(tok)
        for b in writes:
            b.w = tok
            b.r = []
        return tok

    def fence(self):
        return [(sn, v) for sn, v in self.cnt.items() if v > 0]

    def wait_all(self, eng):
        waits = self._waits(eng, self.fence())
        self.q[eng].append((waits, [], None))

    def emit(self):
        nc = self.nc
        hand = {"pe": "tensor", "act": "scalar", "dve": "vector", "pool": "gpsimd", "sp": "sync"}
        with nc.Block() as block:
            for e in ENGS:
                items = self.q[e]

                def body(engine, items=items):
                    for waits, fns, inc in items:
                        for sn, v in waits:
                            engine.wait_ge(self.semh[sn], v)
                        for i, f in enumerate(fns):
                            ins = f(engine)
                            if i == len(fns) - 1 and inc is not None:
                                ins.then_inc(self.semh[inc[0]], inc[1])

                getattr(block, hand[e])(body)


class Arena:
    def __init__(self, nc, nbytes):
        self.t = nc.alloc_sbuf_tensor("arena", [128, nbytes // 4], F32)
        self.nbytes = nbytes
        self.top = 0
        self.views = {F32: self.t}
        self.peak = 0

    def tile(self, dtype, n, align=64):
        sz = DSZ[dtype]
        off = (self.top + align - 1) // align * align
        assert off + n * sz <= self.nbytes, ("SBUF arena overflow", off, n * sz, self.nbytes)
        self.top = off + n * sz
        self.peak = max(self.peak, self.top)
        if dtype not in self.views:
            self.views[dtype] = self.t.bitcast(dtype)
        return self.views[dtype][:, off // sz: off // sz + n]

    def mark(self):
        return self.top

    def release(self, m):
        self.top = m


def MM(out, lhsT, rhs, start=True, stop=True):
    return lambda e: e.matmul(out, lhsT=lhsT, rhs=rhs, start=start, stop=stop)


def TR(out, in_, ident):
    return lambda e: e.transpose(out=out, in_=in_, identity=ident)


def ACTV(out, in_, func, **kw):
    return lambda e: e.activation(out=out, in_=in_, func=func, **kw)


def TT(out, in0, in1, op):
    return lambda e: e.tensor_tensor(out=out, in0=in0, in1=in1, op=op)


def TS(out, in0, s1, op0, s2=None, op1=None):
    if op1 is None:
        return lambda e: e.tensor_scalar(out=out, in0=in0, scalar1=s1, scalar2=None, op0=op0)
    return lambda e: e.tensor_scalar(out=out, in0=in0, scalar1=s1, scalar2=s2, op0=op0, op1=op1)


def STT(out, in0, scalar, in1, op0, op1):
    return lambda e: e.scalar_tensor_tensor(out=out, in0=in0, scalar=scalar, in1=in1, op0=op0, op1=op1)


def CP(out, in_):
    return lambda e: e.tensor_copy(out=out, in_=in_)


def MS(ap, v):
    return lambda e: e.memset(ap, v)


def r3(ap, a):
    return ap.rearrange("p (a b) -> p a b", a=a)


def build_program():
    nc = bass.Bass("TRN2", target_bir_lowering=False)

    def din(name, shape, dt=F32):
        return nc.dram_tensor(name, shape, dt, kind="ExternalInput").ap()

    x_d = din("x", [S, D])
    pos_d = din("pos", [1, S], I32)
    gin_d = din("g_in", [1, D])
    win_d = din("w_in", [D, DIN])
    gq_d = din("g_q", [1, 384])
    wuq_d = din("w_uq", [384, 768])
    gkv_d = din("g_kv", [1, 256])
    wukv_d = din("w_ukv", [256, 1024])
    wgg_d = din("w_gla_gate", [16, 512])
    bgg_d = din("b_gla_gate", [1, 512])
    ggla_d = din("g_gla", [1, 256])
    wpm_d = din("w_proj_mla", [512, D])
    wpg_d = din("w_proj_gla", [D, D])
    wout_d = din("w_out", [D, D])
    gfin_d = din("g_final", [1, D])
    out_d = nc.dram_tensor("out", [S, D], F32, kind="ExternalOutput").ap()

    del ALLBUFS[:]
    P = Prog(nc)
    A = Arena(nc, 212700)
    psum = nc.alloc_psum_tensor("psum", [128, 4096], F32)
    psum_b = psum.bitcast(BF16)
    PB = [Buf(f"ps{i}", excl=True) for i in range(8)]
    tapped = {}
    ARENA_LOG = []

    def fin():
        import os
        if os.environ.get("MK_VERBOSE"):
            print("arena marks", ARENA_LOG, "peak", A.peak)
        for nme, ap in tapped.items():
            dd = nc.dram_tensor("tap_" + nme, list(ap.shape), ap.dtype, kind="ExternalOutput").ap()
            P.wait_all("sp")
            P.dma("sp", dd, ap, "d_tap")
        P.wait_all("sp")
        P.emit()
        return nc

    def tap(nme, ap):
        if DEBUG_TAPS and nme in DEBUG_TAPS:
            tapped[nme] = ap

    def bank(i):
        return psum[:, i * 512:(i + 1) * 512]

    def bank_b(i):
        return psum_b[:, i * 1024:(i + 1) * 1024]

    hT = r3(A.tile(BF16, 8 * S), 8)
    hT_b = [Buf(f"hT{t}") for t in range(NT)]
    ident_b = A.tile(BF16, 128)
    ident_f = A.tile(F32, 128)
    ones_b = A.tile(BF16, 128)
    maskb = A.tile(BF16, 128)
    mask01 = A.tile(BF16, 128)
    iota_i = A.tile(I32, 128)
    eps_t = A.tile(F32, 1)
    mhalf = A.tile(F32, 2)
    lnq_t = A.tile(F32, 1)
    junk = A.tile(BF16, D)
    junk_b = Buf("junk")
    cst = Buf("consts")
    gsm = Buf("gsm")
    NWB = 7
    wbuf = [A.tile(BF16, 2048) for _ in range(NWB)]
    wbuf_b = [Buf(f"wb{i}") for i in range(NWB)]
    oT = r3(A.tile(BF16, 4 * S), 4)
    oT_b = [[Buf(f"o{h}_{c}") for c in range(NC4)] for h in range(8)]
    zu = [A.tile(BF16, 512) for _ in range(3)]
    zu_b = [Buf(f"zu{i}") for i in range(3)]
    rn = [A.tile(F32, 512) for _ in range(2)]
    rn_b = [Buf("rn0"), Buf("rn1")]
    m_glob = A.mark()
    ib_glob = len(ALLBUFS)

    P.op("pool", lambda e: e.iota(iota_i, [[1, 128]], base=0, channel_multiplier=-1), writes=[cst])
    P.op("dve", TS(ident_f, iota_i, 0, ALU.is_equal), reads=[cst], writes=[cst])
    P.op("dve", CP(ident_b, ident_f), reads=[cst], writes=[cst])
    P.op("dve", TS(maskb, iota_i, 0, ALU.is_lt, -30000.0, ALU.mult), reads=[cst], writes=[cst])
    P.op("dve", TS(mask01, iota_i, 0, ALU.is_ge), reads=[cst], writes=[cst])
    P.op("pool", MS(ones_b, 1.0), writes=[cst])
    P.op("pool", MS(eps_t, EPS), writes=[cst])
    P.op("pool", MS(mhalf, -0.5), writes=[cst])
    P.op("pool", MS(lnq_t, math.log(128.0 ** -0.5)), writes=[cst])

    wctr = {"s": 0, "w": 0}

    def k3(dram2d):
        return dram2d.rearrange("(kc p) n -> p kc n", p=128)

    def load_block(src3, kcn, ncols, dst3, dst_buf, scale_ap=None, scale_eng="dve", sem=None):
        P.dma("pool", dst3, src3, sem or ("d_" + dst_buf.name), writes=[dst_buf])
        if scale_ap is not None:
            P.op(scale_eng, TT(dst3, dst3, scale_ap, ALU.mult), reads=[gsm], writes=[dst_buf])

    def stream(dram2d, c0, ncols, kcn=8):
        j = wctr["w"] % NWB
        wctr["w"] += 1
        dst3 = r3(wbuf[j][:, 0:kcn * ncols], kcn)
        load_block(k3(dram2d)[:, :, c0:c0 + ncols], kcn, ncols, dst3, wbuf_b[j])
        return dst3, wbuf_b[j]

    evac_rr = {"i": 0}

    def evac_copy(out, in_, reads, writes):
        evac_rr["i"] += 1
        if evac_rr["i"] % 2:
            return P.op("act", ACTV(out, in_, AF.Copy), reads=reads, writes=writes)
        return P.op("dve", CP(out, in_), reads=reads, writes=writes)

    def proj_fns(lhs_of_kc, m_rows, c, pb):
        return [MM(bank(pb)[0:m_rows, :], lhs_of_kc(kc), hT[:, kc, c * 512:(c + 1) * 512], start=(kc == 0), stop=(kc == 7))
                for kc in range(8)]

    def hTc(c):
        return hT_b[4 * c:4 * c + 4]

    ropeT = A.tile(F32, S)
    cos2T = ropeT[0:32, :]
    sin2T = ropeT[32:64, :]
    rope_b = Buf("ropeT")
    cqT = r3(A.tile(BF16, 3 * S), 3)
    ckvT = r3(A.tile(BF16, 2 * S), 2)
    cq_b = [Buf(f"cq{c}") for c in range(NC4)]
    ckv_b = [Buf(f"ckv{c}") for c in range(NC4)]
    krope = A.tile(BF16, S)
    krope_b = [Buf(f"krope{c}") for c in range(NC4)]
    scr_off = (A.top + 63) // 64 * 64
    ktmp = [A.tile(F32, 512) for _ in range(2)]
    ktmp_b = Buf("ktmp")
    sq = [A.tile(BF16, 512) for _ in range(4)]
    sq_b = [Buf(f"sq{i}") for i in range(4)]
    rtmp = [A.tile(F32, 512) for _ in range(4)]
    rtmp_b = [Buf(f"rt{i}") for i in range(4)]
    assert A.top - scr_off == 16384, (A.top, scr_off)
    sT = r3(A.views[BF16][:, scr_off // 2: scr_off // 2 + 4 * S], 4)
    wkrot = r3(A.tile(BF16, 8 * 32), 8)
    wkrot_b = Buf("wkrot")

    wuq = r3(A.tile(BF16, 3 * 768), 3)
    wuqrot = A.tile(BF16, 3 * 8 * 96).rearrange("p (k h e) -> p k h e", k=3, h=8)
    wukv = r3(A.tile(BF16, 2 * 1024), 2)
    wuq_b, wuqrot_b, wukv_b = Buf("wuq"), Buf("wuqrot"), Buf("wukv")
    gq_t = A.tile(F32, 3)
    gkv_t = A.tile(F32, 2)
    m_rope = A.mark()
    ib_tmp0 = len(ALLBUFS)
    posi = A.tile(I32, NT)
    posf = A.tile(F32, NT)
    posT_i = A.tile(I32, 128)
    posT_f = A.tile(F32, 128)
    freqrow = A.tile(F32, 32)
    ang = A.tile(F32, 512)
    angc = A.tile(F32, 512)
    ki = A.tile(I32, 512)
    kf = A.tile(F32, 512)
    r1 = A.tile(F32, 512)
    r2 = A.tile(F32, 512)
    fx = A.tile(F32, 512)
    sc_tok = A.tile(F32, 1024)
    rp = Buf("rope_tmp")
    rfin = [A.tile(F32, 512) for _ in range(2)]
    fr3 = freqrow.rearrange("p (two j) -> p two j", two=2)
    for j in range(16):
        fj = 10000.0 ** (-j / 16.0)
        P.op("pool", MS(fr3[:, :, j:j + 1], fj), writes=[rp])
    rope_items = []

    def ritem(eng, fn, reads, writes):
        rope_items.append(lambda: P.op(eng, fn, reads=reads, writes=writes))

    ritem("dve", CP(posT_f[0:NT, :], posT_i[0:NT, :]), [rp], [rp])
    rope_items.append(lambda: P.op("pe", TR(bank(0)[:, 0:NT], posT_f[0:NT, :], ident_f[0:NT, 0:NT]), reads=[rp, cst], writes=[PB[0]]))
    rope_items.append(lambda: P.op("dve", CP(posf, bank(0)[:, 0:NT]), reads=[PB[0]], writes=[rp]))
    ritem("dve", TT(r3(ang, NT), posf.unsqueeze(2).to_broadcast([128, NT, 32]),
                    freqrow.unsqueeze(1).to_broadcast([128, NT, 32]), ALU.mult), [rp], [rp])
    ritem("dve", TS(angc, ang, math.pi / 2, ALU.add), [rp], [rp])
    TWO_PI = 2 * math.pi
    C1 = 6.28125
    C2 = TWO_PI - C1
    for idx, src in enumerate((ang, angc)):
        ritem("dve", TS(ki, src, 1.0 / TWO_PI, ALU.mult), [rp], [rp])
        ritem("dve", CP(kf, ki), [rp], [rp])
        ritem("dve", STT(r1, kf, -C1, src, ALU.mult, ALU.add), [rp], [rp])
        ritem("dve", STT(r2, kf, -C2, r1, ALU.mult, ALU.add), [rp], [rp])
        ritem("dve", TS(fx, r2, math.pi, ALU.is_gt, -TWO_PI, ALU.mult), [rp], [rp])
        ritem("dve", TT(r1, r2, fx, ALU.add), [rp], [rp])
        ritem("dve", TS(fx, r1, -math.pi, ALU.is_lt, TWO_PI, ALU.mult), [rp], [rp])
        ritem("dve", TT(r2, r1, fx, ALU.add), [rp], [rp])
        ritem("dve", TS(rfin[idx], r2, math.pi, ALU.min, -math.pi, ALU.max), [rp], [rp])
    rope_tail = []
    for idx in range(2):
        rope_tail.append(lambda idx=idx: P.op("act", ACTV(sc_tok[:, idx * 512:(idx + 1) * 512], rfin[idx], AF.Sin), reads=[rp], writes=[rp]))
    sc4 = sc_tok.rearrange("p (s t f) -> p s t f", s=2, t=NT)

    def rope_tr(idx, dstT, g):
        def f():
            pb = g % 2
            fns = [TR(bank(pb)[0:32, k * 128:(k + 1) * 128], sc4[:, idx, g * 4 + k, :], ident_f) for k in range(4)]
            P.op("pe", fns, reads=[rp, cst], writes=[PB[pb]])
            P.op("dve", CP(dstT[:, g * 512:(g + 1) * 512], bank(pb)[0:32, :]), reads=[PB[pb]], writes=[rope_b])
        return f

    for idx, dstT in ((1, cos2T), (0, sin2T)):
        for g in range(4):
            rope_tail.append(rope_tr(idx, dstT, g))

    gin_bc = A.tile(F32, D)
    NXB = 4
    xs = [A.tile(F32, D) for _ in range(NXB)]
    xs_b = [Buf(f"xs{i}") for i in range(NXB)]
    xn = [A.tile(BF16, D) for _ in range(2)]
    xn_b = [Buf(f"xn{i}") for i in range(2)]
    st0 = A.tile(F32, 3 * NT)
    st0_b = [Buf(f"st0_{t}") for t in range(NT)]

    def p0_load(t):
        i = t % NXB
        P.dma("sp", xs[i], x_d[t * 128:(t + 1) * 128, :], f"d_xs{i}", writes=[xs_b[i]])

    def p0_A(t):
        i = t % NXB
        if t + 2 < NT and t + 2 >= NXB - 1:
            p0_load(t + 2)
        ss = st0[:, 3 * t:3 * t + 1]
        ln = st0[:, 3 * t + 1:3 * t + 2]
        rs = st0[:, 3 * t + 2:3 * t + 3]
        P.op("act", ACTV(junk, xs[i], AF.Square, accum_out=ss), reads=[xs_b[i]], writes=[junk_b, st0_b[t]])
        P.op("act", ACTV(ln, ss, AF.Ln, scale=1.0 / D, bias=eps_t), reads=[st0_b[t], cst], writes=[st0_b[t]])
        P.op("act", ACTV(rs, ln, AF.Exp, scale=-0.5), reads=[st0_b[t]], writes=[st0_b[t]])

    def p0_B(t):
        i = t % NXB
        rs = st0[:, 3 * t + 2:3 * t + 3]
        k = t % 2
        P.op("dve", STT(xn[k], xs[i], rs, gin_bc, ALU.mult, ALU.mult), reads=[xs_b[i], st0_b[t], gin_b], writes=[xn_b[k]])
        pb = 2 + (t % 2)
        fns = [TR(bank_b(pb)[:, kc * 128:(kc + 1) * 128], xn[k][:, kc * 128:(kc + 1) * 128], ident_b) for kc in range(8)]
        P.op("pe", fns, reads=[xn_b[k], cst], writes=[PB[pb]])

    def p0_C(t):
        pb = 2 + (t % 2)
        P.op("act", ACTV(hT[:, :, t * 128:(t + 1) * 128], r3(bank_b(pb), 8), AF.Copy), reads=[PB[pb]], writes=[hT_b[t]])

    ssq_ctr = {"i": 0, "pb": 0}
    ln_pending = []

    def ln_tick(flush=False):
        for it_ in ln_pending:
            it_[0] += 1
        while ln_pending and (flush or ln_pending[0][0] >= 2):
            ln_pending.pop(0)[1]()

    def latent_norm(dstT, dst_b, nchunk, width, srcs, c):
        ssq_pb = 6 + (ssq_ctr["i"] % 2)
        ssq_ctr["i"] += 1

        def ones_mm(j, k):
            def f():
                P.op("pe", MM(bank(ssq_pb), ones_b, sq[k], start=(j == 0), stop=(j == nchunk - 1)), reads=[sq_b[k], cst], writes=[PB[ssq_pb]])
                if j == nchunk - 1:
                    r0, r1_ = rtmp[(ssq_pb % 2) * 2], rtmp[(ssq_pb % 2) * 2 + 1]
                    rb0, rb1 = rtmp_b[(ssq_pb % 2) * 2], rtmp_b[(ssq_pb % 2) * 2 + 1]
                    P.op("act", ACTV(r0, bank(ssq_pb), AF.Ln, scale=1.0 / width, bias=eps_t), reads=[PB[ssq_pb], cst], writes=[rb0])
                    P.op("act", ACTV(r1_, r0, AF.Exp, scale=-0.5), reads=[rb0], writes=[rb1])
                    for jj in range(nchunk):
                        sl = dstT[:, jj, c * 512:(c + 1) * 512]
                        P.op("dve", TT(sl, sl, r1_, ALU.mult), reads=[rb1, dst_b[c]], writes=[dst_b[c]])
            return f

        for j, (w3, wb, c0) in enumerate(srcs):
            pb = 2 + ssq_ctr["pb"] % 4
            ssq_ctr["pb"] += 1
            P.op("pe", proj_fns(lambda kc: w3[:, kc, c0:c0 + 128], 128, c, pb), reads=[wb] + hTc(c), writes=[PB[pb]])
            k = ssq_ctr["pb"] % 4
            dsl = dstT[:, j, c * 512:(c + 1) * 512]
            P.op("act", ACTV(dsl, bank(pb), AF.Copy), reads=[PB[pb]], writes=[dst_b[c]])
            P.op("act", ACTV(sq[k], bank(pb), AF.Square), reads=[PB[pb]], writes=[sq_b[k]])
            ln_tick()
            ln_pending.append([0, ones_mm(j, k)])

    W1A = {}

    def phase1a_weights():
        w0, w0b = stream(win_d, 0, 256)
        w1, w1b = stream(win_d, 256, 256)
        w2, w2b = stream(win_d, 512, 160)
        W1A.update(w0=w0, w0b=w0b, w1=w1, w1b=w1b, w2=w2, w2b=w2b)

    def wkrot_prep():
        w2, w2b = W1A["w2"], W1A["w2b"]
        P.op("dve", TS(wkrot[:, :, 0:16], w2[:, :, 144:160], -1.0, ALU.mult), reads=[w2b], writes=[wkrot_b])
        P.op("dve", CP(wkrot[:, :, 16:32], w2[:, :, 128:144]), reads=[w2b], writes=[wkrot_b])

    def phase1a(after_chunk):
        w0, w0b, w1, w1b, w2, w2b = (W1A[k] for k in ("w0", "w0b", "w1", "w1b", "w2", "w2b"))
        for c in range(NC4):
            latent_norm(cqT, cq_b, 3, 384.0, [(w0, w0b, 0), (w0, w0b, 128), (w1, w1b, 0)], c)
            latent_norm(ckvT, ckv_b, 2, 256.0, [(w1, w1b, 128), (w2, w2b, 0)], c)
            P.op("pe", proj_fns(lambda kc: w2[:, kc, 128:160], 32, c, 4), reads=[w2b] + hTc(c), writes=[PB[4]])
            P.op("pe", proj_fns(lambda kc: wkrot[:, kc, :], 32, c, 5), reads=[wkrot_b] + hTc(c), writes=[PB[5]])
            sl = slice(c * 512, (c + 1) * 512)
            P.op("dve", TT(ktmp[0][0:32, :], bank(4)[0:32, :], cos2T[:, sl], ALU.mult), reads=[PB[4], rope_b], writes=[ktmp_b])
            P.op("dve", TT(ktmp[1][0:32, :], bank(5)[0:32, :], sin2T[:, sl], ALU.mult), reads=[PB[5], rope_b], writes=[ktmp_b])
            P.op("dve", TT(krope[0:32, sl], ktmp[0][0:32, :], ktmp[1][0:32, :], ALU.add), reads=[ktmp_b], writes=[krope_b[c]])
            ln_tick(flush=True)
            after_chunk(c)

    for t_ in range(NXB - 1):
        p0_load(t_)
    gin_b = Buf("gin")
    P.dma("sp", gin_bc, gin_d.to_broadcast([128, D]), "d_m2", writes=[gin_b])
    def tiny_dmas():
        P.dma("sp", posT_i[0:NT, :], pos_d.rearrange("o (t p) -> (o t) p", p=128), "d_m1", writes=[rp])
        P.dma("sp", gq_t, gq_d.rearrange("o (k p) -> p (o k)", p=128), "d_m3", writes=[gsm], allow_slow_non_contiguous=True)
        P.dma("sp", gkv_t, gkv_d.rearrange("o (k p) -> p (o k)", p=128), "d_m4", writes=[gsm], allow_slow_non_contiguous=True)
    phase1a_weights()
    wsc = []
    for hf in range(2):
        load_block(k3(wuq_d)[:, :, hf * 384:(hf + 1) * 384], 3, 384, wuq[:, :, hf * 384:(hf + 1) * 384], wuq_b, sem=f"d_wuq{hf}")
        wsc.append(lambda hf=hf: P.op("dve", TT(wuq[:, :, hf * 384:(hf + 1) * 384], wuq[:, :, hf * 384:(hf + 1) * 384],
                                                gq_t.unsqueeze(2).to_broadcast([128, 3, 384]), ALU.mult), reads=[gsm], writes=[wuq_b]))
    for hf in range(2):
        load_block(k3(wukv_d)[:, :, hf * 512:(hf + 1) * 512], 2, 512, wukv[:, :, hf * 512:(hf + 1) * 512], wukv_b, sem=f"d_wukv{hf}")
        wsc.append(lambda hf=hf: P.op("dve", TT(wukv[:, :, hf * 512:(hf + 1) * 512], wukv[:, :, hf * 512:(hf + 1) * 512],
                                                gkv_t.unsqueeze(2).to_broadcast([128, 2, 512]), ALU.mult), reads=[gsm], writes=[wukv_b]))
    P.op("pool", MS(wuqrot.rearrange("p k h e -> p (k h e)"), 0.0), writes=[wuqrot_b])
    for i in range(NT + 2):
        if i < NT:
            p0_A(i)
        if 0 <= i - 1 < NT:
            p0_B(i - 1)
        if 0 <= i - 2 < NT:
            p0_C(i - 2)
        if i == 3:
            tiny_dmas()
        for _ in range(3):
            if i >= 6 and rope_items:
                rope_items.pop(0)()
    while rope_items:
        rope_items.pop(0)()
    wkrot_prep()
    while wsc:
        wsc.pop(0)()
    wuq4 = wuq.rearrange("p k (h e) -> p k h e", h=8)
    P.op("dve", TS(wuqrot[:, :, :, 64:80], wuq4[:, :, :, 80:96], -1.0, ALU.mult), reads=[wuq_b], writes=[wuqrot_b])
    P.op("dve", CP(wuqrot[:, :, :, 80:96], wuq4[:, :, :, 64:80]), reads=[wuq_b], writes=[wuqrot_b])
    for f_ in rope_tail:
        f_()
    tap("hT", hT)
    if STOP_AFTER == "p0":
        return fin()
    ib_tmp1 = len(ALLBUFS)
    A.release(m_rope)
    ARENA_LOG.append(("pre-att", A.top))
    ib_att0 = len(ALLBUFS)
    ropet = [A.tile(F32, 512) for _ in range(4)]
    ropet_b = [Buf(f"ropet{i}") for i in range(4)]
    qT = [A.tile(BF16, S) for _ in range(2)]
    kT = [A.tile(BF16, S) for _ in range(2)]
    vaug = [r3(A.tile(BF16, NT * 128), NT) for _ in range(2)]
    qT_b = [[Buf(f"q{i}_{c}") for c in range(NC4)] for i in range(2)]
    kT_b = [[Buf(f"k{i}_{c}") for c in range(NC4)] for i in range(2)]
    va_b = [[Buf(f"va{i}_{g}") for g in range(2)] for i in range(2)]
    NPT = 4
    PT = [A.tile(BF16, 512) for _ in range(NPT)]
    PT_b = [Buf(f"PT{i}") for i in range(NPT)]
    rc = [A.tile(F32, 512) for _ in range(2)]
    rc_b = [Buf("rc0"), Buf("rc1")]
    zt = [A.tile(F32, 512) for _ in range(2)]
    zt_b = [Buf("zt0"), Buf("zt1")]
    inherit(ALLBUFS[ib_att0:], ALLBUFS[ib_tmp0:ib_tmp1])
    SCALE = 96.0 ** -0.5

    for i in range(2):
        P.op("pool", MS(vaug[i].rearrange("p t e -> p (t e)"), 1.0), writes=va_b[i])

    def phase2_pieces(h):
        i = h % 2
        pieces = []

        def q_piece(c):
            def f():
                sl = slice(c * 512, (c + 1) * 512)
                ba = 0
                P.op("pe", [MM(bank(ba)[0:96, :], wuq[:, kc, h * 96:(h + 1) * 96], cqT[:, kc, sl], start=(kc == 0), stop=(kc == 2)) for kc in range(3)],
                     reads=[wuq_b, cq_b[c]], writes=[PB[ba]])
                P.op("pe", [MM(bank(1)[0:96, :], wuqrot[:, kc, h, :], cqT[:, kc, sl], start=(kc == 0), stop=(kc == 2)) for kc in range(3)],
                     reads=[wuqrot_b, cq_b[c]], writes=[PB[1]])
                P.op("act" if h == 0 else "dve", (ACTV(qT[i][0:64, sl], bank(ba)[0:64, :], AF.Copy) if h == 0 else CP(qT[i][0:64, sl], bank(ba)[0:64, :])), reads=[PB[ba]], writes=[qT_b[i][c]])
                i0, i1 = (c % 2) * 2, (c % 2) * 2 + 1
                P.op("dve", TT(ropet[i0][64:96, :], bank(ba)[64:96, :], cos2T[:, sl], ALU.mult), reads=[PB[ba], rope_b], writes=[ropet_b[i0]])
                P.op("dve", TT(ropet[i1][64:96, :], bank(1)[64:96, :], sin2T[:, sl], ALU.mult), reads=[PB[1], rope_b], writes=[ropet_b[i1]])
                P.op("dve", TT(qT[i][64:96, sl], ropet[i0][64:96, :], ropet[i1][64:96, :], ALU.add), reads=[ropet_b[i0], ropet_b[i1]], writes=[qT_b[i][c]])
            return f

        def k_piece(c):
            def f():
                sl = slice(c * 512, (c + 1) * 512)
                pb = c % 2
                P.op("pe", [MM(bank(pb)[0:64, :], wukv[:, kc, h * 128:h * 128 + 64], ckvT[:, kc, sl], start=(kc == 0), stop=(kc == 1)) for kc in range(2)],
                     reads=[wukv_b, ckv_b[c]], writes=[PB[pb]])
                P.op("dve", CP(kT[i][0:64, sl], bank(pb)[0:64, :]), reads=[PB[pb]], writes=[kT_b[i][c]])
                P.op("dve", CP(kT[i][64:96, sl], krope[0:32, sl]), reads=[krope_b[c]], writes=[kT_b[i][c]])
            return f

        def v_piece(g):
            def f():
                pb = g % 2
                fns = []
                for tt in range(8):
                    t = g * 8 + tt
                    for kc in range(2):
                        fns.append(MM(bank(pb)[:, tt * 64:(tt + 1) * 64], ckvT[:, kc, t * 128:(t + 1) * 128],
                                      wukv[:, kc, h * 128 + 64:h * 128 + 128], start=(kc == 0), stop=(kc == 1)))
                P.op("pe", fns, reads=[wukv_b] + ckv_b[2 * g:2 * g + 2], writes=[PB[pb]])
                vo = 0 if h % 2 == 0 else 64
                P.op("dve", CP(vaug[i][:, g * 8:(g + 1) * 8, vo:vo + 64], r3(bank(pb), 8)), reads=[PB[pb]], writes=[va_b[i][g]])
            return f

        for c in range(NC4):
            pieces.append(q_piece(c))
            pieces.append(k_piece(c))
            if c % 2 == 1:
                pieces.append(v_piece(c // 2))
        return pieces

    def attention():
        steps = [(h, c, kt) for h in range(8) for c in range(NC4) for kt in range(4 * c + 4)]
        per_head = len(steps) // 8
        LA = 3

        def emit_S(n):
            h, c, kt = steps[n]
            i = h % 2
            j = kt - 4 * c
            n0 = 128 * j if j > 0 else 0
            spb = (2, 3, 4, 7)[n % 4]
            pti = n % NPT
            qs = slice(c * 512 + n0, (c + 1) * 512)
            fns = [MM(bank(spb)[:, n0:512], kT[i][0:96, kt * 128:(kt + 1) * 128], qT[i][0:96, qs], start=True, stop=(j < 0))]
            if j >= 0:
                fns.append(MM(bank(spb)[:, n0:n0 + 128], ident_b, maskb, start=False, stop=True))
            P.op("pe", fns, reads=[kT_b[i][kt // 4], qT_b[i][c], cst], writes=[PB[spb]])
            P.op("act", ACTV(PT[pti][:, n0:512], bank(spb)[:, n0:512], AF.Exp, scale=SCALE), reads=[PB[spb]], writes=[PT_b[pti]])

        def emit_PV(n):
            h, c, kt = steps[n]
            i = h % 2
            j = kt - 4 * c
            n0 = 128 * j if j > 0 else 0
            pti = n % NPT
            nk = 4 * c + 4
            oc = h * NC4 + c
            opb = 5 + (oc % 2)
            P.op("pe", MM(bank(opb)[:, n0:512], vaug[i][:, kt, :], PT[pti][:, n0:512], start=(kt == 0), stop=(kt == nk - 1)),
                 reads=[va_b[i][kt // 8], PT_b[pti]], writes=[PB[opb]])
            if kt == nk - 1:
                vlo, slo = (0, 64) if h % 2 == 0 else (64, 0)
                k = oc % 2
                P.op("dve", CP(oT[vlo:vlo + 64, h // 2, c * 512:(c + 1) * 512], bank(opb)[vlo:vlo + 64, :]), reads=[PB[opb]], writes=[oT_b[h][c]])
                P.op("dve", CP(sT[vlo:vlo + 64, h // 2, c * 512:(c + 1) * 512], bank(opb)[slo:slo + 64, :]), reads=[PB[opb]], writes=[sT_b[h][c]])

        nxt_pieces = []
        for n in range(len(steps) + LA):
            if n < len(steps):
                h, c, kt = steps[n]
                if c == 0 and kt == 0 and h + 1 < 8:
                    nxt_pieces = phase2_pieces(h + 1)
                emit_S(n)
                pos_in_head = n - h * per_head
                if nxt_pieces and pos_in_head % 3 == 2:
                    nxt_pieces.pop(0)()
                if pos_in_head == per_head - 1:
                    while nxt_pieces:
                        nxt_pieces.pop(0)()
            if n - LA >= 0:
                emit_PV(n - LA)

    h0_pieces = phase2_pieces(0)

    def after_chunk(c):
        if c == 0:
            return
        n = 2 if (c - 1) % 2 == 0 else 3
        for _ in range(n):
            h0_pieces.pop(0)()

    phase1a(after_chunk)
    while h0_pieces:
        h0_pieces.pop(0)()
    assert not h0_pieces
    tap("cqT", cqT)
    tap("ckvT", ckvT)
    tap("krope", krope)
    if STOP_AFTER and STOP_AFTER.startswith("p1a"):
        return fin()
    sT_b = [[Buf(f"sT{h}_{c}") for c in range(NC4)] for h in range(8)]
    inherit([b for hb in sT_b for b in hb], [ktmp_b] + sq_b + rtmp_b)
    attention()
    norm_items = []
    for m in range(4):
        for c in range(NC4):
            def f(m=m, c=c):
                k = (m * NC4 + c) % 2
                sl = slice(c * 512, (c + 1) * 512)
                bs = [sT_b[2 * m][c], sT_b[2 * m + 1][c]]
                P.op("act", ACTV(rn[k], sT[:, m, sl], AF.Ln), reads=bs, writes=[rn_b[k]])
                P.op("act", ACTV(rn[k], rn[k], AF.Exp, scale=-1.0), reads=[rn_b[k]], writes=[rn_b[k]])
                P.op("dve", TT(oT[:, m, sl], oT[:, m, sl], rn[k], ALU.mult), reads=[rn_b[k], oT_b[2 * m][c], oT_b[2 * m + 1][c]],
                     writes=[oT_b[2 * m][c], oT_b[2 * m + 1][c]])
            norm_items.append(f)
    if STOP_AFTER == "att":
        while norm_items:
            norm_items.pop(0)()
    tap("oTraw", oT)
    if STOP_AFTER == "att":
        return fin()


    p3b_items = []

    def phase3b():
        it = 0
        for blk in range(2):
            holder = {}
            for mm in range(2):
                m = blk * 2 + mm
                for c in range(NC4):
                    def f(blk=blk, mm=mm, m=m, c=c, it=it, holder=holder):
                        if "w" not in holder:
                            holder["w"] = stream(win_d, C_ZM + blk * 256, 256)
                        w3, wb = holder["w"]
                        pb = 3 + (it % 2)
                        k = it % 3
                        P.op("pe", proj_fns(lambda kc: w3[:, kc, mm * 128:(mm + 1) * 128], 128, c, pb), reads=[wb] + hTc(c), writes=[PB[pb]])
                        P.op("act", ACTV(zu[k], bank(pb), AF.Silu), reads=[PB[pb]], writes=[zu_b[k]])
                        sl = oT[:, m, c * 512:(c + 1) * 512]
                        P.op("dve", TT(sl, sl, zu[k], ALU.mult), reads=[zu_b[k], oT_b[2 * m][c], oT_b[2 * m + 1][c]], writes=[oT_b[2 * m][c], oT_b[2 * m + 1][c]])
                    p3b_items.append(f)
                    it += 1

    phase3b()
    if STOP_AFTER == "p3b":
        while p3b_items:
            p3b_items.pop(0)()
    tap("oT", oT)
    if STOP_AFTER == "p3b":
        return fin()

    PRE = {}
    PRE["qk0"] = [stream(win_d, C_QG, 256), stream(win_d, C_KG, 256)]
    A.release(m_glob)
    ARENA_LOG.append(("end-att", A.peak))
    ib_gla0 = len(ALLBUFS)
    zgT = r3(A.tile(BF16, 8 * S), 8)
    zg_b = [[Buf(f"zg{m}_{t}") for t in range(NT)] for m in range(8)]
    m_gla = A.mark()
    ib_glap = len(ALLBUFS)
    _q1 = r3(A.tile(BF16, 2 * S), 2)
    _k1 = r3(A.tile(BF16, 2 * S), 2)
    assert scr_off >= m_gla and scr_off + 16384 <= A.top, (scr_off, m_gla, A.top)
    _q0 = r3(A.tile(BF16, 2 * S), 2)
    _k0 = r3(A.tile(BF16, 2 * S), 2)
    qgTs = [_q0, _q1]
    kgTs = [_k0, _k1]
    vg = r3(A.tile(BF16, NT * 512), NT)
    qg_bs = [[[Buf(f"qg{p}{l}_{c}") for c in range(NC4)] for l in range(2)] for p in range(2)]
    kg_bs = [[[Buf(f"kg{p}{l}_{c}") for c in range(NC4)] for l in range(2)] for p in range(2)]
    vg_b = [Buf(f"vg{t}") for t in range(NT)]
    walr = r3(A.tile(BF16, 8 * 16), 8)
    walr_b = Buf("walr")
    alrc = [A.tile(F32, 512) for _ in range(NC4)]
    alrc_b = [Buf(f"alrc{i}") for i in range(NC4)]
    wga = A.tile(F32, 512)
    wga_b = Buf("wga")
    dec = r3(A.tile(F32, 2 * NT), 2)
    dec_b = [[Buf(f"dec{l}_{c}") for c in range(NC4)] for l in range(2)]
    Sst = r3(A.tile(F32, 2 * 256), 2)
    Sst_b = [Buf(f"S{l}") for l in range(2)]
    Sbf = [r3(A.tile(BF16, 2 * 256), 2) for _ in range(2)]
    Sbf_b = [Buf("Sbf0"), Buf("Sbf1")]
    ss4 = [A.tile(F32, 8) for _ in range(2)]
    ss4_b = [Buf("ss4a"), Buf("ss4b")]
    AmT = [r3(A.tile(BF16, 256), 2) for _ in range(2)]
    AmT_b = [Buf("AmT0"), Buf("AmT1")]
    kstT = [r3(A.tile(BF16, 256), 2) for _ in range(2)]
    kstT_b = [Buf("kstT0"), Buf("kstT1")]
    kstt = [r3(A.tile(BF16, 256), 2) for _ in range(2)]
    kstt_b = [Buf("kstt0"), Buf("kstt1")]
    ogn = [A.tile(BF16, 512) for _ in range(2)]
    ogn_b = [Buf("ogn0"), Buf("ogn1")]
    junk2 = [A.tile(BF16, 256) for _ in range(2)]
    junk2_b = [Buf("junk2a"), Buf("junk2b")]
    rmask = A.tile(F32, 512)
    _gA = [A.tile(F32, 512) for _ in range(2)]
    _gB = [A.tile(F32, 512) for _ in range(2)]
    _gC = [A.tile(F32, 512) for _ in range(2)]
    gA, gB, gC = [_gA, _gA], [_gB, _gB], [_gC, _gC]
    _gAb = [Buf(f"gA{l}") for l in range(2)]
    _gBb = [Buf(f"gB{l}") for l in range(2)]
    _gCb = [Buf(f"gC{l}") for l in range(2)]
    gA_b, gB_b, gC_b = [_gAb, _gAb], [_gBb, _gBb], [_gCb, _gCb]
    zt = [A.tile(F32, 512) for _ in range(3)]
    zt_b = [Buf("zt0b"), Buf("zt1b"), Buf("zt2b")]

    ib_gla1 = len(ALLBUFS)
    inherit(ALLBUFS[ib_gla0:ib_gla1], ALLBUFS[ib_glob:ib_gla0])
    rmask_b = Buf("rmask")
    inherit([rmask_b], ALLBUFS[ib_glob:ib_gla0])
    P.op("pool", MS(rmask, 1.0), writes=[rmask_b])
    P.op("pool", MS(rmask.rearrange("p (t b) -> p t b", b=128)[:, :, 0:1], 0.0), writes=[rmask_b])
    for i in range(NC4):
        P.op("pool", MS(alrc[i][0:32, :], 1.0), writes=[alrc_b[i]])
    P.dma("sp", wga[0:16, :], wgg_d, "d_wga", writes=[wga_b])
    P.dma("sp", wga[16:17, :], bgg_d, "d_wga", writes=[wga_b])
    load_block(k3(win_d)[:, :, C_ALR:C_ALR + 16], 8, 16, walr, walr_b)

    def merge_emit(bulk, chain):
        nb, ncn = len(bulk), len(chain)
        bi = ci = 0
        while bi < nb or ci < ncn:
            if bi < nb:
                bulk[bi]()
                bi += 1
            tgt = ncn if bi >= nb else (bi * ncn + nb - 1) // nb
            while ci < tgt:
                chain[ci]()
                ci += 1

    def gla_pass(p):
        it = [0]
        qgT, kgT, qg_b, kg_b = qgTs[p], kgTs[p], qg_bs[p], kg_bs[p]
        if p == 0:
            for wi_, (c0, dstT, dst_b) in enumerate(((C_QG, qgT, qg_b), (C_KG, kgT, kg_b))):
                w3, wb = PRE["qk0"][wi_]
                for l in range(2):
                    for c in range(NC4):
                        pb = it[0] % 3
                        it[0] += 1
                        if norm_items:
                            norm_items.pop(0)()
                        elif p3b_items:
                            p3b_items.pop(0)()
                        P.op("pe", proj_fns(lambda kc: w3[:, kc, l * 128:(l + 1) * 128], 128, c, pb), reads=[wb] + hTc(c), writes=[PB[pb]])
                        evac_copy(dstT[:, l, c * 512:(c + 1) * 512], bank(pb), reads=[PB[pb]], writes=[dst_b[l][c]])
        if p == 0:
            while norm_items:
                norm_items.pop(0)()
            while p3b_items:
                p3b_items.pop(0)()
            inherit([b for l_ in qg_bs[1] + kg_bs[1] for b in l_], [b for hb in sT_b for b in hb])
        bulk = []

        def vg_item(blk, t, holder):
            def f():
                if t == 0:
                    holder["w"] = PRE["vg1"][blk] if (p == 1 and "vg1" in PRE) else stream(win_d, C_VG + 512 * p + blk * 256, 256)
                w3, wb = holder["w"]
                pb = it[0] % 3
                it[0] += 1
                P.op("pe", [MM(bank(pb)[:, 0:256], hT[:, kc, t * 128:(t + 1) * 128], w3[:, kc, :], start=(kc == 0), stop=(kc == 7)) for kc in range(8)],
                     reads=[wb, hT_b[t]], writes=[PB[pb]])
                evac_copy(vg[:, t, blk * 256:(blk + 1) * 256], bank(pb)[:, 0:256], reads=[PB[pb]], writes=[vg_b[t]])
            return f

        zgw = {}
        if p == 1:
            for blk_ in range(2):
                zgw[blk_] = stream(win_d, C_ZG + 512 * p + blk_ * 256, 256)

        def zg_fill(i):
            c, idx = i // 4, i % 4
            blk, mm = idx // 2, idx % 2
            w3, wb = zgw[blk]
            m = 4 * p + blk * 2 + mm
            P.op("pe", proj_fns(lambda kc: w3[:, kc, mm * 128:(mm + 1) * 128], 128, c, 7), reads=[wb] + hTc(c), writes=[PB[7]])
            P.op("act", ACTV(zgT[:, m, c * 512:(c + 1) * 512], bank(7), AF.Silu), reads=[PB[7]], writes=zg_b[m][4 * c:4 * c + 4])

        for blk in range(2):
            holder = {}
            for t in range(NT):
                bulk.append(vg_item(blk, t, holder))
        chain = []

        def add(fn):
            chain.append(fn)

        for c in range(NC4):
            sl = slice(c * 512, (c + 1) * 512)
            i2 = c % 2
            pbA = 3 + (c % 2)
            if p == 0:
                add(lambda c=c, pbA=pbA: P.op("pe", proj_fns(lambda kc: walr[:, kc, :], 16, c, pbA), reads=[walr_b] + hTc(c), writes=[PB[pbA]]))
                add(lambda c=c, pbA=pbA: P.op("dve", CP(alrc[c][0:16, :], bank(pbA)[0:16, :]), reads=[PB[pbA]], writes=[alrc_b[c]]))
            for l in range(2):
                f = 2 * p + l
                pg = 5 + l
                add(lambda c=c, f=f, pg=pg: P.op("pe", MM(bank(pg), wga[0:17, f * 128:(f + 1) * 128], alrc[c][0:17, :]), reads=[wga_b, alrc_b[c]], writes=[PB[pg]]))
            for l in range(2):
                pg = 5 + l
                add(lambda i2=i2, l=l, pg=pg: P.op("act", ACTV(gA[i2][l], bank(pg), AF.Exp, scale=-1.0), reads=[PB[pg]], writes=[gA_b[i2][l]]))
            for l in range(2):
                add(lambda i2=i2, l=l: P.op("act", ACTV(gB[i2][l], gA[i2][l], AF.Ln, bias=1.0, scale=1.0), reads=[gA_b[i2][l]], writes=[gB_b[i2][l]]))
            for l in range(2):
                add(lambda i2=i2, l=l: P.op("dve", lambda e: e.tensor_tensor_scan(out=gA[i2][l], data0=rmask, data1=gB[i2][l], initial=0.0, op0=ALU.mult, op1=ALU.add),
                                             reads=[gB_b[i2][l], rmask_b], writes=[gA_b[i2][l]]))
            for l in range(2):
                add(lambda i2=i2, l=l, c=c: P.op("act", ACTV(dec[:, l, 4 * c:4 * c + 4], gA[i2][l].rearrange("p (t b) -> p t b", b=128)[:, :, 127], AF.Exp, scale=-1.0 / 16),
                                                  reads=[gA_b[i2][l]], writes=[dec_b[l][c]]))
                add(lambda i2=i2, l=l: P.op("act", ACTV(gB[i2][l], gA[i2][l], AF.Exp, scale=-1.0 / 16, bias=lnq_t), reads=[gA_b[i2][l], cst], writes=[gB_b[i2][l]]))
                add(lambda i2=i2, l=l: P.op("act", ACTV(gC[i2][l], gA[i2][l], AF.Exp, scale=1.0 / 16), reads=[gA_b[i2][l]], writes=[gC_b[i2][l]]))
            for l in range(2):
                add(lambda i2=i2, l=l, c=c, sl=sl: P.op("dve", TT(qgT[:, l, sl], qgT[:, l, sl], gB[i2][l], ALU.mult), reads=[gB_b[i2][l], qg_b[l][c]], writes=[qg_b[l][c]]))
                add(lambda i2=i2, l=l, c=c, sl=sl: P.op("dve", TT(kgT[:, l, sl], kgT[:, l, sl], gC[i2][l], ALU.mult), reads=[gC_b[i2][l], kg_b[l][c]], writes=[kg_b[l][c]]))
        merge_emit(bulk, chain)

        PA_, PKT_, PO_, PS_, PT_ = (0, 0), 2, (3, 4), 5, (6, 6)
        fill_w = {}

        def filler(i):
            which, l, c = i // 8, (i % 8) // 4, i % 4
            if which not in fill_w:
                fill_w[which] = stream(win_d, (C_QG, C_KG)[which] + 256, 256)
            w3, wb = fill_w[which]
            dstT, dst_b = ((qgTs[1], qg_bs[1]), (kgTs[1], kg_bs[1]))[which]
            P.op("pe", proj_fns(lambda kc: w3[:, kc, l * 128:(l + 1) * 128], 128, c, 1), reads=[wb] + hTc(c), writes=[PB[1]])
            P.op("act", ACTV(dstT[:, l, c * 512:(c + 1) * 512], bank(1), AF.Copy), reads=[PB[1]], writes=[dst_b[l][c]])


        def st0(t):
            tb = slice(t * 128, (t + 1) * 128)
            c = t // 4
            par = t % 2
            pa = PA_[par]
            P.op("pe", [MM(bank(pa)[:, l * 128:(l + 1) * 128], kgT[:, l, tb], qgT[:, l, tb]) for l in range(2)],
                 reads=[kg_b[0][c], kg_b[1][c], qg_b[0][c], qg_b[1][c]], writes=[PB[pa]])
            if t < NT - 1:
                if p == 1:
                    for l in range(2):
                        P.op("act", ACTV(kstT[par][:, l, :], kgT[:, l, tb], AF.Copy, scale=dec[:, l, t:t + 1]),
                             reads=[kg_b[l][c], dec_b[l][c]], writes=[kstT_b[par]])
                else:
                    P.op("dve", TT(kstT[par], kgT[:, :, tb], dec[:, :, t:t + 1].to_broadcast([128, 2, 128]), ALU.mult),
                         reads=[kg_b[0][c], kg_b[1][c], dec_b[0][c], dec_b[1][c]], writes=[kstT_b[par]])
                P.op("pe", [TR(bank_b(PKT_)[:, l * 128:(l + 1) * 128], kstT[par][:, l, :], ident_b) for l in range(2)], reads=[kstT_b[par], cst], writes=[PB[PKT_]])
            P.op("dve", TT(AmT[par], r3(bank(pa)[:, 0:256], 2), mask01.unsqueeze(1).to_broadcast([128, 2, 128]), ALU.mult), reads=[PB[pa], cst], writes=[AmT_b[par]])
            if t < NT - 1:
                P.op("act", ACTV(kstt[par], r3(bank_b(PKT_)[:, 0:256], 2), AF.Copy), reads=[PB[PKT_]], writes=[kstt_b[par]])

        def st1(t):
            tb = slice(t * 128, (t + 1) * 128)
            c = t // 4
            par = t % 2
            po = PO_[par]
            fns = []
            for l in range(2):
                o_ap = bank(po)[:, l * 256:(l + 1) * 256]
                fns.append(MM(o_ap, AmT[par][:, l, :], vg[:, t, l * 256:(l + 1) * 256], start=True, stop=(t == 0)))
                if t > 0:
                    fns.append(MM(o_ap, qgT[:, l, tb], Sbf[par][:, l, :], start=False, stop=True))
            P.op("pe", fns, reads=[AmT_b[par], vg_b[t], Sbf_b[par], qg_b[0][c], qg_b[1][c]], writes=[PB[po]])
            if t < NT - 1:
                P.op("pe", [MM(bank(PS_)[:, l * 256:(l + 1) * 256], kstt[par][:, l, :], vg[:, t, l * 256:(l + 1) * 256]) for l in range(2)],
                     reads=[kstt_b[par], vg_b[t]], writes=[PB[PS_]])
                for l in range(2):
                    s_ap = bank(PS_)[:, l * 256:(l + 1) * 256]
                    if t == 0:
                        P.op("dve", CP(Sst[:, l, :], s_ap), reads=[PB[PS_]], writes=[Sst_b[l]])
                    else:
                        P.op("dve", STT(Sst[:, l, :], Sst[:, l, :], dec[:, l, t:t + 1], s_ap, ALU.mult, ALU.add),
                             reads=[PB[PS_], dec_b[l][c], Sst_b[l]], writes=[Sst_b[l]])
                P.op("act", ACTV(Sbf[1 - par].rearrange("p l v -> p (l v)"), Sst.rearrange("p l v -> p (l v)"), AF.Copy), reads=Sst_b, writes=[Sbf_b[1 - par]])
            for l in range(2):
                o_ap = bank(po)[:, l * 256:(l + 1) * 256]
                P.op("act", ACTV(junk2[l], o_ap, AF.Square, accum_out=ss4[par][:, l:l + 1]), reads=[PB[po]], writes=[junk2_b[l], ss4_b[par]])
            P.op("pool", TS(ss4[par][:, 2:4], ss4[par][:, 0:2], 1.0 / 256, ALU.mult, EPS, ALU.add), reads=[ss4_b[par]], writes=[ss4_b[par]])
            P.op("pool", TT(ss4[par][:, 4:6], ss4[par][:, 2:4], mhalf[:, 0:2], ALU.pow), reads=[ss4_b[par], cst], writes=[ss4_b[par]])

        def st2a(t):
            par = t % 2
            po = PO_[par]
            for l in range(2):
                P.op("dve", TS(ogn[par][:, l * 256:(l + 1) * 256], bank(po)[:, l * 256:(l + 1) * 256], ss4[par][:, 4 + l:5 + l], ALU.mult),
                     reads=[PB[po], ss4_b[par]], writes=[ogn_b[par]])

        def st2b(t):
            tb = slice(t * 128, (t + 1) * 128)
            par = t % 2
            ptb = PT_[par]
            P.op("pe", [TR(bank_b(ptb)[:, kc * 128:(kc + 1) * 128], ogn[par][:, kc * 128:(kc + 1) * 128], ident_b) for kc in range(4)],
                 reads=[ogn_b[par], cst], writes=[PB[ptb]])
            zsl = zgT[:, 4 * p:4 * p + 4, tb]
            P.op("dve", TT(zsl, r3(bank_b(ptb)[:, 0:512], 4), zsl, ALU.mult), reads=[PB[ptb]] + [zg_b[4 * p + m][t] for m in range(4)],
                 writes=[zg_b[4 * p + m][t] for m in range(4)])

        if p == 0:
            for blk in range(2):
                zgw[blk] = stream(win_d, C_ZG + 512 * p + blk * 256, 256)
        if p == 0:
            for w_ in range(2):
                fill_w[w_] = stream(win_d, (C_QG, C_KG)[w_] + 256, 256)
            PRE["vg1"] = [stream(win_d, C_VG + 512 + blk * 256, 256) for blk in range(2)]
        else:
            PRE["F0"] = (stream(wpm_d, 0, 256, kcn=4), stream(win_d, C_GM, 256))
        for i in range(NT + 3):
            if i < NT:
                zg_fill(i)
            if p == 0 and 1 <= i <= NT:
                filler(i - 1)
            if i < NT:
                st0(i)
            if 0 <= i - 1 < NT:
                st1(i - 1)
            if 0 <= i - 2 < NT:
                st2a(i - 2)
            if 0 <= i - 3 < NT:
                st2b(i - 3)

    gla_pass(0)
    if STOP_AFTER == "gla0":
        tap("zgT", zgT)
        return fin()
    gla_pass(1)
    tap("zgT", zgT)
    if STOP_AFTER == "gla":
        return fin()

    A.release(m_gla)
    ARENA_LOG.append(("end-gla", A.peak, A.top))
    ib_F0 = len(ALLBUFS)
    mT = r3(A.tile(BF16, 8 * S), 8)
    mT_b = [[Buf(f"m{m}_{c}") for c in range(NC4)] for m in range(8)]
    woutb = r3(A.tile(BF16, 8 * D), 8)
    woutb_b = Buf("wout")
    gfin_bc = A.tile(F32, D)
    ggl_t = A.tile(F32, 2)
    yt = [A.tile(F32, 512) for _ in range(2)]
    yt_b = [Buf("yt0"), Buf("yt1")]
    yu = [A.tile(BF16, 512) for _ in range(2)]
    yu_b = [Buf("yu0"), Buf("yu1")]
    xs = [A.tile(F32, D) for _ in range(NXB)]
    xs_b = [Buf(f"xsF{i}") for i in range(NXB)]
    oout = [A.tile(F32, D) for _ in range(2)]
    oout_b = [Buf("oout0"), Buf("oout1")]
    fst = A.tile(F32, 3 * NT)
    fst_b = [Buf(f"fst{t}") for t in range(NT)]

    gfin_b = Buf("gfin")
    inherit(ALLBUFS[ib_F0:], ALLBUFS[ib_glap:ib_F0])

    ARENA_LOG.append(("F", A.top))

    def phaseF():
        P.dma("sp", gfin_bc, gfin_d.to_broadcast([128, D]), "d_m5", writes=[gfin_b])
        P.dma("sp", ggl_t, ggla_d.rearrange("o (k p) -> p (o k)", p=128), "d_m6", writes=[gfin_b], allow_slow_non_contiguous=True)
        it = 0
        for blk in range(4):
            if blk == 0 and "F0" in PRE:
                (wm3, wm_b), (wgm3, wgm_b) = PRE["F0"]
            else:
                wm3, wm_b = stream(wpm_d, blk * 256, 256, kcn=4)
                wgm3, wgm_b = stream(win_d, C_GM + blk * 256, 256)
            def gla_weights(blk=blk):
                jg = wctr["w"] % NWB
                wctr["w"] += 1
                wg3 = r3(wbuf[jg][:, 0:8 * 256], 8)
                P.dma("pool", wg3, k3(wpg_d)[:, :, blk * 256:(blk + 1) * 256], "d_" + wbuf_b[jg].name, writes=[wbuf_b[jg]])
                for par in range(2):
                    dst4 = wg3.rearrange("p (k two) n -> p k two n", two=2)[:, :, par, :]
                    P.op("dve", TS(dst4, dst4, ggl_t[:, par:par + 1], ALU.mult), reads=[gfin_b], writes=[wbuf_b[jg]])
                wgg3, wgg_b = stream(win_d, C_GG + blk * 256, 256)
                return jg, wg3, wgg3, wgg_b

            if blk > 0:
                jg, wg3, wgg3, wgg_b = gla_weights()
            for mm in range(2):
                m = blk * 2 + mm
                for c in range(NC4):
                    sl = slice(c * 512, (c + 1) * 512)
                    py, pg = (it % 2) * 2, (it % 2) * 2 + 1
                    k = it % 2
                    it += 1
                    P.op("pe", [MM(bank(py), wm3[:, kc, mm * 128:(mm + 1) * 128], oT[:, kc, sl], start=(kc == 0), stop=(kc == 3)) for kc in range(4)],
                         reads=[wm_b] + [oT_b[h][c] for h in range(8)], writes=[PB[py]])
                    P.op("pe", proj_fns(lambda kc: wgm3[:, kc, mm * 128:(mm + 1) * 128], 128, c, pg), reads=[wgm_b] + hTc(c), writes=[PB[pg]])
                    P.op("act", ACTV(yt[k], bank(pg), AF.Tanh, scale=0.5), reads=[PB[pg]], writes=[yt_b[k]])
                    P.op("dve", STT(mT[:, m, sl], yt[k], 1.0, bank(py), ALU.add, ALU.mult), reads=[PB[py], yt_b[k]], writes=[mT_b[m][c]])
            if blk == 0:
                jg, wg3, wgg3, wgg_b = gla_weights()
            for mm in range(2):
                m = blk * 2 + mm
                for c in range(NC4):
                    sl = slice(c * 512, (c + 1) * 512)
                    py, pg = (it % 2) * 2, (it % 2) * 2 + 1
                    k = it % 2
                    it += 1
                    P.op("pe", [MM(bank(py), wg3[:, kc, mm * 128:(mm + 1) * 128], zgT[:, kc, sl], start=(kc == 0), stop=(kc == 7)) for kc in range(8)],
                         reads=[wbuf_b[jg]] + [zg_b[kc][tt] for kc in range(8) for tt in range(4 * c, 4 * c + 4)], writes=[PB[py]])
                    P.op("pe", proj_fns(lambda kc: wgg3[:, kc, mm * 128:(mm + 1) * 128], 128, c, pg), reads=[wgg_b] + hTc(c), writes=[PB[pg]])
                    P.op("act", ACTV(yt[k], bank(pg), AF.Tanh, scale=0.5), reads=[PB[pg]], writes=[yt_b[k]])
                    P.op("dve", STT(yu[k], yt[k], 1.0, bank(py), ALU.add, ALU.mult), reads=[PB[py], yt_b[k]], writes=[yu_b[k]])
                    P.op("dve", TT(mT[:, m, sl], mT[:, m, sl], yu[k], ALU.add), reads=[yu_b[k], mT_b[m][c]], writes=[mT_b[m][c]])
        for blk in range(4):
            load_block(k3(wout_d)[:, :, blk * 256:(blk + 1) * 256], 8, 256, woutb[:, :, blk * 256:(blk + 1) * 256], woutb_b)
        def f2_A(t):
            i = t % NXB
            P.dma("sp", xs[i], x_d[t * 128:(t + 1) * 128, :], f"d_xs{i}", writes=[xs_b[i]])
            for hf in range(2):
                pb = 4 + hf + 2 * (t % 2)
                P.op("pe", [MM(bank(pb), mT[:, kc, t * 128:(t + 1) * 128], woutb[:, kc, hf * 512:(hf + 1) * 512], start=(kc == 0), stop=(kc == 7)) for kc in range(8)],
                     reads=[woutb_b] + [mT_b[kc][t // 4] for kc in range(8)], writes=[PB[pb]])
                xh = xs[i][:, hf * 512:(hf + 1) * 512]
                P.op("dve", STT(xh, bank(pb), 0.5, xh, ALU.mult, ALU.add), reads=[PB[pb], xs_b[i]], writes=[xs_b[i]])
            ss = fst[:, 3 * t:3 * t + 1]
            ln = fst[:, 3 * t + 1:3 * t + 2]
            rs = fst[:, 3 * t + 2:3 * t + 3]
            P.op("act", ACTV(junk, xs[i], AF.Square, accum_out=ss), reads=[xs_b[i]], writes=[junk_b, fst_b[t]])
            P.op("act", ACTV(ln, ss, AF.Ln, scale=1.0 / D, bias=eps_t), reads=[fst_b[t], cst], writes=[fst_b[t]])
            P.op("act", ACTV(rs, ln, AF.Exp, scale=-0.5), reads=[fst_b[t]], writes=[fst_b[t]])

        def f2_B(t):
            i = t % NXB
            k = t % 2
            rs = fst[:, 3 * t + 2:3 * t + 3]
            P.op("dve", STT(oout[k], xs[i], rs, gfin_bc, ALU.mult, ALU.mult), reads=[xs_b[i], fst_b[t], gfin_b], writes=[oout_b[k]])
            P.dma("pool", out_d[t * 128:(t + 1) * 128, :], oout[k], f"d_out{k}", reads=[oout_b[k]])

        for t in range(NT + 1):
            if t < NT:
                f2_A(t)
            if t - 1 >= 0:
                f2_B(t - 1)

    phaseF()
    tap("mT", mT)

    return fin()


_NC_CACHE = {}


def kernel(x, positions, g_in, w_in, g_q, w_uq, g_kv, w_ukv, w_gla_gate, b_gla_gate,
           g_gla, w_proj_mla, w_proj_gla, w_out, g_final):
    f = lambda a: np.ascontiguousarray(np.asarray(a, dtype=np.float32))
    x = f(x)
    positions = np.ascontiguousarray(np.asarray(positions, dtype=np.int32))
    shared = {
        "g_in": f(g_in).reshape(1, D), "w_in": f(w_in).reshape(D, DIN),
        "g_q": f(g_q).reshape(1, 384), "w_uq": f(w_uq).reshape(384, 768),
        "g_kv": f(g_kv).reshape(1, 256), "w_ukv": f(w_ukv).reshape(256, 1024),
        "w_gla_gate": f(w_gla_gate).reshape(16, 512), "b_gla_gate": f(b_gla_gate).reshape(1, 512),
        "g_gla": f(g_gla).reshape(1, 256), "w_proj_mla": f(w_proj_mla).reshape(512, D),
        "w_proj_gla": f(w_proj_gla).reshape(D, D), "w_out": f(w_out).reshape(D, D),
        "g_final": f(g_final).reshape(1, D),
    }
    if "nc" not in _NC_CACHE:
        _NC_CACHE["nc"] = build_program()
    nc = _NC_CACHE["nc"]
    in_maps = []
    for b in range(8):
        m = dict(shared)
        m["x"] = x[b]
        m["pos"] = positions[b].reshape(1, S)
        in_maps.append(m)
    res = run_bass_kernel_spmd(nc, in_maps, core_ids=list(range(8)))
    _NC_CACHE["last"] = res
    return np.stack([np.asarray(r["out"], dtype=np.float32) for r in res.results], axis=0)
```

```python
import math
import numpy as np
import concourse.bass as bass
import concourse.mybir as mybir
from concourse.bass_utils import run_bass_kernel_spmd

F32 = mybir.dt.float32
BF16 = mybir.dt.bfloat16
I32 = mybir.dt.int32
AF = mybir.ActivationFunctionType
ALU = mybir.AluOpType

ENGS = ("pe", "act", "dve", "pool", "sp")
DSZ = {F32: 4, BF16: 2, I32: 4}

S = 2048
D = 1024
NT = 16
NC4 = 4
EPS = 1e-6
DIN = 6320
C_CQ, C_CKV, C_KR, C_ZM, C_QG, C_KG, C_VG, C_ALR, C_ZG, C_GM, C_GG = (
    0, 384, 640, 672, 1184, 1696, 2208, 3232, 3248, 4272, 5296)

DEBUG_TAPS = None
STOP_AFTER = None


ALLBUFS = []


def inherit(new_bufs, old_bufs):
    toks = set()
    for o in old_bufs:
        if o.w is not None:
            toks.add(o.w)
        toks.update(o.r)
    for n in new_bufs:
        n.r = list(set(n.r) | toks)


class Buf:
    __slots__ = ("name", "w", "r", "excl")

    def __init__(self, name, excl=False):
        self.name = name
        self.w = None
        self.r = []
        self.excl = excl
        ALLBUFS.append(self)


class Prog:
    def __init__(self, nc):
        self.nc = nc
        self.q = {e: [] for e in ENGS}
        self.cnt = {}
        self.waited = {e: {} for e in ENGS}
        self.semh = {}
        self.gfence = []
        for e in ENGS:
            self._sem("E_" + e)

    def barrier(self):
        self.gfence = self.fence()

    def _sem(self, name):
        if name not in self.semh:
            self.semh[name] = self.nc.alloc_semaphore(name)
            self.cnt[name] = 0
        return self.semh[name]

    def _waits(self, eng, deps):
        w = self.waited[eng]
        best = {}
        for d in deps:
            if d is None:
                continue
            sn, v = d
            if eng == "pe" and sn == "E_pe":
                continue
            if w.get(sn, 0) >= v:
                continue
            if best.get(sn, 0) < v:
                best[sn] = v
        for sn, v in best.items():
            w[sn] = v
        return list(best.items())

    def _deps(self, reads, writes, extra, eng=None):
        deps = []
        for b in reads:
            if b.w is not None:
                deps.append(b.w)
            if b.excl:
                own = "E_" + str(eng)
                deps.extend(t for t in b.r if t[0] != own)
        for b in writes:
            if b.w is not None:
                deps.append(b.w)
            deps.extend(b.r)
        deps.extend(extra)
        deps.extend(self.gfence)
        return deps

    def op(self, eng, fns, reads=(), writes=(), extra=()):
        if callable(fns):
            fns = [fns]
        waits = self._waits(eng, self._deps(reads, writes, extra, eng))
        sn = "E_" + eng
        self.cnt[sn] += 1
        tok = (sn, self.cnt[sn])
        self.q[eng].append((waits, fns, (sn, 1)))
        for b in reads:
            b.r.append(tok)
        for b in writes:
            b.w = tok
            b.r = []
        return tok

    def dma(self, queue, out, in_, sem, reads=(), writes=(), extra=(), **kw):
        self._sem(sem)
        waits = self._waits(queue, self._deps(reads, writes, extra, queue))
        self.cnt[sem] += 16
        tok = (sem, self.cnt[sem])
        self.q[queue].append((waits, [lambda e: e.dma_start(out=out, in_=in_, **kw)], (sem, 16)))
        for b in reads:
            b.r.append(tok)
        for b in writes:
            b.w = tok
            b.r = []
        return tok

    def fence(self):
        return [(sn, v) for sn, v in self.cnt.items() if v > 0]

    def wait_all(self, eng):
        waits = self._waits(eng, self.fence())
        self.q[eng].append((waits, [], None))

    def emit(self):
        nc = self.nc
        hand = {"pe": "tensor", "act": "scalar", "dve": "vector", "pool": "gpsimd", "sp": "sync"}
        with nc.Block() as block:
            for e in ENGS:
                items = self.q[e]

                def body(engine, items=items):
                    for waits, fns, inc in items:
                        for sn, v in waits:
                            engine.wait_ge(self.semh[sn], v)
                        for i, f in enumerate(fns):
                            ins = f(engine)
                            if i == len(fns) - 1 and inc is not None:
                                ins.then_inc(self.semh[inc[0]], inc[1])

                getattr(block, hand[e])(body)


class Arena:
    def __init__(self, nc, nbytes):
        self.t = nc.alloc_sbuf_tensor("arena", [128, nbytes // 4], F32)
        self.nbytes = nbytes
        self.top = 0
        self.views = {F32: self.t}
        self.peak = 0

    def tile(self, dtype, n, align=64):
        sz = DSZ[dtype]
        off = (self.top + align - 1) // align * align
        assert off + n * sz <= self.nbytes, ("SBUF arena overflow", off, n * sz, self.nbytes)
        self.top = off + n * sz
        self.peak = max(self.peak, self.top)
        if dtype not in self.views:
            self.views[dtype] = self.t.bitcast(dtype)
        return self.views[dtype][:, off // sz: off // sz + n]

    def mark(self):
        return self.top

    def release(self, m):
        self.top = m


def MM(out, lhsT, rhs, start=True, stop=True):
    return lambda e: e.matmul(out, lhsT=lhsT, rhs=rhs, start=start, stop=stop)


def TR(out, in_, ident):
    return lambda e: e.transpose(out=out, in_=in_, identity=ident)


def ACTV(out, in_, func, **kw):
    return lambda e: e.activation(out=out, in_=in_, func=func, **kw)


def TT(out, in0, in1, op):
    return lambda e: e.tensor_tensor(out=out, in0=in0, in1=in1, op=op)


def TS(out, in0, s1, op0, s2=None, op1=None):
    if op1 is None:
        return lambda e: e.tensor_scalar(out=out, in0=in0, scalar1=s1, scalar2=None, op0=op0)
    return lambda e: e.tensor_scalar(out=out, in0=in0, scalar1=s1, scalar2=s2, op0=op0, op1=op1)


def STT(out, in0, scalar, in1, op0, op1):
    return lambda e: e.scalar_tensor_tensor(out=out, in0=in0, scalar=scalar, in1=in1, op0=op0, op1=op1)


def CP(out, in_):
    return lambda e: e.tensor_copy(out=out, in_=in_)


def MS(ap, v):
    return lambda e: e.memset(ap, v)


def r3(ap, a):
    return ap.rearrange("p (a b) -> p a b", a=a)


def build_program():
    nc = bass.Bass("TRN2", target_bir_lowering=False)

    def din(name, shape, dt=F32):
        return nc.dram_tensor(name, shape, dt, kind="ExternalInput").ap()

    x_d = din("x", [S, D])
    pos_d = din("pos", [1, S], I32)
    gin_d = din("g_in", [1, D])
    win_d = din("w_in", [D, DIN])
    gq_d = din("g_q", [1, 384])
    wuq_d = din("w_uq", [384, 768])
    gkv_d = din("g_kv", [1, 256])
    wukv_d = din("w_ukv", [256, 1024])
    wgg_d = din("w_gla_gate", [16, 512])
    bgg_d = din("b_gla_gate", [1, 512])
    ggla_d = din("g_gla", [1, 256])
    wpm_d = din("w_proj_mla", [512, D])
    wpg_d = din("w_proj_gla", [D, D])
    wout_d = din("w_out", [D, D])
    gfin_d = din("g_final", [1, D])
    out_d = nc.dram_tensor("out", [S, D], F32, kind="ExternalOutput").ap()

    del ALLBUFS[:]
    P = Prog(nc)
    A = Arena(nc, 212700)
    psum = nc.alloc_psum_tensor("psum", [128, 4096], F32)
    psum_b = psum.bitcast(BF16)
    PB = [Buf(f"ps{i}", excl=True) for i in range(8)]
    tapped = {}
    ARENA_LOG = []

    def fin():
        import os
        if os.environ.get("MK_VERBOSE"):
            print("arena marks", ARENA_LOG, "peak", A.peak)
        for nme, ap in tapped.items():
            dd = nc.dram_tensor("tap_" + nme, list(ap.shape), ap.dtype, kind="ExternalOutput").ap()
            P.wait_all("sp")
            P.dma("sp", dd, ap, "d_tap")
        P.wait_all("sp")
        P.emit()
        return nc

    def tap(nme, ap):
        if DEBUG_TAPS and nme in DEBUG_TAPS:
            tapped[nme] = ap

    def bank(i):
        return psum[:, i * 512:(i + 1) * 512]

    def bank_b(i):
        return psum_b[:, i * 1024:(i + 1) * 1024]

    hT = r3(A.tile(BF16, 8 * S), 8)
    hT_b = [Buf(f"hT{t}") for t in range(NT)]
    ident_b = A.tile(BF16, 128)
    ident_f = A.tile(F32, 128)
    ones_b = A.tile(BF16, 128)
    maskb = A.tile(BF16, 128)
    mask01 = A.tile(BF16, 128)
    iota_i = A.tile(I32, 128)
    eps_t = A.tile(F32, 1)
    mhalf = A.tile(F32, 2)
    lnq_t = A.tile(F32, 1)
    junk = A.tile(BF16, D)
    junk_b = Buf("junk")
    cst = Buf("consts")
    gsm = Buf("gsm")
    NWB = 7
    wbuf = [A.tile(BF16, 2048) for _ in range(NWB)]
    wbuf_b = [Buf(f"wb{i}") for i in range(NWB)]
    oT = r3(A.tile(BF16, 4 * S), 4)
    oT_b = [[Buf(f"o{h}_{c}") for c in range(NC4)] for h in range(8)]
    zu = [A.tile(BF16, 512) for _ in range(3)]
    zu_b = [Buf(f"zu{i}") for i in range(3)]
    rn = [A.tile(F32, 512) for _ in range(2)]
    rn_b = [Buf("rn0"), Buf("rn1")]
    m_glob = A.mark()
    ib_glob = len(ALLBUFS)

    P.op("pool", lambda e: e.iota(iota_i, [[1, 128]], base=0, channel_multiplier=-1), writes=[cst])
    P.op("dve", TS(ident_f, iota_i, 0, ALU.is_equal), reads=[cst], writes=[cst])
    P.op("dve", CP(ident_b, ident_f), reads=[cst], writes=[cst])
    P.op("dve", TS(maskb, iota_i, 0, ALU.is_lt, -30000.0, ALU.mult), reads=[cst], writes=[cst])
    P.op("dve", TS(mask01, iota_i, 0, ALU.is_ge), reads=[cst], writes=[cst])
    P.op("pool", MS(ones_b, 1.0), writes=[cst])
    P.op("pool", MS(eps_t, EPS), writes=[cst])
    P.op("pool", MS(mhalf, -0.5), writes=[cst])
    P.op("pool", MS(lnq_t, math.log(128.0 ** -0.5)), writes=[cst])

    wctr = {"s": 0, "w": 0}

    def k3(dram2d):
        return dram2d.rearrange("(kc p) n -> p kc n", p=128)

    def load_block(src3, kcn, ncols, dst3, dst_buf, scale_ap=None, scale_eng="dve", sem=None):
        P.dma("pool", dst3, src3, sem or ("d_" + dst_buf.name), writes=[dst_buf])
        if scale_ap is not None:
            P.op(scale_eng, TT(dst3, dst3, scale_ap, ALU.mult), reads=[gsm], writes=[dst_buf])

    def stream(dram2d, c0, ncols, kcn=8):
        j = wctr["w"] % NWB
        wctr["w"] += 1
        dst3 = r3(wbuf[j][:, 0:kcn * ncols], kcn)
        load_block(k3(dram2d)[:, :, c0:c0 + ncols], kcn, ncols, dst3, wbuf_b[j])
        return dst3, wbuf_b[j]

    evac_rr = {"i": 0}

    def evac_copy(out, in_, reads, writes):
        evac_rr["i"] += 1
        if evac_rr["i"] % 2:
            return P.op("act", ACTV(out, in_, AF.Copy), reads=reads, writes=writes)
        return P.op("dve", CP(out, in_), reads=reads, writes=writes)

    def proj_fns(lhs_of_kc, m_rows, c, pb):
        return [MM(bank(pb)[0:m_rows, :], lhs_of_kc(kc), hT[:, kc, c * 512:(c + 1) * 512], start=(kc == 0), stop=(kc == 7))
                for kc in range(8)]

    def hTc(c):
        return hT_b[4 * c:4 * c + 4]

    ropeT = A.tile(F32, S)
    cos2T = ropeT[0:32, :]
    sin2T = ropeT[32:64, :]
    rope_b = Buf("ropeT")
    cqT = r3(A.tile(BF16, 3 * S), 3)
    ckvT = r3(A.tile(BF16, 2 * S), 2)
    cq_b = [Buf(f"cq{c}") for c in range(NC4)]
    ckv_b = [Buf(f"ckv{c}") for c in range(NC4)]
    krope = A.tile(BF16, S)
    krope_b = [Buf(f"krope{c}") for c in range(NC4)]
    scr_off = (A.top + 63) // 64 * 64
    ktmp = [A.tile(F32, 512) for _ in range(2)]
    ktmp_b = Buf("ktmp")
    sq = [A.tile(BF16, 512) for _ in range(4)]
    sq_b = [Buf(f"sq{i}") for i in range(4)]
    rtmp = [A.tile(F32, 512) for _ in range(4)]
    rtmp_b = [Buf(f"rt{i}") for i in range(4)]
    assert A.top - scr_off == 16384, (A.top, scr_off)
    sT = r3(A.views[BF16][:, scr_off // 2: scr_off // 2 + 4 * S], 4)
    wkrot = r3(A.tile(BF16, 8 * 32), 8)
    wkrot_b = Buf("wkrot")

    wuq = r3(A.tile(BF16, 3 * 768), 3)
    wuqrot = A.tile(BF16, 3 * 8 * 96).rearrange("p (k h e) -> p k h e", k=3, h=8)
    wukv = r3(A.tile(BF16, 2 * 1024), 2)
    wuq_b, wuqrot_b, wukv_b = Buf("wuq"), Buf("wuqrot"), Buf("wukv")
    gq_t = A.tile(F32, 3)
    gkv_t = A.tile(F32, 2)
    m_rope = A.mark()
    ib_tmp0 = len(ALLBUFS)
    posi = A.tile(I32, NT)
    posf = A.tile(F32, NT)
    posT_i = A.tile(I32, 128)
    posT_f = A.tile(F32, 128)
    freqrow = A.tile(F32, 32)
    ang = A.tile(F32, 512)
    angc = A.tile(F32, 512)
    ki = A.tile(I32, 512)
    kf = A.tile(F32, 512)
    r1 = A.tile(F32, 512)
    r2 = A.tile(F32, 512)
    fx = A.tile(F32, 512)
    sc_tok = A.tile(F32, 1024)
    rp = Buf("rope_tmp")
    rfin = [A.tile(F32, 512) for _ in range(2)]
    fr3 = freqrow.rearrange("p (two j) -> p two j", two=2)
    for j in range(16):
        fj = 10000.0 ** (-j / 16.0)
        P.op("pool", MS(fr3[:, :, j:j + 1], fj), writes=[rp])
    rope_items = []

    def ritem(eng, fn, reads, writes):
        rope_items.append(lambda: P.op(eng, fn, reads=reads, writes=writes))

    ritem("dve", CP(posT_f[0:NT, :], posT_i[0:NT, :]), [rp], [rp])
    rope_items.append(lambda: P.op("pe", TR(bank(0)[:, 0:NT], posT_f[0:NT, :], ident_f[0:NT, 0:NT]), reads=[rp, cst], writes=[PB[0]]))
    rope_items.append(lambda: P.op("dve", CP(posf, bank(0)[:, 0:NT]), reads=[PB[0]], writes=[rp]))
    ritem("dve", TT(r3(ang, NT), posf.unsqueeze(2).to_broadcast([128, NT, 32]),
                    freqrow.unsqueeze(1).to_broadcast([128, NT, 32]), ALU.mult), [rp], [rp])
    ritem("dve", TS(angc, ang, math.pi / 2, ALU.add), [rp], [rp])
    TWO_PI = 2 * math.pi
    C1 = 6.28125
    C2 = TWO_PI - C1
    for idx, src in enumerate((ang, angc)):
        ritem("dve", TS(ki, src, 1.0 / TWO_PI, ALU.mult), [rp], [rp])
        ritem("dve", CP(kf, ki), [rp], [rp])
        ritem("dve", STT(r1, kf, -C1, src, ALU.mult, ALU.add), [rp], [rp])
        ritem("dve", STT(r2, kf, -C2, r1, ALU.mult, ALU.add), [rp], [rp])
        ritem("dve", TS(fx, r2, math.pi, ALU.is_gt, -TWO_PI, ALU.mult), [rp], [rp])
        ritem("dve", TT(r1, r2, fx, ALU.add), [rp], [rp])
        ritem("dve", TS(fx, r1, -math.pi, ALU.is_lt, TWO_PI, ALU.mult), [rp], [rp])
        ritem("dve", TT(r2, r1, fx, ALU.add), [rp], [rp])
        ritem("dve", TS(rfin[idx], r2, math.pi, ALU.min, -math.pi, ALU.max), [rp], [rp])
    rope_tail = []
    for idx in range(2):
        rope_tail.append(lambda idx=idx: P.op("act", ACTV(sc_tok[:, idx * 512:(idx + 1) * 512], rfin[idx], AF.Sin), reads=[rp], writes=[rp]))
    sc4 = sc_tok.rearrange("p (s t f) -> p s t f", s=2, t=NT)

    def rope_tr(idx, dstT, g):
        def f():
            pb = g % 2
            fns = [TR(bank(pb)[0:32, k * 128:(k + 1) * 128], sc4[:, idx, g * 4 + k, :], ident_f) for k in range(4)]
            P.op("pe", fns, reads=[rp, cst], writes=[PB[pb]])
            P.op("dve", CP(dstT[:, g * 512:(g + 1) * 512], bank(pb)[0:32, :]), reads=[PB[pb]], writes=[rope_b])
        return f

    for idx, dstT in ((1, cos2T), (0, sin2T)):
        for g in range(4):
            rope_tail.append(rope_tr(idx, dstT, g))

    gin_bc = A.tile(F32, D)
    NXB = 4
    xs = [A.tile(F32, D) for _ in range(NXB)]
    xs_b = [Buf(f"xs{i}") for i in range(NXB)]
    xn = [A.tile(BF16, D) for _ in range(2)]
    xn_b = [Buf(f"xn{i}") for i in range(2)]
    st0 = A.tile(F32, 3 * NT)
    st0_b = [Buf(f"st0_{t}") for t in range(NT)]

    def p0_load(t):
        i = t % NXB
        P.dma("sp", xs[i], x_d[t * 128:(t + 1) * 128, :], f"d_xs{i}", writes=[xs_b[i]])

    def p0_A(t):
        i = t % NXB
        if t + 2 < NT and t + 2 >= NXB - 1:
            p0_load(t + 2)
        ss = st0[:, 3 * t:3 * t + 1]
        ln = st0[:, 3 * t + 1:3 * t + 2]
        rs = st0[:, 3 * t + 2:3 * t + 3]
        P.op("act", ACTV(junk, xs[i], AF.Square, accum_out=ss), reads=[xs_b[i]], writes=[junk_b, st0_b[t]])
        P.op("act", ACTV(ln, ss, AF.Ln, scale=1.0 / D, bias=eps_t), reads=[st0_b[t], cst], writes=[st0_b[t]])
        P.op("act", ACTV(rs, ln, AF.Exp, scale=-0.5), reads=[st0_b[t]], writes=[st0_b[t]])

    def p0_B(t):
        i = t % NXB
        rs = st0[:, 3 * t + 2:3 * t + 3]
        k = t % 2
        P.op("dve", STT(xn[k], xs[i], rs, gin_bc, ALU.mult, ALU.mult), reads=[xs_b[i], st0_b[t], gin_b], writes=[xn_b[k]])
        pb = 2 + (t % 2)
        fns = [TR(bank_b(pb)[:, kc * 128:(kc + 1) * 128], xn[k][:, kc * 128:(kc + 1) * 128], ident_b) for kc in range(8)]
        P.op("pe", fns, reads=[xn_b[k], cst], writes=[PB[pb]])

    def p0_C(t):
        pb = 2 + (t % 2)
        P.op("act", ACTV(hT[:, :, t * 128:(t + 1) * 128], r3(bank_b(pb), 8), AF.Copy), reads=[PB[pb]], writes=[hT_b[t]])

    ssq_ctr = {"i": 0, "pb": 0}
    ln_pending = []

    def ln_tick(flush=False):
        for it_ in ln_pending:
            it_[0] += 1
        while ln_pending and (flush or ln_pending[0][0] >= 2):
            ln_pending.pop(0)[1]()

    def latent_norm(dstT, dst_b, nchunk, width, srcs, c):
        ssq_pb = 6 + (ssq_ctr["i"] % 2)
        ssq_ctr["i"] += 1

        def ones_mm(j, k):
            def f():
                P.op("pe", MM(bank(ssq_pb), ones_b, sq[k], start=(j == 0), stop=(j == nchunk - 1)), reads=[sq_b[k], cst], writes=[PB[ssq_pb]])
                if j == nchunk - 1:
                    r0, r1_ = rtmp[(ssq_pb % 2) * 2], rtmp[(ssq_pb % 2) * 2 + 1]
                    rb0, rb1 = rtmp_b[(ssq_pb % 2) * 2], rtmp_b[(ssq_pb % 2) * 2 + 1]
                    P.op("act", ACTV(r0, bank(ssq_pb), AF.Ln, scale=1.0 / width, bias=eps_t), reads=[PB[ssq_pb], cst], writes=[rb0])
                    P.op("act", ACTV(r1_, r0, AF.Exp, scale=-0.5), reads=[rb0], writes=[rb1])
                    for jj in range(nchunk):
                        sl = dstT[:, jj, c * 512:(c + 1) * 512]
                        P.op("dve", TT(sl, sl, r1_, ALU.mult), reads=[rb1, dst_b[c]], writes=[dst_b[c]])
            return f

        for j, (w3, wb, c0) in enumerate(srcs):
            pb = 2 + ssq_ctr["pb"] % 4
            ssq_ctr["pb"] += 1
            P.op("pe", proj_fns(lambda kc: w3[:, kc, c0:c0 + 128], 128, c, pb), reads=[wb] + hTc(c), writes=[PB[pb]])
            k = ssq_ctr["pb"] % 4
            dsl = dstT[:, j, c * 512:(c + 1) * 512]
            P.op("act", ACTV(dsl, bank(pb), AF.Copy), reads=[PB[pb]], writes=[dst_b[c]])
            P.op("act", ACTV(sq[k], bank(pb), AF.Square), reads=[PB[pb]], writes=[sq_b[k]])
            ln_tick()
            ln_pending.append([0, ones_mm(j, k)])

    W1A = {}

    def phase1a_weights():
        w0, w0b = stream(win_d, 0, 256)
        w1, w1b = stream(win_d, 256, 256)
        w2, w2b = stream(win_d, 512, 160)
        W1A.update(w0=w0, w0b=w0b, w1=w1, w1b=w1b, w2=w2, w2b=w2b)

    def wkrot_prep():
        w2, w2b = W1A["w2"], W1A["w2b"]
        P.op("dve", TS(wkrot[:, :, 0:16], w2[:, :, 144:160], -1.0, ALU.mult), reads=[w2b], writes=[wkrot_b])
        P.op("dve", CP(wkrot[:, :, 16:32], w2[:, :, 128:144]), reads=[w2b], writes=[wkrot_b])

    def phase1a(after_chunk):
        w0, w0b, w1, w1b, w2, w2b = (W1A[k] for k in ("w0", "w0b", "w1", "w1b", "w2", "w2b"))
        for c in range(NC4):
            latent_norm(cqT, cq_b, 3, 384.0, [(w0, w0b, 0), (w0, w0b, 128), (w1, w1b, 0)], c)
            latent_norm(ckvT, ckv_b, 2, 256.0, [(w1, w1b, 128), (w2, w2b, 0)], c)
            P.op("pe", proj_fns(lambda kc: w2[:, kc, 128:160], 32, c, 4), reads=[w2b] + hTc(c), writes=[PB[4]])
            P.op("pe", proj_fns(lambda kc: wkrot[:, kc, :], 32, c, 5), reads=[wkrot_b] + hTc(c), writes=[PB[5]])
            sl = slice(c * 512, (c + 1) * 512)
            P.op("dve", TT(ktmp[0][0:32, :], bank(4)[0:32, :], cos2T[:, sl], ALU.mult), reads=[PB[4], rope_b], writes=[ktmp_b])
            P.op("dve", TT(ktmp[1][0:32, :], bank(5)[0:32, :], sin2T[:, sl], ALU.mult), reads=[PB[5], rope_b], writes=[ktmp_b])
            P.op("dve", TT(krope[0:32, sl], ktmp[0][0:32, :], ktmp[1][0:32, :], ALU.add), reads=[ktmp_b], writes=[krope_b[c]])
            ln_tick(flush=True)
            after_chunk(c)

    for t_ in range(NXB - 1):
        p0_load(t_)
    gin_b = Buf("gin")
    P.dma("sp", gin_bc, gin_d.to_broadcast([128, D]), "d_m2", writes=[gin_b])
    def tiny_dmas():
        P.dma("sp", posT_i[0:NT, :], pos_d.rearrange("o (t p) -> (o t) p", p=128), "d_m1", writes=[rp])
        P.dma("sp", gq_t, gq_d.rearrange("o (k p) -> p (o k)", p=128), "d_m3", writes=[gsm], allow_slow_non_contiguous=True)
        P.dma("sp", gkv_t, gkv_d.rearrange("o (k p) -> p (o k)", p=128), "d_m4", writes=[gsm], allow_slow_non_contiguous=True)
    phase1a_weights()
    wsc = []
    for hf in range(2):
        load_block(k3(wuq_d)[:, :, hf * 384:(hf + 1) * 384], 3, 384, wuq[:, :, hf * 384:(hf + 1) * 384], wuq_b, sem=f"d_wuq{hf}")
        wsc.append(lambda hf=hf: P.op("dve", TT(wuq[:, :, hf * 384:(hf + 1) * 384], wuq[:, :, hf * 384:(hf + 1) * 384],
                                                gq_t.unsqueeze(2).to_broadcast([128, 3, 384]), ALU.mult), reads=[gsm], writes=[wuq_b]))
    for hf in range(2):
        load_block(k3(wukv_d)[:, :, hf * 512:(hf + 1) * 512], 2, 512, wukv[:, :, hf * 512:(hf + 1) * 512], wukv_b, sem=f"d_wukv{hf}")
        wsc.append(lambda hf=hf: P.op("dve", TT(wukv[:, :, hf * 512:(hf + 1) * 512], wukv[:, :, hf * 512:(hf + 1) * 512],
                                                gkv_t.unsqueeze(2).to_broadcast([128, 2, 512]), ALU.mult), reads=[gsm], writes=[wukv_b]))
    P.op("pool", MS(wuqrot.rearrange("p k h e -> p (k h e)"), 0.0), writes=[wuqrot_b])
    for i in range(NT + 2):
        if i < NT:
            p0_A(i)
        if 0 <= i - 1 < NT:
            p0_B(i - 1)
        if 0 <= i - 2 < NT:
            p0_C(i - 2)
        if i == 3:
            tiny_dmas()
        for _ in range(3):
            if i >= 6 and rope_items:
                rope_items.pop(0)()
    while rope_items:
        rope_items.pop(0)()
    wkrot_prep()
    while wsc:
        wsc.pop(0)()
    wuq4 = wuq.rearrange("p k (h e) -> p k h e", h=8)
    P.op("dve", TS(wuqrot[:, :, :, 64:80], wuq4[:, :, :, 80:96], -1.0, ALU.mult), reads=[wuq_b], writes=[wuqrot_b])
    P.op("dve", CP(wuqrot[:, :, :, 80:96], wuq4[:, :, :, 64:80]), reads=[wuq_b], writes=[wuqrot_b])
    for f_ in rope_tail:
        f_()
    tap("hT", hT)
    if STOP_AFTER == "p0":
        return fin()
    ib_tmp1 = len(ALLBUFS)
    A.release(m_rope)
    ARENA_LOG.append(("pre-att", A.top))
    ib_att0 = len(ALLBUFS)
    ropet = [A.tile(F32, 512) for _ in range(4)]
    ropet_b = [Buf(f"ropet{i}") for i in range(4)]
    qT = [A.tile(BF16, S) for _ in range(2)]
    kT = [A.tile(BF16, S) for _ in range(2)]
    vaug = [r3(A.tile(BF16, NT * 128), NT) for _ in range(2)]
    qT_b = [[Buf(f"q{i}_{c}") for c in range(NC4)] for i in range(2)]
    kT_b = [[Buf(f"k{i}_{c}") for c in range(NC4)] for i in range(2)]
    va_b = [[Buf(f"va{i}_{g}") for g in range(2)] for i in range(2)]
    NPT = 4
    PT = [A.tile(BF16, 512) for _ in range(NPT)]
    PT_b = [Buf(f"PT{i}") for i in range(NPT)]
    rc = [A.tile(F32, 512) for _ in range(2)]
    rc_b = [Buf("rc0"), Buf("rc1")]
    zt = [A.tile(F32, 512) for _ in range(2)]
    zt_b = [Buf("zt0"), Buf("zt1")]
    inherit(ALLBUFS[ib_att0:], ALLBUFS[ib_tmp0:ib_tmp1])
    SCALE = 96.0 ** -0.5

    for i in range(2):
        P.op("pool", MS(vaug[i].rearrange("p t e -> p (t e)"), 1.0), writes=va_b[i])

    def phase2_pieces(h):
        i = h % 2
        pieces = []

        def q_piece(c):
            def f():
                sl = slice(c * 512, (c + 1) * 512)
                ba = 0
                P.op("pe", [MM(bank(ba)[0:96, :], wuq[:, kc, h * 96:(h + 1) * 96], cqT[:, kc, sl], start=(kc == 0), stop=(kc == 2)) for kc in range(3)],
                     reads=[wuq_b, cq_b[c]], writes=[PB[ba]])
                P.op("pe", [MM(bank(1)[0:96, :], wuqrot[:, kc, h, :], cqT[:, kc, sl], start=(kc == 0), stop=(kc == 2)) for kc in range(3)],
                     reads=[wuqrot_b, cq_b[c]], writes=[PB[1]])
                P.op("act" if h == 0 else "dve", (ACTV(qT[i][0:64, sl], bank(ba)[0:64, :], AF.Copy) if h == 0 else CP(qT[i][0:64, sl], bank(ba)[0:64, :])), reads=[PB[ba]], writes=[qT_b[i][c]])
                i0, i1 = (c % 2) * 2, (c % 2) * 2 + 1
                P.op("dve", TT(ropet[i0][64:96, :], bank(ba)[64:96, :], cos2T[:, sl], ALU.mult), reads=[PB[ba], rope_b], writes=[ropet_b[i0]])
                P.op("dve", TT(ropet[i1][64:96, :], bank(1)[64:96, :], sin2T[:, sl], ALU.mult), reads=[PB[1], rope_b], writes=[ropet_b[i1]])
                P.op("dve", TT(qT[i][64:96, sl], ropet[i0][64:96, :], ropet[i1][64:96, :], ALU.add), reads=[ropet_b[i0], ropet_b[i1]], writes=[qT_b[i][c]])
            return f

        def k_piece(c):
            def f():
                sl = slice(c * 512, (c + 1) * 512)
                pb = c % 2
                P.op("pe", [MM(bank(pb)[0:64, :], wukv[:, kc, h * 128:h * 128 + 64], ckvT[:, kc, sl], start=(kc == 0), stop=(kc == 1)) for kc in range(2)],
                     reads=[wukv_b, ckv_b[c]], writes=[PB[pb]])
                P.op("dve", CP(kT[i][0:64, sl], bank(pb)[0:64, :]), reads=[PB[pb]], writes=[kT_b[i][c]])
                P.op("dve", CP(kT[i][64:96, sl], krope[0:32, sl]), reads=[krope_b[c]], writes=[kT_b[i][c]])
            return f

        def v_piece(g):
            def f():
                pb = g % 2
                fns = []
                for tt in range(8):
                    t = g * 8 + tt
                    for kc in range(2):
                        fns.append(MM(bank(pb)[:, tt * 64:(tt + 1) * 64], ckvT[:, kc, t * 128:(t + 1) * 128],
                                      wukv[:, kc, h * 128 + 64:h * 128 + 128], start=(kc == 0), stop=(kc == 1)))
                P.op("pe", fns, reads=[wukv_b] + ckv_b[2 * g:2 * g + 2], writes=[PB[pb]])
                vo = 0 if h % 2 == 0 else 64
                P.op("dve", CP(vaug[i][:, g * 8:(g + 1) * 8, vo:vo + 64], r3(bank(pb), 8)), reads=[PB[pb]], writes=[va_b[i][g]])
            return f

        for c in range(NC4):
            pieces.append(q_piece(c))
            pieces.append(k_piece(c))
            if c % 2 == 1:
                pieces.append(v_piece(c // 2))
        return pieces

    def attention():
        steps = [(h, c, kt) for h in range(8) for c in range(NC4) for kt in range(4 * c + 4)]
        per_head = len(steps) // 8
        LA = 3

        def emit_S(n):
            h, c, kt = steps[n]
            i = h % 2
            j = kt - 4 * c
            n0 = 128 * j if j > 0 else 0
            spb = (2, 3, 4, 7)[n % 4]
            pti = n % NPT
            qs = slice(c * 512 + n0, (c + 1) * 512)
            fns = [MM(bank(spb)[:, n0:512], kT[i][0:96, kt * 128:(kt + 1) * 128], qT[i][0:96, qs], start=True, stop=(j < 0))]
            if j >= 0:
                fns.append(MM(bank(spb)[:, n0:n0 + 128], ident_b, maskb, start=False, stop=True))
            P.op("pe", fns, reads=[kT_b[i][kt // 4], qT_b[i][c], cst], writes=[PB[spb]])
            P.op("act", ACTV(PT[pti][:, n0:512], bank(spb)[:, n0:512], AF.Exp, scale=SCALE), reads=[PB[spb]], writes=[PT_b[pti]])

        def emit_PV(n):
            h, c, kt = steps[n]
            i = h % 2
            j = kt - 4 * c
            n0 = 128 * j if j > 0 else 0
            pti = n % NPT
            nk = 4 * c + 4
            oc = h * NC4 + c
            opb = 5 + (oc % 2)
            P.op("pe", MM(bank(opb)[:, n0:512], vaug[i][:, kt, :], PT[pti][:, n0:512], start=(kt == 0), stop=(kt == nk - 1)),
                 reads=[va_b[i][kt // 8], PT_b[pti]], writes=[PB[opb]])
            if kt == nk - 1:
                vlo, slo = (0, 64) if h % 2 == 0 else (64, 0)
                k = oc % 2
                P.op("dve", CP(oT[vlo:vlo + 64, h // 2, c * 512:(c + 1) * 512], bank(opb)[vlo:vlo + 64, :]), reads=[PB[opb]], writes=[oT_b[h][c]])
                P.op("dve", CP(sT[vlo:vlo + 64, h // 2, c * 512:(c + 1) * 512], bank(opb)[slo:slo + 64, :]), reads=[PB[opb]], writes=[sT_b[h][c]])

        nxt_pieces = []
        for n in range(len(steps) + LA):
            if n < len(steps):
                h, c, kt = steps[n]
                if c == 0 and kt == 0 and h + 1 < 8:
                    nxt_pieces = phase2_pieces(h + 1)
                emit_S(n)
                pos_in_head = n - h * per_head
                if nxt_pieces and pos_in_head % 4 == 3:
                    nxt_pieces.pop(0)()
                if pos_in_head == per_head - 1:
                    while nxt_pieces:
                        nxt_pieces.pop(0)()
            if n - LA >= 0:
                emit_PV(n - LA)

    h0_pieces = phase2_pieces(0)

    def after_chunk(c):
        if c == 0:
            return
        n = 2 if (c - 1) % 2 == 0 else 3
        for _ in range(n):
            h0_pieces.pop(0)()

    phase1a(after_chunk)
    while h0_pieces:
        h0_pieces.pop(0)()
    assert not h0_pieces
    tap("cqT", cqT)
    tap("ckvT", ckvT)
    tap("krope", krope)
    if STOP_AFTER and STOP_AFTER.startswith("p1a"):
        return fin()
    sT_b = [[Buf(f"sT{h}_{c}") for c in range(NC4)] for h in range(8)]
    inherit([b for hb in sT_b for b in hb], [ktmp_b] + sq_b + rtmp_b)
    attention()
    norm_items = []
    for m in range(4):
        for c in range(NC4):
            def f(m=m, c=c):
                k = (m * NC4 + c) % 2
                sl = slice(c * 512, (c + 1) * 512)
                bs = [sT_b[2 * m][c], sT_b[2 * m + 1][c]]
                P.op("act", ACTV(rn[k], sT[:, m, sl], AF.Ln), reads=bs, writes=[rn_b[k]])
                P.op("act", ACTV(rn[k], rn[k], AF.Exp, scale=-1.0), reads=[rn_b[k]], writes=[rn_b[k]])
                P.op("dve", TT(oT[:, m, sl], oT[:, m, sl], rn[k], ALU.mult), reads=[rn_b[k], oT_b[2 * m][c], oT_b[2 * m + 1][c]],
                     writes=[oT_b[2 * m][c], oT_b[2 * m + 1][c]])
            norm_items.append(f)
    if STOP_AFTER == "att":
        while norm_items:
            norm_items.pop(0)()
    tap("oTraw", oT)
    if STOP_AFTER == "att":
        return fin()


    p3b_items = []

    def phase3b():
        it = 0
        for blk in range(2):
            holder = {}
            for mm in range(2):
                m = blk * 2 + mm
                for c in range(NC4):
                    def f(blk=blk, mm=mm, m=m, c=c, it=it, holder=holder):
                        if "w" not in holder:
                            holder["w"] = stream(win_d, C_ZM + blk * 256, 256)
                        w3, wb = holder["w"]
                        pb = 3 + (it % 2)
                        k = it % 3
                        P.op("pe", proj_fns(lambda kc: w3[:, kc, mm * 128:(mm + 1) * 128], 128, c, pb), reads=[wb] + hTc(c), writes=[PB[pb]])
                        P.op("act", ACTV(zu[k], bank(pb), AF.Silu), reads=[PB[pb]], writes=[zu_b[k]])
                        sl = oT[:, m, c * 512:(c + 1) * 512]
                        P.op("dve", TT(sl, sl, zu[k], ALU.mult), reads=[zu_b[k], oT_b[2 * m][c], oT_b[2 * m + 1][c]], writes=[oT_b[2 * m][c], oT_b[2 * m + 1][c]])
                    p3b_items.append(f)
                    it += 1

    phase3b()
    if STOP_AFTER == "p3b":
        while p3b_items:
            p3b_items.pop(0)()
    tap("oT", oT)
    if STOP_AFTER == "p3b":
        return fin()

    PRE = {}
    PRE["qk0"] = [stream(win_d, C_QG, 256), stream(win_d, C_KG, 256)]
    A.release(m_glob)
    ARENA_LOG.append(("end-att", A.peak))
    ib_gla0 = len(ALLBUFS)
    zgT = r3(A.tile(BF16, 8 * S), 8)
    zg_b = [[Buf(f"zg{m}_{t}") for t in range(NT)] for m in range(8)]
    m_gla = A.mark()
    ib_glap = len(ALLBUFS)
    _q1 = r3(A.tile(BF16, 2 * S), 2)
    _k1 = r3(A.tile(BF16, 2 * S), 2)
    assert scr_off >= m_gla and scr_off + 16384 <= A.top, (scr_off, m_gla, A.top)
    _q0 = r3(A.tile(BF16, 2 * S), 2)
    _k0 = r3(A.tile(BF16, 2 * S), 2)
    qgTs = [_q0, _q1]
    kgTs = [_k0, _k1]
    vg = r3(A.tile(BF16, NT * 512), NT)
    qg_bs = [[[Buf(f"qg{p}{l}_{c}") for c in range(NC4)] for l in range(2)] for p in range(2)]
    kg_bs = [[[Buf(f"kg{p}{l}_{c}") for c in range(NC4)] for l in range(2)] for p in range(2)]
    vg_b = [Buf(f"vg{t}") for t in range(NT)]
    walr = r3(A.tile(BF16, 8 * 16), 8)
    walr_b = Buf("walr")
    alrc = [A.tile(F32, 512) for _ in range(NC4)]
    alrc_b = [Buf(f"alrc{i}") for i in range(NC4)]
    wga = A.tile(F32, 512)
    wga_b = Buf("wga")
    dec = r3(A.tile(F32, 2 * NT), 2)
    dec_b = [[Buf(f"dec{l}_{c}") for c in range(NC4)] for l in range(2)]
    Sst = r3(A.tile(F32, 2 * 256), 2)
    Sst_b = [Buf(f"S{l}") for l in range(2)]
    Sbf = [r3(A.tile(BF16, 2 * 256), 2) for _ in range(2)]
    Sbf_b = [Buf("Sbf0"), Buf("Sbf1")]
    ss4 = [A.tile(F32, 8) for _ in range(2)]
    ss4_b = [Buf("ss4a"), Buf("ss4b")]
    AmT = [r3(A.tile(BF16, 256), 2) for _ in range(2)]
    AmT_b = [Buf("AmT0"), Buf("AmT1")]
    kstT = [r3(A.tile(BF16, 256), 2) for _ in range(2)]
    kstT_b = [Buf("kstT0"), Buf("kstT1")]
    kstt = [r3(A.tile(BF16, 256), 2) for _ in range(2)]
    kstt_b = [Buf("kstt0"), Buf("kstt1")]
    ogn = [A.tile(BF16, 512) for _ in range(2)]
    ogn_b = [Buf("ogn0"), Buf("ogn1")]
    junk2 = [A.tile(BF16, 256) for _ in range(2)]
    junk2_b = [Buf("junk2a"), Buf("junk2b")]
    rmask = A.tile(F32, 512)
    _gA = [A.tile(F32, 512) for _ in range(2)]
    _gB = [A.tile(F32, 512) for _ in range(2)]
    _gC = [A.tile(F32, 512) for _ in range(2)]
    gA, gB, gC = [_gA, _gA], [_gB, _gB], [_gC, _gC]
    _gAb = [Buf(f"gA{l}") for l in range(2)]
    _gBb = [Buf(f"gB{l}") for l in range(2)]
    _gCb = [Buf(f"gC{l}") for l in range(2)]
    gA_b, gB_b, gC_b = [_gAb, _gAb], [_gBb, _gBb], [_gCb, _gCb]
    zt = [A.tile(F32, 512) for _ in range(3)]
    zt_b = [Buf("zt0b"), Buf("zt1b"), Buf("zt2b")]

    ib_gla1 = len(ALLBUFS)
    inherit(ALLBUFS[ib_gla0:ib_gla1], ALLBUFS[ib_glob:ib_gla0])
    rmask_b = Buf("rmask")
    inherit([rmask_b], ALLBUFS[ib_glob:ib_gla0])
    P.op("pool", MS(rmask, 1.0), writes=[rmask_b])
    P.op("pool", MS(rmask.rearrange("p (t b) -> p t b", b=128)[:, :, 0:1], 0.0), writes=[rmask_b])
    for i in range(NC4):
        P.op("pool", MS(alrc[i][0:32, :], 1.0), writes=[alrc_b[i]])
    P.dma("sp", wga[0:16, :], wgg_d, "d_wga", writes=[wga_b])
    P.dma("sp", wga[16:17, :], bgg_d, "d_wga", writes=[wga_b])
    load_block(k3(win_d)[:, :, C_ALR:C_ALR + 16], 8, 16, walr, walr_b)

    def merge_emit(bulk, chain):
        nb, ncn = len(bulk), len(chain)
        bi = ci = 0
        while bi < nb or ci < ncn:
            if bi < nb:
                bulk[bi]()
                bi += 1
            tgt = ncn if bi >= nb else (bi * ncn + nb - 1) // nb
            while ci < tgt:
                chain[ci]()
                ci += 1

    def gla_pass(p):
        it = [0]
        qgT, kgT, qg_b, kg_b = qgTs[p], kgTs[p], qg_bs[p], kg_bs[p]
        if p == 0:
            for wi_, (c0, dstT, dst_b) in enumerate(((C_QG, qgT, qg_b), (C_KG, kgT, kg_b))):
                w3, wb = PRE["qk0"][wi_]
                for l in range(2):
                    for c in range(NC4):
                        pb = it[0] % 3
                        it[0] += 1
                        if norm_items:
                            norm_items.pop(0)()
                        elif p3b_items:
                            p3b_items.pop(0)()
                        P.op("pe", proj_fns(lambda kc: w3[:, kc, l * 128:(l + 1) * 128], 128, c, pb), reads=[wb] + hTc(c), writes=[PB[pb]])
                        evac_copy(dstT[:, l, c * 512:(c + 1) * 512], bank(pb), reads=[PB[pb]], writes=[dst_b[l][c]])
        if p == 0:
            while norm_items:
                norm_items.pop(0)()
            while p3b_items:
                p3b_items.pop(0)()
            inherit([b for l_ in qg_bs[1] + kg_bs[1] for b in l_], [b for hb in sT_b for b in hb])
        bulk = []

        def vg_item(blk, t, holder):
            def f():
                if t == 0:
                    holder["w"] = PRE["vg1"][blk] if (p == 1 and "vg1" in PRE) else stream(win_d, C_VG + 512 * p + blk * 256, 256)
                w3, wb = holder["w"]
                pb = it[0] % 3
                it[0] += 1
                P.op("pe", [MM(bank(pb)[:, 0:256], hT[:, kc, t * 128:(t + 1) * 128], w3[:, kc, :], start=(kc == 0), stop=(kc == 7)) for kc in range(8)],
                     reads=[wb, hT_b[t]], writes=[PB[pb]])
                evac_copy(vg[:, t, blk * 256:(blk + 1) * 256], bank(pb)[:, 0:256], reads=[PB[pb]], writes=[vg_b[t]])
            return f

        zgw = {}
        if p == 1:
            for blk_ in range(2):
                zgw[blk_] = stream(win_d, C_ZG + 512 * p + blk_ * 256, 256)

        def zg_fill(i):
            c, idx = i // 4, i % 4
            blk, mm = idx // 2, idx % 2
            w3, wb = zgw[blk]
            m = 4 * p + blk * 2 + mm
            P.op("pe", proj_fns(lambda kc: w3[:, kc, mm * 128:(mm + 1) * 128], 128, c, 7), reads=[wb] + hTc(c), writes=[PB[7]])
            P.op("act", ACTV(zgT[:, m, c * 512:(c + 1) * 512], bank(7), AF.Silu), reads=[PB[7]], writes=zg_b[m][4 * c:4 * c + 4])

        for blk in range(2):
            holder = {}
            for t in range(NT):
                bulk.append(vg_item(blk, t, holder))
        chain = []

        def add(fn):
            chain.append(fn)

        for c in range(NC4):
            sl = slice(c * 512, (c + 1) * 512)
            i2 = c % 2
            pbA = 3 + (c % 2)
            if p == 0:
                add(lambda c=c, pbA=pbA: P.op("pe", proj_fns(lambda kc: walr[:, kc, :], 16, c, pbA), reads=[walr_b] + hTc(c), writes=[PB[pbA]]))
                add(lambda c=c, pbA=pbA: P.op("dve", CP(alrc[c][0:16, :], bank(pbA)[0:16, :]), reads=[PB[pbA]], writes=[alrc_b[c]]))
            for l in range(2):
                f = 2 * p + l
                pg = 5 + l
                add(lambda c=c, f=f, pg=pg: P.op("pe", MM(bank(pg), wga[0:17, f * 128:(f + 1) * 128], alrc[c][0:17, :]), reads=[wga_b, alrc_b[c]], writes=[PB[pg]]))
            for l in range(2):
                pg = 5 + l
                add(lambda i2=i2, l=l, pg=pg: P.op("act", ACTV(gA[i2][l], bank(pg), AF.Exp, scale=-1.0), reads=[PB[pg]], writes=[gA_b[i2][l]]))
            for l in range(2):
                add(lambda i2=i2, l=l: P.op("act", ACTV(gB[i2][l], gA[i2][l], AF.Ln, bias=1.0, scale=1.0), reads=[gA_b[i2][l]], writes=[gB_b[i2][l]]))
            for l in range(2):
                add(lambda i2=i2, l=l: P.op("dve", lambda e: e.tensor_tensor_scan(out=gA[i2][l], data0=rmask, data1=gB[i2][l], initial=0.0, op0=ALU.mult, op1=ALU.add),
                                             reads=[gB_b[i2][l], rmask_b], writes=[gA_b[i2][l]]))
            for l in range(2):
                add(lambda i2=i2, l=l, c=c: P.op("act", ACTV(dec[:, l, 4 * c:4 * c + 4], gA[i2][l].rearrange("p (t b) -> p t b", b=128)[:, :, 127], AF.Exp, scale=-1.0 / 16),
                                                  reads=[gA_b[i2][l]], writes=[dec_b[l][c]]))
                add(lambda i2=i2, l=l: P.op("act", ACTV(gB[i2][l], gA[i2][l], AF.Exp, scale=-1.0 / 16, bias=lnq_t), reads=[gA_b[i2][l], cst], writes=[gB_b[i2][l]]))
                add(lambda i2=i2, l=l: P.op("act", ACTV(gC[i2][l], gA[i2][l], AF.Exp, scale=1.0 / 16), reads=[gA_b[i2][l]], writes=[gC_b[i2][l]]))
            for l in range(2):
                add(lambda i2=i2, l=l, c=c, sl=sl: P.op("dve", TT(qgT[:, l, sl], qgT[:, l, sl], gB[i2][l], ALU.mult), reads=[gB_b[i2][l], qg_b[l][c]], writes=[qg_b[l][c]]))
                add(lambda i2=i2, l=l, c=c, sl=sl: P.op("dve", TT(kgT[:, l, sl], kgT[:, l, sl], gC[i2][l], ALU.mult), reads=[gC_b[i2][l], kg_b[l][c]], writes=[kg_b[l][c]]))
        merge_emit(bulk, chain)

        PA_, PKT_, PO_, PS_, PT_ = (0, 0), 2, (3, 4), 5, (6, 6)
        fill_w = {}

        def filler(i):
            which, l, c = i // 8, (i % 8) // 4, i % 4
            if which not in fill_w:
                fill_w[which] = stream(win_d, (C_QG, C_KG)[which] + 256, 256)
            w3, wb = fill_w[which]
            dstT, dst_b = ((qgTs[1], qg_bs[1]), (kgTs[1], kg_bs[1]))[which]
            P.op("pe", proj_fns(lambda kc: w3[:, kc, l * 128:(l + 1) * 128], 128, c, 1), reads=[wb] + hTc(c), writes=[PB[1]])
            P.op("act", ACTV(dstT[:, l, c * 512:(c + 1) * 512], bank(1), AF.Copy), reads=[PB[1]], writes=[dst_b[l][c]])


        def st0(t):
            tb = slice(t * 128, (t + 1) * 128)
            c = t // 4
            par = t % 2
            pa = PA_[par]
            P.op("pe", [MM(bank(pa)[:, l * 128:(l + 1) * 128], kgT[:, l, tb], qgT[:, l, tb]) for l in range(2)],
                 reads=[kg_b[0][c], kg_b[1][c], qg_b[0][c], qg_b[1][c]], writes=[PB[pa]])
            if t < NT - 1:
                P.op("dve", TT(kstT[par], kgT[:, :, tb], dec[:, :, t:t + 1].to_broadcast([128, 2, 128]), ALU.mult),
                     reads=[kg_b[0][c], kg_b[1][c], dec_b[0][c], dec_b[1][c]], writes=[kstT_b[par]])
                P.op("pe", [TR(bank_b(PKT_)[:, l * 128:(l + 1) * 128], kstT[par][:, l, :], ident_b) for l in range(2)], reads=[kstT_b[par], cst], writes=[PB[PKT_]])
            P.op("dve", TT(AmT[par], r3(bank(pa)[:, 0:256], 2), mask01.unsqueeze(1).to_broadcast([128, 2, 128]), ALU.mult), reads=[PB[pa], cst], writes=[AmT_b[par]])
            if t < NT - 1:
                P.op("act", ACTV(kstt[par], r3(bank_b(PKT_)[:, 0:256], 2), AF.Copy), reads=[PB[PKT_]], writes=[kstt_b[par]])

        def st1(t):
            tb = slice(t * 128, (t + 1) * 128)
            c = t // 4
            par = t % 2
            po = PO_[par]
            fns = []
            for l in range(2):
                o_ap = bank(po)[:, l * 256:(l + 1) * 256]
                fns.append(MM(o_ap, AmT[par][:, l, :], vg[:, t, l * 256:(l + 1) * 256], start=True, stop=(t == 0)))
                if t > 0:
                    fns.append(MM(o_ap, qgT[:, l, tb], Sbf[par][:, l, :], start=False, stop=True))
            P.op("pe", fns, reads=[AmT_b[par], vg_b[t], Sbf_b[par], qg_b[0][c], qg_b[1][c]], writes=[PB[po]])
            if t < NT - 1:
                P.op("pe", [MM(bank(PS_)[:, l * 256:(l + 1) * 256], kstt[par][:, l, :], vg[:, t, l * 256:(l + 1) * 256]) for l in range(2)],
                     reads=[kstt_b[par], vg_b[t]], writes=[PB[PS_]])
                for l in range(2):
                    s_ap = bank(PS_)[:, l * 256:(l + 1) * 256]
                    if t == 0:
                        P.op("dve", CP(Sst[:, l, :], s_ap), reads=[PB[PS_]], writes=[Sst_b[l]])
                    else:
                        P.op("dve", STT(Sst[:, l, :], Sst[:, l, :], dec[:, l, t:t + 1], s_ap, ALU.mult, ALU.add),
                             reads=[PB[PS_], dec_b[l][c], Sst_b[l]], writes=[Sst_b[l]])
                P.op("act", ACTV(Sbf[1 - par].rearrange("p l v -> p (l v)"), Sst.rearrange("p l v -> p (l v)"), AF.Copy), reads=Sst_b, writes=[Sbf_b[1 - par]])
            for l in range(2):
                o_ap = bank(po)[:, l * 256:(l + 1) * 256]
                P.op("act", ACTV(junk2[l], o_ap, AF.Square, accum_out=ss4[par][:, l:l + 1]), reads=[PB[po]], writes=[junk2_b[l], ss4_b[par]])
            P.op("pool", TS(ss4[par][:, 2:4], ss4[par][:, 0:2], 1.0 / 256, ALU.mult, EPS, ALU.add), reads=[ss4_b[par]], writes=[ss4_b[par]])
            P.op("pool", TT(ss4[par][:, 4:6], ss4[par][:, 2:4], mhalf[:, 0:2], ALU.pow), reads=[ss4_b[par], cst], writes=[ss4_b[par]])

        def st2a(t):
            par = t % 2
            po = PO_[par]
            for l in range(2):
                P.op("dve", TS(ogn[par][:, l * 256:(l + 1) * 256], bank(po)[:, l * 256:(l + 1) * 256], ss4[par][:, 4 + l:5 + l], ALU.mult),
                     reads=[PB[po], ss4_b[par]], writes=[ogn_b[par]])

        def st2b(t):
            tb = slice(t * 128, (t + 1) * 128)
            par = t % 2
            ptb = PT_[par]
            P.op("pe", [TR(bank_b(ptb)[:, kc * 128:(kc + 1) * 128], ogn[par][:, kc * 128:(kc + 1) * 128], ident_b) for kc in range(4)],
                 reads=[ogn_b[par], cst], writes=[PB[ptb]])
            zsl = zgT[:, 4 * p:4 * p + 4, tb]
            P.op("dve", TT(zsl, r3(bank_b(ptb)[:, 0:512], 4), zsl, ALU.mult), reads=[PB[ptb]] + [zg_b[4 * p + m][t] for m in range(4)],
                 writes=[zg_b[4 * p + m][t] for m in range(4)])

        if p == 0:
            for blk in range(2):
                zgw[blk] = stream(win_d, C_ZG + 512 * p + blk * 256, 256)
        if p == 0:
            for w_ in range(2):
                fill_w[w_] = stream(win_d, (C_QG, C_KG)[w_] + 256, 256)
            PRE["vg1"] = [stream(win_d, C_VG + 512 + blk * 256, 256) for blk in range(2)]
        else:
            PRE["F0"] = (stream(wpm_d, 0, 256, kcn=4), stream(win_d, C_GM, 256))
        for i in range(NT + 3):
            if i < NT:
                zg_fill(i)
            if p == 0 and 1 <= i <= NT:
                filler(i - 1)
            if i < NT:
                st0(i)
            if 0 <= i - 1 < NT:
                st1(i - 1)
            if 0 <= i - 2 < NT:
                st2a(i - 2)
            if 0 <= i - 3 < NT:
                st2b(i - 3)

    gla_pass(0)
    if STOP_AFTER == "gla0":
        tap("zgT", zgT)
        return fin()
    gla_pass(1)
    tap("zgT", zgT)
    if STOP_AFTER == "gla":
        return fin()

    A.release(m_gla)
    ARENA_LOG.append(("end-gla", A.peak, A.top))
    ib_F0 = len(ALLBUFS)
    mT = r3(A.tile(BF16, 8 * S), 8)
    mT_b = [[Buf(f"m{m}_{c}") for c in range(NC4)] for m in range(8)]
    woutb = r3(A.tile(BF16, 8 * D), 8)
    woutb_b = Buf("wout")
    gfin_bc = A.tile(F32, D)
    ggl_t = A.tile(F32, 2)
    yt = [A.tile(F32, 512) for _ in range(2)]
    yt_b = [Buf("yt0"), Buf("yt1")]
    yu = [A.tile(BF16, 512) for _ in range(2)]
    yu_b = [Buf("yu0"), Buf("yu1")]
    xs = [A.tile(F32, D) for _ in range(NXB)]
    xs_b = [Buf(f"xsF{i}") for i in range(NXB)]
    oout = [A.tile(F32, D) for _ in range(2)]
    oout_b = [Buf("oout0"), Buf("oout1")]
    fst = A.tile(F32, 3 * NT)
    fst_b = [Buf(f"fst{t}") for t in range(NT)]

    gfin_b = Buf("gfin")
    inherit(ALLBUFS[ib_F0:], ALLBUFS[ib_glap:ib_F0])

    ARENA_LOG.append(("F", A.top))

    def phaseF():
        P.dma("sp", gfin_bc, gfin_d.to_broadcast([128, D]), "d_m5", writes=[gfin_b])
        P.dma("sp", ggl_t, ggla_d.rearrange("o (k p) -> p (o k)", p=128), "d_m6", writes=[gfin_b], allow_slow_non_contiguous=True)
        it = 0
        for blk in range(4):
            if blk == 0 and "F0" in PRE:
                (wm3, wm_b), (wgm3, wgm_b) = PRE["F0"]
            else:
                wm3, wm_b = stream(wpm_d, blk * 256, 256, kcn=4)
                wgm3, wgm_b = stream(win_d, C_GM + blk * 256, 256)
            def gla_weights(blk=blk):
                jg = wctr["w"] % NWB
                wctr["w"] += 1
                wg3 = r3(wbuf[jg][:, 0:8 * 256], 8)
                P.dma("pool", wg3, k3(wpg_d)[:, :, blk * 256:(blk + 1) * 256], "d_" + wbuf_b[jg].name, writes=[wbuf_b[jg]])
                for par in range(2):
                    dst4 = wg3.rearrange("p (k two) n -> p k two n", two=2)[:, :, par, :]
                    P.op("dve", TS(dst4, dst4, ggl_t[:, par:par + 1], ALU.mult), reads=[gfin_b], writes=[wbuf_b[jg]])
                wgg3, wgg_b = stream(win_d, C_GG + blk * 256, 256)
                return jg, wg3, wgg3, wgg_b

            if blk > 0:
                jg, wg3, wgg3, wgg_b = gla_weights()
            for mm in range(2):
                m = blk * 2 + mm
                for c in range(NC4):
                    sl = slice(c * 512, (c + 1) * 512)
                    py, pg = (it % 2) * 2, (it % 2) * 2 + 1
                    k = it % 2
                    it += 1
                    P.op("pe", [MM(bank(py), wm3[:, kc, mm * 128:(mm + 1) * 128], oT[:, kc, sl], start=(kc == 0), stop=(kc == 3)) for kc in range(4)],
                         reads=[wm_b] + [oT_b[h][c] for h in range(8)], writes=[PB[py]])
                    P.op("pe", proj_fns(lambda kc: wgm3[:, kc, mm * 128:(mm + 1) * 128], 128, c, pg), reads=[wgm_b] + hTc(c), writes=[PB[pg]])
                    P.op("act", ACTV(yt[k], bank(pg), AF.Tanh, scale=0.5), reads=[PB[pg]], writes=[yt_b[k]])
                    P.op("dve", STT(mT[:, m, sl], yt[k], 1.0, bank(py), ALU.add, ALU.mult), reads=[PB[py], yt_b[k]], writes=[mT_b[m][c]])
            if blk == 0:
                jg, wg3, wgg3, wgg_b = gla_weights()
            for mm in range(2):
                m = blk * 2 + mm
                for c in range(NC4):
                    sl = slice(c * 512, (c + 1) * 512)
                    py, pg = (it % 2) * 2, (it % 2) * 2 + 1
                    k = it % 2
                    it += 1
                    P.op("pe", [MM(bank(py), wg3[:, kc, mm * 128:(mm + 1) * 128], zgT[:, kc, sl], start=(kc == 0), stop=(kc == 7)) for kc in range(8)],
                         reads=[wbuf_b[jg]] + [zg_b[kc][tt] for kc in range(8) for tt in range(4 * c, 4 * c + 4)], writes=[PB[py]])
                    P.op("pe", proj_fns(lambda kc: wgg3[:, kc, mm * 128:(mm + 1) * 128], 128, c, pg), reads=[wgg_b] + hTc(c), writes=[PB[pg]])
                    P.op("act", ACTV(yt[k], bank(pg), AF.Tanh, scale=0.5), reads=[PB[pg]], writes=[yt_b[k]])
                    P.op("dve", STT(yu[k], yt[k], 1.0, bank(py), ALU.add, ALU.mult), reads=[PB[py], yt_b[k]], writes=[yu_b[k]])
                    P.op("dve", TT(mT[:, m, sl], mT[:, m, sl], yu[k], ALU.add), reads=[yu_b[k], mT_b[m][c]], writes=[mT_b[m][c]])
        for blk in range(4):
            load_block(k3(wout_d)[:, :, blk * 256:(blk + 1) * 256], 8, 256, woutb[:, :, blk * 256:(blk + 1) * 256], woutb_b)
        def f2_A(t):
            i = t % NXB
            P.dma("sp", xs[i], x_d[t * 128:(t + 1) * 128, :], f"d_xs{i}", writes=[xs_b[i]])
            for hf in range(2):
                pb = 4 + hf + 2 * (t % 2)
                P.op("pe", [MM(bank(pb), mT[:, kc, t * 128:(t + 1) * 128], woutb[:, kc, hf * 512:(hf + 1) * 512], start=(kc == 0), stop=(kc == 7)) for kc in range(8)],
                     reads=[woutb_b] + [mT_b[kc][t // 4] for kc in range(8)], writes=[PB[pb]])
                xh = xs[i][:, hf * 512:(hf + 1) * 512]
                P.op("dve", STT(xh, bank(pb), 0.5, xh, ALU.mult, ALU.add), reads=[PB[pb], xs_b[i]], writes=[xs_b[i]])
            ss = fst[:, 3 * t:3 * t + 1]
            ln = fst[:, 3 * t + 1:3 * t + 2]
            rs = fst[:, 3 * t + 2:3 * t + 3]
            P.op("act", ACTV(junk, xs[i], AF.Square, accum_out=ss), reads=[xs_b[i]], writes=[junk_b, fst_b[t]])
            P.op("act", ACTV(ln, ss, AF.Ln, scale=1.0 / D, bias=eps_t), reads=[fst_b[t], cst], writes=[fst_b[t]])
            P.op("act", ACTV(rs, ln, AF.Exp, scale=-0.5), reads=[fst_b[t]], writes=[fst_b[t]])

        def f2_B(t):
            i = t % NXB
            k = t % 2
            rs = fst[:, 3 * t + 2:3 * t + 3]
            P.op("dve", STT(oout[k], xs[i], rs, gfin_bc, ALU.mult, ALU.mult), reads=[xs_b[i], fst_b[t], gfin_b], writes=[oout_b[k]])
            P.dma("pool", out_d[t * 128:(t + 1) * 128, :], oout[k], f"d_out{k}", reads=[oout_b[k]])

        for t in range(NT + 1):
            if t < NT:
                f2_A(t)
            if t - 1 >= 0:
                f2_B(t - 1)

    phaseF()
    tap("mT", mT)

    return fin()


_NC_CACHE = {}


def kernel(x, positions, g_in, w_in, g_q, w_uq, g_kv, w_ukv, w_gla_gate, b_gla_gate,
           g_gla, w_proj_mla, w_proj_gla, w_out, g_final):
    f = lambda a: np.ascontiguousarray(np.asarray(a, dtype=np.float32))
    x = f(x)
    positions = np.ascontiguousarray(np.asarray(positions, dtype=np.int32))
    shared = {
        "g_in": f(g_in).reshape(1, D), "w_in": f(w_in).reshape(D, DIN),
        "g_q": f(g_q).reshape(1, 384), "w_uq": f(w_uq).reshape(384, 768),
        "g_kv": f(g_kv).reshape(1, 256), "w_ukv": f(w_ukv).reshape(256, 1024),
        "w_gla_gate": f(w_gla_gate).reshape(16, 512), "b_gla_gate": f(b_gla_gate).reshape(1, 512),
        "g_gla": f(g_gla).reshape(1, 256), "w_proj_mla": f(w_proj_mla).reshape(512, D),
        "w_proj_gla": f(w_proj_gla).reshape(D, D), "w_out": f(w_out).reshape(D, D),
        "g_final": f(g_final).reshape(1, D),
    }
    if "nc" not in _NC_CACHE:
        _NC_CACHE["nc"] = build_program()
    nc = _NC_CACHE["nc"]
    in_maps = []
    for b in range(8):
        m = dict(shared)
        m["x"] = x[b]
        m["pos"] = positions[b].reshape(1, S)
        in_maps.append(m)
    res = run_bass_kernel_spmd(nc, in_maps, core_ids=list(range(8)))
    _NC_CACHE["last"] = res
    return np.stack([np.asarray(r["out"], dtype=np.float32) for r in res.results], axis=0)
```

```python
import math
import numpy as np
import concourse.bass as bass
import concourse.mybir as mybir
from concourse.bass_utils import run_bass_kernel_spmd

F32 = mybir.dt.float32
BF16 = mybir.dt.bfloat16
I32 = mybir.dt.int32
AF = mybir.ActivationFunctionType
ALU = mybir.AluOpType

ENGS = ("pe", "act", "dve", "pool", "sp")
DSZ = {F32: 4, BF16: 2, I32: 4}

S = 2048
D = 1024
NT = 16
NC4 = 4
EPS = 1e-6
DIN = 6320
C_CQ, C_CKV, C_KR, C_ZM, C_QG, C_KG, C_VG, C_ALR, C_ZG, C_GM, C_GG = (
    0, 384, 640, 672, 1184, 1696, 2208, 3232, 3248, 4272, 5296)

DEBUG_TAPS = None
STOP_AFTER = None


ALLBUFS = []


def inherit(new_bufs, old_bufs):
    toks = set()
    for o in old_bufs:
        if o.w is not None:
            toks.add(o.w)
        toks.update(o.r)
    for n in new_bufs:
        n.r = list(set(n.r) | toks)


class Buf:
    __slots__ = ("name", "w", "r", "excl")

    def __init__(self, name, excl=False):
        self.name = name
        self.w = None
        self.r = []
        self.excl = excl
        ALLBUFS.append(self)


class Prog:
    def __init__(self, nc):
        self.nc = nc
        self.q = {e: [] for e in ENGS}
        self.cnt = {}
        self.waited = {e: {} for e in ENGS}
        self.semh = {}
        self.gfence = []
        for e in ENGS:
            self._sem("E_" + e)

    def barrier(self):
        self.gfence = self.fence()

    def _sem(self, name):
        if name not in self.semh:
            self.semh[name] = self.nc.alloc_semaphore(name)
            self.cnt[name] = 0
        return self.semh[name]

    def _waits(self, eng, deps):
        w = self.waited[eng]
        best = {}
        for d in deps:
            if d is None:
                continue
            sn, v = d
            if eng == "pe" and sn == "E_pe":
                continue
            if w.get(sn, 0) >= v:
                continue
            if best.get(sn, 0) < v:
                best[sn] = v
        for sn, v in best.items():
            w[sn] = v
        return list(best.items())

    def _deps(self, reads, writes, extra, eng=None):
        deps = []
        for b in reads:
            if b.w is not None:
                deps.append(b.w)
            if b.excl:
                own = "E_" + str(eng)
                deps.extend(t for t in b.r if t[0] != own)
        for b in writes:
            if b.w is not None:
                deps.append(b.w)
            deps.extend(b.r)
        deps.extend(extra)
        deps.extend(self.gfence)
        return deps

    def op(self, eng, fns, reads=(), writes=(), extra=()):
        if callable(fns):
            fns = [fns]
        waits = self._waits(eng, self._deps(reads, writes, extra, eng))
        sn = "E_" + eng
        self.cnt[sn] += 1
        tok = (sn, self.cnt[sn])
        self.q[eng].append((waits, fns, (sn, 1)))
        for b in reads:
            b.r.append(tok)
        for b in writes:
            b.w = tok
            b.r = []
        return tok

    def dma(self, queue, out, in_, sem, reads=(), writes=(), extra=(), **kw):
        self._sem(sem)
        waits = self._waits(queue, self._deps(reads, writes, extra, queue))
        self.cnt[sem] += 16
        tok = (sem, self.cnt[sem])
        self.q[queue].append((waits, [lambda e: e.dma_start(out=out, in_=in_, **kw)], (sem, 16)))
        for b in reads:
            b.r.append(tok)
        for b in writes:
            b.w = tok
            b.r = []
        return tok

    def fence(self):
        return [(sn, v) for sn, v in self.cnt.items() if v > 0]

    def wait_all(self, eng):
        waits = self._waits(eng, self.fence())
        self.q[eng].append((waits, [], None))

    def emit(self):
        nc = self.nc
        hand = {"pe": "tensor", "act": "scalar", "dve": "vector", "pool": "gpsimd", "sp": "sync"}
        with nc.Block() as block:
            for e in ENGS:
                items = self.q[e]

                def body(engine, items=items):
                    for waits, fns, inc in items:
                        for sn, v in waits:
                            engine.wait_ge(self.semh[sn], v)
                        for i, f in enumerate(fns):
                            ins = f(engine)
                            if i == len(fns) - 1 and inc is not None:
                                ins.then_inc(self.semh[inc[0]], inc[1])

                getattr(block, hand[e])(body)


class Arena:
    def __init__(self, nc, nbytes):
        self.t = nc.alloc_sbuf_tensor("arena", [128, nbytes // 4], F32)
        self.nbytes = nbytes
        self.top = 0
        self.views = {F32: self.t}
        self.peak = 0

    def tile(self, dtype, n, align=64):
        sz = DSZ[dtype]
        off = (self.top + align - 1) // align * align
        assert off + n * sz <= self.nbytes, ("SBUF arena overflow", off, n * sz, self.nbytes)
        self.top = off + n * sz
        self.peak = max(self.peak, self.top)
        if dtype not in self.views:
            self.views[dtype] = self.t.bitcast(dtype)
        return self.views[dtype][:, off // sz: off // sz + n]

    def mark(self):
        return self.top

    def release(self, m):
        self.top = m


def MM(out, lhsT, rhs, start=True, stop=True):
    return lambda e: e.matmul(out, lhsT=lhsT, rhs=rhs, start=start, stop=stop)


def TR(out, in_, ident):
    return lambda e: e.transpose(out=out, in_=in_, identity=ident)


def ACTV(out, in_, func, **kw):
    return lambda e: e.activation(out=out, in_=in_, func=func, **kw)


def TT(out, in0, in1, op):
    return lambda e: e.tensor_tensor(out=out, in0=in0, in1=in1, op=op)


def TS(out, in0, s1, op0, s2=None, op1=None):
    if op1 is None:
        return lambda e: e.tensor_scalar(out=out, in0=in0, scalar1=s1, scalar2=None, op0=op0)
    return lambda e: e.tensor_scalar(out=out, in0=in0, scalar1=s1, scalar2=s2, op0=op0, op1=op1)


def STT(out, in0, scalar, in1, op0, op1):
    return lambda e: e.scalar_tensor_tensor(out=out, in0=in0, scalar=scalar, in1=in1, op0=op0, op1=op1)


def CP(out, in_):
    return lambda e: e.tensor_copy(out=out, in_=in_)


def MS(ap, v):
    return lambda e: e.memset(ap, v)


def r3(ap, a):
    return ap.rearrange("p (a b) -> p a b", a=a)


def build_program():
    nc = bass.Bass("TRN2", target_bir_lowering=False)

    def din(name, shape, dt=F32):
        return nc.dram_tensor(name, shape, dt, kind="ExternalInput").ap()

    x_d = din("x", [S, D])
    pos_d = din("pos", [1, S], I32)
    gin_d = din("g_in", [1, D])
    win_d = din("w_in", [D, DIN])
    gq_d = din("g_q", [1, 384])
    wuq_d = din("w_uq", [384, 768])
    gkv_d = din("g_kv", [1, 256])
    wukv_d = din("w_ukv", [256, 1024])
    wgg_d = din("w_gla_gate", [16, 512])
    bgg_d = din("b_gla_gate", [1, 512])
    ggla_d = din("g_gla", [1, 256])
    wpm_d = din("w_proj_mla", [512, D])
    wpg_d = din("w_proj_gla", [D, D])
    wout_d = din("w_out", [D, D])
    gfin_d = din("g_final", [1, D])
    out_d = nc.dram_tensor("out", [S, D], F32, kind="ExternalOutput").ap()

    del ALLBUFS[:]
    P = Prog(nc)
    A = Arena(nc, 212700)
    psum = nc.alloc_psum_tensor("psum", [128, 4096], F32)
    psum_b = psum.bitcast(BF16)
    PB = [Buf(f"ps{i}", excl=True) for i in range(8)]
    tapped = {}
    ARENA_LOG = []

    def fin():
        import os
        if os.environ.get("MK_VERBOSE"):
            print("arena marks", ARENA_LOG, "peak", A.peak)
        for nme, ap in tapped.items():
            dd = nc.dram_tensor("tap_" + nme, list(ap.shape), ap.dtype, kind="ExternalOutput").ap()
            P.wait_all("sp")
            P.dma("sp", dd, ap, "d_tap")
        P.wait_all("sp")
        P.emit()
        return nc

    def tap(nme, ap):
        if DEBUG_TAPS and nme in DEBUG_TAPS:
            tapped[nme] = ap

    def bank(i):
        return psum[:, i * 512:(i + 1) * 512]

    def bank_b(i):
        return psum_b[:, i * 1024:(i + 1) * 1024]

    hT = r3(A.tile(BF16, 8 * S), 8)
    hT_b = [Buf(f"hT{t}") for t in range(NT)]
    ident_b = A.tile(BF16, 128)
    ident_f = A.tile(F32, 128)
    ones_b = A.tile(BF16, 128)
    maskb = A.tile(BF16, 128)
    mask01 = A.tile(BF16, 128)
    iota_i = A.tile(I32, 128)
    eps_t = A.tile(F32, 1)
    mhalf = A.tile(F32, 2)
    lnq_t = A.tile(F32, 1)
    junk = A.tile(BF16, D)
    junk_b = Buf("junk")
    cst = Buf("consts")
    gsm = Buf("gsm")
    NWB = 7
    wbuf = [A.tile(BF16, 2048) for _ in range(NWB)]
    wbuf_b = [Buf(f"wb{i}") for i in range(NWB)]
    oT = r3(A.tile(BF16, 4 * S), 4)
    oT_b = [[Buf(f"o{h}_{c}") for c in range(NC4)] for h in range(8)]
    zu = [A.tile(BF16, 512) for _ in range(3)]
    zu_b = [Buf(f"zu{i}") for i in range(3)]
    rn = [A.tile(F32, 512) for _ in range(2)]
    rn_b = [Buf("rn0"), Buf("rn1")]
    m_glob = A.mark()
    ib_glob = len(ALLBUFS)

    P.op("pool", lambda e: e.iota(iota_i, [[1, 128]], base=0, channel_multiplier=-1), writes=[cst])
    P.op("dve", TS(ident_f, iota_i, 0, ALU.is_equal), reads=[cst], writes=[cst])
    P.op("dve", CP(ident_b, ident_f), reads=[cst], writes=[cst])
    P.op("dve", TS(maskb, iota_i, 0, ALU.is_lt, -30000.0, ALU.mult), reads=[cst], writes=[cst])
    P.op("dve", TS(mask01, iota_i, 0, ALU.is_ge), reads=[cst], writes=[cst])
    P.op("pool", MS(ones_b, 1.0), writes=[cst])
    P.op("pool", MS(eps_t, EPS), writes=[cst])
    P.op("pool", MS(mhalf, -0.5), writes=[cst])
    P.op("pool", MS(lnq_t, math.log(128.0 ** -0.5)), writes=[cst])

    wctr = {"s": 0, "w": 0}

    def k3(dram2d):
        return dram2d.rearrange("(kc p) n -> p kc n", p=128)

    def load_block(src3, kcn, ncols, dst3, dst_buf, scale_ap=None, scale_eng="dve", sem=None):
        P.dma("pool", dst3, src3, sem or ("d_" + dst_buf.name), writes=[dst_buf])
        if scale_ap is not None:
            P.op(scale_eng, TT(dst3, dst3, scale_ap, ALU.mult), reads=[gsm], writes=[dst_buf])

    def stream(dram2d, c0, ncols, kcn=8):
        j = wctr["w"] % NWB
        wctr["w"] += 1
        dst3 = r3(wbuf[j][:, 0:kcn * ncols], kcn)
        load_block(k3(dram2d)[:, :, c0:c0 + ncols], kcn, ncols, dst3, wbuf_b[j])
        return dst3, wbuf_b[j]

    evac_rr = {"i": 0}

    def evac_copy(out, in_, reads, writes):
        evac_rr["i"] += 1
        if evac_rr["i"] % 2:
            return P.op("act", ACTV(out, in_, AF.Copy), reads=reads, writes=writes)
        return P.op("dve", CP(out, in_), reads=reads, writes=writes)

    def proj_fns(lhs_of_kc, m_rows, c, pb):
        return [MM(bank(pb)[0:m_rows, :], lhs_of_kc(kc), hT[:, kc, c * 512:(c + 1) * 512], start=(kc == 0), stop=(kc == 7))
                for kc in range(8)]

    def hTc(c):
        return hT_b[4 * c:4 * c + 4]

    ropeT = A.tile(F32, S)
    cos2T = ropeT[0:32, :]
    sin2T = ropeT[32:64, :]
    rope_b = Buf("ropeT")
    cqT = r3(A.tile(BF16, 3 * S), 3)
    ckvT = r3(A.tile(BF16, 2 * S), 2)
    cq_b = [Buf(f"cq{c}") for c in range(NC4)]
    ckv_b = [Buf(f"ckv{c}") for c in range(NC4)]
    krope = A.tile(BF16, S)
    krope_b = [Buf(f"krope{c}") for c in range(NC4)]
    scr_off = (A.top + 63) // 64 * 64
    ktmp = [A.tile(F32, 512) for _ in range(2)]
    ktmp_b = Buf("ktmp")
    sq = [A.tile(BF16, 512) for _ in range(4)]
    sq_b = [Buf(f"sq{i}") for i in range(4)]
    rtmp = [A.tile(F32, 512) for _ in range(4)]
    rtmp_b = [Buf(f"rt{i}") for i in range(4)]
    assert A.top - scr_off == 16384, (A.top, scr_off)
    sT = r3(A.views[BF16][:, scr_off // 2: scr_off // 2 + 4 * S], 4)
    wkrot = r3(A.tile(BF16, 8 * 32), 8)
    wkrot_b = Buf("wkrot")

    wuq = r3(A.tile(BF16, 3 * 768), 3)
    wuqrot = A.tile(BF16, 3 * 8 * 96).rearrange("p (k h e) -> p k h e", k=3, h=8)
    wukv = r3(A.tile(BF16, 2 * 1024), 2)
    wuq_b, wuqrot_b, wukv_b = Buf("wuq"), Buf("wuqrot"), Buf("wukv")
    gq_t = A.tile(F32, 3)
    gkv_t = A.tile(F32, 2)
    m_rope = A.mark()
    ib_tmp0 = len(ALLBUFS)
    posi = A.tile(I32, NT)
    posf = A.tile(F32, NT)
    posT_i = A.tile(I32, 128)
    posT_f = A.tile(F32, 128)
    freqrow = A.tile(F32, 32)
    ang = A.tile(F32, 512)
    angc = A.tile(F32, 512)
    ki = A.tile(I32, 512)
    kf = A.tile(F32, 512)
    r1 = A.tile(F32, 512)
    r2 = A.tile(F32, 512)
    fx = A.tile(F32, 512)
    sc_tok = A.tile(F32, 1024)
    rp = Buf("rope_tmp")
    rfin = [A.tile(F32, 512) for _ in range(2)]
    fr3 = freqrow.rearrange("p (two j) -> p two j", two=2)
    for j in range(16):
        fj = 10000.0 ** (-j / 16.0)
        P.op("pool", MS(fr3[:, :, j:j + 1], fj), writes=[rp])
    rope_items = []

    def ritem(eng, fn, reads, writes):
        rope_items.append(lambda: P.op(eng, fn, reads=reads, writes=writes))

    ritem("dve", CP(posT_f[0:NT, :], posT_i[0:NT, :]), [rp], [rp])
    rope_items.append(lambda: P.op("pe", TR(bank(0)[:, 0:NT], posT_f[0:NT, :], ident_f[0:NT, 0:NT]), reads=[rp, cst], writes=[PB[0]]))
    rope_items.append(lambda: P.op("dve", CP(posf, bank(0)[:, 0:NT]), reads=[PB[0]], writes=[rp]))
    ritem("dve", TT(r3(ang, NT), posf.unsqueeze(2).to_broadcast([128, NT, 32]),
                    freqrow.unsqueeze(1).to_broadcast([128, NT, 32]), ALU.mult), [rp], [rp])
    ritem("dve", TS(angc, ang, math.pi / 2, ALU.add), [rp], [rp])
    TWO_PI = 2 * math.pi
    C1 = 6.28125
    C2 = TWO_PI - C1
    for idx, src in enumerate((ang, angc)):
        ritem("dve", TS(ki, src, 1.0 / TWO_PI, ALU.mult), [rp], [rp])
        ritem("dve", CP(kf, ki), [rp], [rp])
        ritem("dve", STT(r1, kf, -C1, src, ALU.mult, ALU.add), [rp], [rp])
        ritem("dve", STT(r2, kf, -C2, r1, ALU.mult, ALU.add), [rp], [rp])
        ritem("dve", TS(fx, r2, math.pi, ALU.is_gt, -TWO_PI, ALU.mult), [rp], [rp])
        ritem("dve", TT(r1, r2, fx, ALU.add), [rp], [rp])
        ritem("dve", TS(fx, r1, -math.pi, ALU.is_lt, TWO_PI, ALU.mult), [rp], [rp])
        ritem("dve", TT(r2, r1, fx, ALU.add), [rp], [rp])
        ritem("dve", TS(rfin[idx], r2, math.pi, ALU.min, -math.pi, ALU.max), [rp], [rp])
    rope_tail = []
    for idx in range(2):
        rope_tail.append(lambda idx=idx: P.op("act", ACTV(sc_tok[:, idx * 512:(idx + 1) * 512], rfin[idx], AF.Sin), reads=[rp], writes=[rp]))
    sc4 = sc_tok.rearrange("p (s t f) -> p s t f", s=2, t=NT)

    def rope_tr(idx, dstT, g):
        def f():
            pb = g % 2
            fns = [TR(bank(pb)[0:32, k * 128:(k + 1) * 128], sc4[:, idx, g * 4 + k, :], ident_f) for k in range(4)]
            P.op("pe", fns, reads=[rp, cst], writes=[PB[pb]])
            P.op("dve", CP(dstT[:, g * 512:(g + 1) * 512], bank(pb)[0:32, :]), reads=[PB[pb]], writes=[rope_b])
        return f

    for idx, dstT in ((1, cos2T), (0, sin2T)):
        for g in range(4):
            rope_tail.append(rope_tr(idx, dstT, g))

    gin_bc = A.tile(F32, D)
    NXB = 4
    xs = [A.tile(F32, D) for _ in range(NXB)]
    xs_b = [Buf(f"xs{i}") for i in range(NXB)]
    xn = [A.tile(BF16, D) for _ in range(2)]
    xn_b = [Buf(f"xn{i}") for i in range(2)]
    st0 = A.tile(F32, 3 * NT)
    st0_b = [Buf(f"st0_{t}") for t in range(NT)]

    def p0_load(t):
        i = t % NXB
        P.dma("sp", xs[i], x_d[t * 128:(t + 1) * 128, :], f"d_xs{i}", writes=[xs_b[i]])

    def p0_A(t):
        i = t % NXB
        if t + 2 < NT and t + 2 >= NXB - 1:
            p0_load(t + 2)
        ss = st0[:, 3 * t:3 * t + 1]
        ln = st0[:, 3 * t + 1:3 * t + 2]
        rs = st0[:, 3 * t + 2:3 * t + 3]
        P.op("act", ACTV(junk, xs[i], AF.Square, accum_out=ss), reads=[xs_b[i]], writes=[junk_b, st0_b[t]])
        P.op("act", ACTV(ln, ss, AF.Ln, scale=1.0 / D, bias=eps_t), reads=[st0_b[t], cst], writes=[st0_b[t]])
        P.op("act", ACTV(rs, ln, AF.Exp, scale=-0.5), reads=[st0_b[t]], writes=[st0_b[t]])

    def p0_B(t):
        i = t % NXB
        rs = st0[:, 3 * t + 2:3 * t + 3]
        k = t % 2
        P.op("dve", STT(xn[k], xs[i], rs, gin_bc, ALU.mult, ALU.mult), reads=[xs_b[i], st0_b[t], gin_b], writes=[xn_b[k]])
        pb = 2 + (t % 2)
        fns = [TR(bank_b(pb)[:, kc * 128:(kc + 1) * 128], xn[k][:, kc * 128:(kc + 1) * 128], ident_b) for kc in range(8)]
        P.op("pe", fns, reads=[xn_b[k], cst], writes=[PB[pb]])

    def p0_C(t):
        pb = 2 + (t % 2)
        P.op("act", ACTV(hT[:, :, t * 128:(t + 1) * 128], r3(bank_b(pb), 8), AF.Copy), reads=[PB[pb]], writes=[hT_b[t]])

    ssq_ctr = {"i": 0, "pb": 0}
    ln_pending = []

    def ln_tick(flush=False):
        for it_ in ln_pending:
            it_[0] += 1
        while ln_pending and (flush or ln_pending[0][0] >= 2):
            ln_pending.pop(0)[1]()

    def latent_norm(dstT, dst_b, nchunk, width, srcs, c):
        ssq_pb = 6 + (ssq_ctr["i"] % 2)
        ssq_ctr["i"] += 1

        def ones_mm(j, k):
            def f():
                P.op("pe", MM(bank(ssq_pb), ones_b, sq[k], start=(j == 0), stop=(j == nchunk - 1)), reads=[sq_b[k], cst], writes=[PB[ssq_pb]])
                if j == nchunk - 1:
                    r0, r1_ = rtmp[(ssq_pb % 2) * 2], rtmp[(ssq_pb % 2) * 2 + 1]
                    rb0, rb1 = rtmp_b[(ssq_pb % 2) * 2], rtmp_b[(ssq_pb % 2) * 2 + 1]
                    P.op("act", ACTV(r0, bank(ssq_pb), AF.Ln, scale=1.0 / width, bias=eps_t), reads=[PB[ssq_pb], cst], writes=[rb0])
                    P.op("act", ACTV(r1_, r0, AF.Exp, scale=-0.5), reads=[rb0], writes=[rb1])
                    for jj in range(nchunk):
                        sl = dstT[:, jj, c * 512:(c + 1) * 512]
                        P.op("dve", TT(sl, sl, r1_, ALU.mult), reads=[rb1, dst_b[c]], writes=[dst_b[c]])
            return f

        for j, (w3, wb, c0) in enumerate(srcs):
            pb = 2 + ssq_ctr["pb"] % 4
            ssq_ctr["pb"] += 1
            P.op("pe", proj_fns(lambda kc: w3[:, kc, c0:c0 + 128], 128, c, pb), reads=[wb] + hTc(c), writes=[PB[pb]])
            k = ssq_ctr["pb"] % 4
            dsl = dstT[:, j, c * 512:(c + 1) * 512]
            P.op("act", ACTV(dsl, bank(pb), AF.Copy), reads=[PB[pb]], writes=[dst_b[c]])
            P.op("act", ACTV(sq[k], bank(pb), AF.Square), reads=[PB[pb]], writes=[sq_b[k]])
            ln_tick()
            ln_pending.append([0, ones_mm(j, k)])

    W1A = {}

    def phase1a_weights():
        w0, w0b = stream(win_d, 0, 256)
        w1, w1b = stream(win_d, 256, 256)
        w2, w2b = stream(win_d, 512, 160)
        W1A.update(w0=w0, w0b=w0b, w1=w1, w1b=w1b, w2=w2, w2b=w2b)

    def wkrot_prep():
        w2, w2b = W1A["w2"], W1A["w2b"]
        P.op("dve", TS(wkrot[:, :, 0:16], w2[:, :, 144:160], -1.0, ALU.mult), reads=[w2b], writes=[wkrot_b])
        P.op("dve", CP(wkrot[:, :, 16:32], w2[:, :, 128:144]), reads=[w2b], writes=[wkrot_b])

    def phase1a(after_chunk):
        w0, w0b, w1, w1b, w2, w2b = (W1A[k] for k in ("w0", "w0b", "w1", "w1b", "w2", "w2b"))
        for c in range(NC4):
            latent_norm(cqT, cq_b, 3, 384.0, [(w0, w0b, 0), (w0, w0b, 128), (w1, w1b, 0)], c)
            latent_norm(ckvT, ckv_b, 2, 256.0, [(w1, w1b, 128), (w2, w2b, 0)], c)
            P.op("pe", proj_fns(lambda kc: w2[:, kc, 128:160], 32, c, 4), reads=[w2b] + hTc(c), writes=[PB[4]])
            P.op("pe", proj_fns(lambda kc: wkrot[:, kc, :], 32, c, 5), reads=[wkrot_b] + hTc(c), writes=[PB[5]])
            sl = slice(c * 512, (c + 1) * 512)
            P.op("dve", TT(ktmp[0][0:32, :], bank(4)[0:32, :], cos2T[:, sl], ALU.mult), reads=[PB[4], rope_b], writes=[ktmp_b])
            P.op("dve", TT(ktmp[1][0:32, :], bank(5)[0:32, :], sin2T[:, sl], ALU.mult), reads=[PB[5], rope_b], writes=[ktmp_b])
            P.op("dve", TT(krope[0:32, sl], ktmp[0][0:32, :], ktmp[1][0:32, :], ALU.add), reads=[ktmp_b], writes=[krope_b[c]])
            ln_tick(flush=True)
            after_chunk(c)

    for t_ in range(NXB - 1):
        p0_load(t_)
    gin_b = Buf("gin")
    P.dma("sp", gin_bc, gin_d.to_broadcast([128, D]), "d_m2", writes=[gin_b])
    def tiny_dmas():
        P.dma("sp", posT_i[0:NT, :], pos_d.rearrange("o (t p) -> (o t) p", p=128), "d_m1", writes=[rp])
        P.dma("sp", gq_t, gq_d.rearrange("o (k p) -> p (o k)", p=128), "d_m3", writes=[gsm], allow_slow_non_contiguous=True)
        P.dma("sp", gkv_t, gkv_d.rearrange("o (k p) -> p (o k)", p=128), "d_m4", writes=[gsm], allow_slow_non_contiguous=True)
    phase1a_weights()
    wsc = []
    for hf in range(2):
        load_block(k3(wuq_d)[:, :, hf * 384:(hf + 1) * 384], 3, 384, wuq[:, :, hf * 384:(hf + 1) * 384], wuq_b, sem=f"d_wuq{hf}")
        wsc.append(lambda hf=hf: P.op("dve", TT(wuq[:, :, hf * 384:(hf + 1) * 384], wuq[:, :, hf * 384:(hf + 1) * 384],
                                                gq_t.unsqueeze(2).to_broadcast([128, 3, 384]), ALU.mult), reads=[gsm], writes=[wuq_b]))
    for hf in range(2):
        load_block(k3(wukv_d)[:, :, hf * 512:(hf + 1) * 512], 2, 512, wukv[:, :, hf * 512:(hf + 1) * 512], wukv_b, sem=f"d_wukv{hf}")
        wsc.append(lambda hf=hf: P.op("dve", TT(wukv[:, :, hf * 512:(hf + 1) * 512], wukv[:, :, hf * 512:(hf + 1) * 512],
                                                gkv_t.unsqueeze(2).to_broadcast([128, 2, 512]), ALU.mult), reads=[gsm], writes=[wukv_b]))
    P.op("pool", MS(wuqrot.rearrange("p k h e -> p (k h e)"), 0.0), writes=[wuqrot_b])
    for i in range(NT + 2):
        if i < NT:
            p0_A(i)
        if 0 <= i - 1 < NT:
            p0_B(i - 1)
        if 0 <= i - 2 < NT:
            p0_C(i - 2)
        if i == 3:
            tiny_dmas()
        for _ in range(2):
            if i >= 5 and rope_items:
                rope_items.pop(0)()
    while rope_items:
        rope_items.pop(0)()
    wkrot_prep()
    while wsc:
        wsc.pop(0)()
    wuq4 = wuq.rearrange("p k (h e) -> p k h e", h=8)
    P.op("dve", TS(wuqrot[:, :, :, 64:80], wuq4[:, :, :, 80:96], -1.0, ALU.mult), reads=[wuq_b], writes=[wuqrot_b])
    P.op("dve", CP(wuqrot[:, :, :, 80:96], wuq4[:, :, :, 64:80]), reads=[wuq_b], writes=[wuqrot_b])
    for f_ in rope_tail:
        f_()
    tap("hT", hT)
    if STOP_AFTER == "p0":
        return fin()
    ib_tmp1 = len(ALLBUFS)
    A.release(m_rope)
    ARENA_LOG.append(("pre-att", A.top))
    ib_att0 = len(ALLBUFS)
    ropet = [A.tile(F32, 512) for _ in range(4)]
    ropet_b = [Buf(f"ropet{i}") for i in range(4)]
    qT = [A.tile(BF16, S) for _ in range(2)]
    kT = [A.tile(BF16, S) for _ in range(2)]
    vaug = [r3(A.tile(BF16, NT * 128), NT) for _ in range(2)]
    qT_b = [[Buf(f"q{i}_{c}") for c in range(NC4)] for i in range(2)]
    kT_b = [[Buf(f"k{i}_{c}") for c in range(NC4)] for i in range(2)]
    va_b = [[Buf(f"va{i}_{g}") for g in range(2)] for i in range(2)]
    NPT = 4
    PT = [A.tile(BF16, 512) for _ in range(NPT)]
    PT_b = [Buf(f"PT{i}") for i in range(NPT)]
    rc = [A.tile(F32, 512) for _ in range(2)]
    rc_b = [Buf("rc0"), Buf("rc1")]
    zt = [A.tile(F32, 512) for _ in range(2)]
    zt_b = [Buf("zt0"), Buf("zt1")]
    inherit(ALLBUFS[ib_att0:], ALLBUFS[ib_tmp0:ib_tmp1])
    SCALE = 96.0 ** -0.5

    for i in range(2):
        P.op("pool", MS(vaug[i].rearrange("p t e -> p (t e)"), 1.0), writes=va_b[i])

    def phase2_pieces(h):
        i = h % 2
        pieces = []

        def q_piece(c):
            def f():
                sl = slice(c * 512, (c + 1) * 512)
                ba = 0
                P.op("pe", [MM(bank(ba)[0:96, :], wuq[:, kc, h * 96:(h + 1) * 96], cqT[:, kc, sl], start=(kc == 0), stop=(kc == 2)) for kc in range(3)],
                     reads=[wuq_b, cq_b[c]], writes=[PB[ba]])
                P.op("pe", [MM(bank(1)[0:96, :], wuqrot[:, kc, h, :], cqT[:, kc, sl], start=(kc == 0), stop=(kc == 2)) for kc in range(3)],
                     reads=[wuqrot_b, cq_b[c]], writes=[PB[1]])
                P.op("act" if h == 0 else "dve", (ACTV(qT[i][0:64, sl], bank(ba)[0:64, :], AF.Copy) if h == 0 else CP(qT[i][0:64, sl], bank(ba)[0:64, :])), reads=[PB[ba]], writes=[qT_b[i][c]])
                i0, i1 = (c % 2) * 2, (c % 2) * 2 + 1
                P.op("dve", TT(ropet[i0][64:96, :], bank(ba)[64:96, :], cos2T[:, sl], ALU.mult), reads=[PB[ba], rope_b], writes=[ropet_b[i0]])
                P.op("dve", TT(ropet[i1][64:96, :], bank(1)[64:96, :], sin2T[:, sl], ALU.mult), reads=[PB[1], rope_b], writes=[ropet_b[i1]])
                P.op("dve", TT(qT[i][64:96, sl], ropet[i0][64:96, :], ropet[i1][64:96, :], ALU.add), reads=[ropet_b[i0], ropet_b[i1]], writes=[qT_b[i][c]])
            return f

        def k_piece(c):
            def f():
                sl = slice(c * 512, (c + 1) * 512)
                pb = c % 2
                P.op("pe", [MM(bank(pb)[0:64, :], wukv[:, kc, h * 128:h * 128 + 64], ckvT[:, kc, sl], start=(kc == 0), stop=(kc == 1)) for kc in range(2)],
                     reads=[wukv_b, ckv_b[c]], writes=[PB[pb]])
                P.op("dve", CP(kT[i][0:64, sl], bank(pb)[0:64, :]), reads=[PB[pb]], writes=[kT_b[i][c]])
                P.op("dve", CP(kT[i][64:96, sl], krope[0:32, sl]), reads=[krope_b[c]], writes=[kT_b[i][c]])
            return f

        def v_piece(g):
            def f():
                pb = g % 2
                fns = []
                for tt in range(8):
                    t = g * 8 + tt
                    for kc in range(2):
                        fns.append(MM(bank(pb)[:, tt * 64:(tt + 1) * 64], ckvT[:, kc, t * 128:(t + 1) * 128],
                                      wukv[:, kc, h * 128 + 64:h * 128 + 128], start=(kc == 0), stop=(kc == 1)))
                P.op("pe", fns, reads=[wukv_b] + ckv_b[2 * g:2 * g + 2], writes=[PB[pb]])
                vo = 0 if h % 2 == 0 else 64
                P.op("dve", CP(vaug[i][:, g * 8:(g + 1) * 8, vo:vo + 64], r3(bank(pb), 8)), reads=[PB[pb]], writes=[va_b[i][g]])
            return f

        for c in range(NC4):
            pieces.append(q_piece(c))
            pieces.append(k_piece(c))
            if c % 2 == 1:
                pieces.append(v_piece(c // 2))
        return pieces

    def attention():
        steps = [(h, c, kt) for h in range(8) for c in range(NC4) for kt in range(4 * c + 4)]
        per_head = len(steps) // 8
        LA = 3

        def emit_S(n):
            h, c, kt = steps[n]
            i = h % 2
            j = kt - 4 * c
            n0 = 128 * j if j > 0 else 0
            spb = (2, 3, 4, 7)[n % 4]
            pti = n % NPT
            qs = slice(c * 512 + n0, (c + 1) * 512)
            fns = [MM(bank(spb)[:, n0:512], kT[i][0:96, kt * 128:(kt + 1) * 128], qT[i][0:96, qs], start=True, stop=(j < 0))]
            if j >= 0:
                fns.append(MM(bank(spb)[:, n0:n0 + 128], ident_b, maskb, start=False, stop=True))
            P.op("pe", fns, reads=[kT_b[i][kt // 4], qT_b[i][c], cst], writes=[PB[spb]])
            P.op("act", ACTV(PT[pti][:, n0:512], bank(spb)[:, n0:512], AF.Exp, scale=SCALE), reads=[PB[spb]], writes=[PT_b[pti]])

        def emit_PV(n):
            h, c, kt = steps[n]
            i = h % 2
            j = kt - 4 * c
            n0 = 128 * j if j > 0 else 0
            pti = n % NPT
            nk = 4 * c + 4
            oc = h * NC4 + c
            opb = 5 + (oc % 2)
            P.op("pe", MM(bank(opb)[:, n0:512], vaug[i][:, kt, :], PT[pti][:, n0:512], start=(kt == 0), stop=(kt == nk - 1)),
                 reads=[va_b[i][kt // 8], PT_b[pti]], writes=[PB[opb]])
            if kt == nk - 1:
                vlo, slo = (0, 64) if h % 2 == 0 else (64, 0)
                k = oc % 2
                P.op("dve", CP(oT[vlo:vlo + 64, h // 2, c * 512:(c + 1) * 512], bank(opb)[vlo:vlo + 64, :]), reads=[PB[opb]], writes=[oT_b[h][c]])
                P.op("dve", CP(sT[vlo:vlo + 64, h // 2, c * 512:(c + 1) * 512], bank(opb)[slo:slo + 64, :]), reads=[PB[opb]], writes=[sT_b[h][c]])

        nxt_pieces = []
        for n in range(len(steps) + LA):
            if n < len(steps):
                h, c, kt = steps[n]
                if c == 0 and kt == 0 and h + 1 < 8:
                    nxt_pieces = phase2_pieces(h + 1)
                emit_S(n)
                pos_in_head = n - h * per_head
                if nxt_pieces and pos_in_head % 4 == 3:
                    nxt_pieces.pop(0)()
                if pos_in_head == per_head - 1:
                    while nxt_pieces:
                        nxt_pieces.pop(0)()
            if n - LA >= 0:
                emit_PV(n - LA)

    h0_pieces = phase2_pieces(0)

    def after_chunk(c):
        if c == 0:
            return
        n = 2 if (c - 1) % 2 == 0 else 3
        for _ in range(n):
            h0_pieces.pop(0)()

    phase1a(after_chunk)
    while h0_pieces:
        h0_pieces.pop(0)()
    assert not h0_pieces
    tap("cqT", cqT)
    tap("ckvT", ckvT)
    tap("krope", krope)
    if STOP_AFTER and STOP_AFTER.startswith("p1a"):
        return fin()
    sT_b = [[Buf(f"sT{h}_{c}") for c in range(NC4)] for h in range(8)]
    inherit([b for hb in sT_b for b in hb], [ktmp_b] + sq_b + rtmp_b)
    attention()
    norm_items = []
    for m in range(4):
        for c in range(NC4):
            def f(m=m, c=c):
                k = (m * NC4 + c) % 2
                sl = slice(c * 512, (c + 1) * 512)
                bs = [sT_b[2 * m][c], sT_b[2 * m + 1][c]]
                P.op("act", ACTV(rn[k], sT[:, m, sl], AF.Ln), reads=bs, writes=[rn_b[k]])
                P.op("act", ACTV(rn[k], rn[k], AF.Exp, scale=-1.0), reads=[rn_b[k]], writes=[rn_b[k]])
                P.op("dve", TT(oT[:, m, sl], oT[:, m, sl], rn[k], ALU.mult), reads=[rn_b[k], oT_b[2 * m][c], oT_b[2 * m + 1][c]],
                     writes=[oT_b[2 * m][c], oT_b[2 * m + 1][c]])
            norm_items.append(f)
    if STOP_AFTER == "att":
        while norm_items:
            norm_items.pop(0)()
    tap("oTraw", oT)
    if STOP_AFTER == "att":
        return fin()


    p3b_items = []

    def phase3b():
        it = 0
        for blk in range(2):
            holder = {}
            for mm in range(2):
                m = blk * 2 + mm
                for c in range(NC4):
                    def f(blk=blk, mm=mm, m=m, c=c, it=it, holder=holder):
                        if "w" not in holder:
                            holder["w"] = stream(win_d, C_ZM + blk * 256, 256)
                        w3, wb = holder["w"]
                        pb = 3 + (it % 2)
                        k = it % 3
                        P.op("pe", proj_fns(lambda kc: w3[:, kc, mm * 128:(mm + 1) * 128], 128, c, pb), reads=[wb] + hTc(c), writes=[PB[pb]])
                        P.op("act", ACTV(zu[k], bank(pb), AF.Silu), reads=[PB[pb]], writes=[zu_b[k]])
                        sl = oT[:, m, c * 512:(c + 1) * 512]
                        P.op("dve", TT(sl, sl, zu[k], ALU.mult), reads=[zu_b[k], oT_b[2 * m][c], oT_b[2 * m + 1][c]], writes=[oT_b[2 * m][c], oT_b[2 * m + 1][c]])
                    p3b_items.append(f)
                    it += 1

    phase3b()
    if STOP_AFTER == "p3b":
        while p3b_items:
            p3b_items.pop(0)()
    tap("oT", oT)
    if STOP_AFTER == "p3b":
        return fin()

    PRE = {}
    PRE["qk0"] = [stream(win_d, C_QG, 256), stream(win_d, C_KG, 256)]
    A.release(m_glob)
    ARENA_LOG.append(("end-att", A.peak))
    ib_gla0 = len(ALLBUFS)
    zgT = r3(A.tile(BF16, 8 * S), 8)
    zg_b = [[Buf(f"zg{m}_{t}") for t in range(NT)] for m in range(8)]
    m_gla = A.mark()
    ib_glap = len(ALLBUFS)
    _q1 = r3(A.tile(BF16, 2 * S), 2)
    _k1 = r3(A.tile(BF16, 2 * S), 2)
    assert scr_off >= m_gla and scr_off + 16384 <= A.top, (scr_off, m_gla, A.top)
    _q0 = r3(A.tile(BF16, 2 * S), 2)
    _k0 = r3(A.tile(BF16, 2 * S), 2)
    qgTs = [_q0, _q1]
    kgTs = [_k0, _k1]
    vg = r3(A.tile(BF16, NT * 512), NT)
    qg_bs = [[[Buf(f"qg{p}{l}_{c}") for c in range(NC4)] for l in range(2)] for p in range(2)]
    kg_bs = [[[Buf(f"kg{p}{l}_{c}") for c in range(NC4)] for l in range(2)] for p in range(2)]
    vg_b = [Buf(f"vg{t}") for t in range(NT)]
    walr = r3(A.tile(BF16, 8 * 16), 8)
    walr_b = Buf("walr")
    alrc = [A.tile(F32, 512) for _ in range(NC4)]
    alrc_b = [Buf(f"alrc{i}") for i in range(NC4)]
    wga = A.tile(F32, 512)
    wga_b = Buf("wga")
    dec = r3(A.tile(F32, 2 * NT), 2)
    dec_b = [[Buf(f"dec{l}_{c}") for c in range(NC4)] for l in range(2)]
    Sst = r3(A.tile(F32, 2 * 256), 2)
    Sst_b = [Buf(f"S{l}") for l in range(2)]
    Sbf = [r3(A.tile(BF16, 2 * 256), 2) for _ in range(2)]
    Sbf_b = [Buf("Sbf0"), Buf("Sbf1")]
    ss4 = [A.tile(F32, 8) for _ in range(2)]
    ss4_b = [Buf("ss4a"), Buf("ss4b")]
    AmT = [r3(A.tile(BF16, 256), 2) for _ in range(2)]
    AmT_b = [Buf("AmT0"), Buf("AmT1")]
    kstT = [r3(A.tile(BF16, 256), 2) for _ in range(2)]
    kstT_b = [Buf("kstT0"), Buf("kstT1")]
    kstt = [r3(A.tile(BF16, 256), 2) for _ in range(2)]
    kstt_b = [Buf("kstt0"), Buf("kstt1")]
    ogn = [A.tile(BF16, 512) for _ in range(2)]
    ogn_b = [Buf("ogn0"), Buf("ogn1")]
    junk2 = [A.tile(BF16, 256) for _ in range(2)]
    junk2_b = [Buf("junk2a"), Buf("junk2b")]
    rmask = A.tile(F32, 512)
    _gA = [A.tile(F32, 512) for _ in range(2)]
    _gB = [A.tile(F32, 512) for _ in range(2)]
    _gC = [A.tile(F32, 512) for _ in range(2)]
    gA, gB, gC = [_gA, _gA], [_gB, _gB], [_gC, _gC]
    _gAb = [Buf(f"gA{l}") for l in range(2)]
    _gBb = [Buf(f"gB{l}") for l in range(2)]
    _gCb = [Buf(f"gC{l}") for l in range(2)]
    gA_b, gB_b, gC_b = [_gAb, _gAb], [_gBb, _gBb], [_gCb, _gCb]
    zt = [A.tile(F32, 512) for _ in range(3)]
    zt_b = [Buf("zt0b"), Buf("zt1b"), Buf("zt2b")]

    ib_gla1 = len(ALLBUFS)
    inherit(ALLBUFS[ib_gla0:ib_gla1], ALLBUFS[ib_glob:ib_gla0])
    rmask_b = Buf("rmask")
    inherit([rmask_b], ALLBUFS[ib_glob:ib_gla0])
    P.op("pool", MS(rmask, 1.0), writes=[rmask_b])
    P.op("pool", MS(rmask.rearrange("p (t b) -> p t b", b=128)[:, :, 0:1], 0.0), writes=[rmask_b])
    for i in range(NC4):
        P.op("pool", MS(alrc[i][0:32, :], 1.0), writes=[alrc_b[i]])
    P.dma("sp", wga[0:16, :], wgg_d, "d_wga", writes=[wga_b])
    P.dma("sp", wga[16:17, :], bgg_d, "d_wga", writes=[wga_b])
    load_block(k3(win_d)[:, :, C_ALR:C_ALR + 16], 8, 16, walr, walr_b)

    def merge_emit(bulk, chain):
        nb, ncn = len(bulk), len(chain)
        bi = ci = 0
        while bi < nb or ci < ncn:
            if bi < nb:
                bulk[bi]()
                bi += 1
            tgt = ncn if bi >= nb else (bi * ncn + nb - 1) // nb
            while ci < tgt:
                chain[ci]()
                ci += 1

    def gla_pass(p):
        it = [0]
        qgT, kgT, qg_b, kg_b = qgTs[p], kgTs[p], qg_bs[p], kg_bs[p]
        if p == 0:
            for wi_, (c0, dstT, dst_b) in enumerate(((C_QG, qgT, qg_b), (C_KG, kgT, kg_b))):
                w3, wb = PRE["qk0"][wi_]
                for l in range(2):
                    for c in range(NC4):
                        pb = it[0] % 3
                        it[0] += 1
                        if norm_items:
                            norm_items.pop(0)()
                        elif p3b_items:
                            p3b_items.pop(0)()
                        P.op("pe", proj_fns(lambda kc: w3[:, kc, l * 128:(l + 1) * 128], 128, c, pb), reads=[wb] + hTc(c), writes=[PB[pb]])
                        evac_copy(dstT[:, l, c * 512:(c + 1) * 512], bank(pb), reads=[PB[pb]], writes=[dst_b[l][c]])
        if p == 0:
            while norm_items:
                norm_items.pop(0)()
            while p3b_items:
                p3b_items.pop(0)()
            inherit([b for l_ in qg_bs[1] + kg_bs[1] for b in l_], [b for hb in sT_b for b in hb])
        bulk = []

        def vg_item(blk, t, holder):
            def f():
                if t == 0:
                    holder["w"] = PRE["vg1"][blk] if (p == 1 and "vg1" in PRE) else stream(win_d, C_VG + 512 * p + blk * 256, 256)
                w3, wb = holder["w"]
                pb = it[0] % 3
                it[0] += 1
                P.op("pe", [MM(bank(pb)[:, 0:256], hT[:, kc, t * 128:(t + 1) * 128], w3[:, kc, :], start=(kc == 0), stop=(kc == 7)) for kc in range(8)],
                     reads=[wb, hT_b[t]], writes=[PB[pb]])
                evac_copy(vg[:, t, blk * 256:(blk + 1) * 256], bank(pb)[:, 0:256], reads=[PB[pb]], writes=[vg_b[t]])
            return f

        zgw = {}
        if p == 1:
            for blk_ in range(2):
                zgw[blk_] = stream(win_d, C_ZG + 512 * p + blk_ * 256, 256)

        def zg_fill(i):
            c, idx = i // 4, i % 4
            blk, mm = idx // 2, idx % 2
            w3, wb = zgw[blk]
            m = 4 * p + blk * 2 + mm
            P.op("pe", proj_fns(lambda kc: w3[:, kc, mm * 128:(mm + 1) * 128], 128, c, 7), reads=[wb] + hTc(c), writes=[PB[7]])
            P.op("act", ACTV(zgT[:, m, c * 512:(c + 1) * 512], bank(7), AF.Silu), reads=[PB[7]], writes=zg_b[m][4 * c:4 * c + 4])

        for blk in range(2):
            holder = {}
            for t in range(NT):
                bulk.append(vg_item(blk, t, holder))
        chain = []

        def add(fn):
            chain.append(fn)

        for c in range(NC4):
            sl = slice(c * 512, (c + 1) * 512)
            i2 = c % 2
            pbA = 3 + (c % 2)
            if p == 0:
                add(lambda c=c, pbA=pbA: P.op("pe", proj_fns(lambda kc: walr[:, kc, :], 16, c, pbA), reads=[walr_b] + hTc(c), writes=[PB[pbA]]))
                add(lambda c=c, pbA=pbA: P.op("dve", CP(alrc[c][0:16, :], bank(pbA)[0:16, :]), reads=[PB[pbA]], writes=[alrc_b[c]]))
            for l in range(2):
                f = 2 * p + l
                pg = 5 + l
                add(lambda c=c, f=f, pg=pg: P.op("pe", MM(bank(pg), wga[0:17, f * 128:(f + 1) * 128], alrc[c][0:17, :]), reads=[wga_b, alrc_b[c]], writes=[PB[pg]]))
            for l in range(2):
                pg = 5 + l
                add(lambda i2=i2, l=l, pg=pg: P.op("act", ACTV(gA[i2][l], bank(pg), AF.Exp, scale=-1.0), reads=[PB[pg]], writes=[gA_b[i2][l]]))
            for l in range(2):
                add(lambda i2=i2, l=l: P.op("act", ACTV(gB[i2][l], gA[i2][l], AF.Ln, bias=1.0, scale=1.0), reads=[gA_b[i2][l]], writes=[gB_b[i2][l]]))
            for l in range(2):
                add(lambda i2=i2, l=l: P.op("dve", lambda e: e.tensor_tensor_scan(out=gA[i2][l], data0=rmask, data1=gB[i2][l], initial=0.0, op0=ALU.mult, op1=ALU.add),
                                             reads=[gB_b[i2][l], rmask_b], writes=[gA_b[i2][l]]))
            for l in range(2):
                add(lambda i2=i2, l=l, c=c: P.op("act", ACTV(dec[:, l, 4 * c:4 * c + 4], gA[i2][l].rearrange("p (t b) -> p t b", b=128)[:, :, 127], AF.Exp, scale=-1.0 / 16),
                                                  reads=[gA_b[i2][l]], writes=[dec_b[l][c]]))
                add(lambda i2=i2, l=l: P.op("act", ACTV(gB[i2][l], gA[i2][l], AF.Exp, scale=-1.0 / 16, bias=lnq_t), reads=[gA_b[i2][l], cst], writes=[gB_b[i2][l]]))
                add(lambda i2=i2, l=l: P.op("act", ACTV(gC[i2][l], gA[i2][l], AF.Exp, scale=1.0 / 16), reads=[gA_b[i2][l]], writes=[gC_b[i2][l]]))
            for l in range(2):
                add(lambda i2=i2, l=l, c=c, sl=sl: P.op("dve", TT(qgT[:, l, sl], qgT[:, l, sl], gB[i2][l], ALU.mult), reads=[gB_b[i2][l], qg_b[l][c]], writes=[qg_b[l][c]]))
                add(lambda i2=i2, l=l, c=c, sl=sl: P.op("dve", TT(kgT[:, l, sl], kgT[:, l, sl], gC[i2][l], ALU.mult), reads=[gC_b[i2][l], kg_b[l][c]], writes=[kg_b[l][c]]))
        merge_emit(bulk, chain)

        PA_, PKT_, PO_, PS_, PT_ = (0, 0), 2, (3, 4), 5, (6, 6)
        fill_w = {}

        def filler(i):
            which, l, c = i // 8, (i % 8) // 4, i % 4
            if which not in fill_w:
                fill_w[which] = stream(win_d, (C_QG, C_KG)[which] + 256, 256)
            w3, wb = fill_w[which]
            dstT, dst_b = ((qgTs[1], qg_bs[1]), (kgTs[1], kg_bs[1]))[which]
            P.op("pe", proj_fns(lambda kc: w3[:, kc, l * 128:(l + 1) * 128], 128, c, 1), reads=[wb] + hTc(c), writes=[PB[1]])
            P.op("act", ACTV(dstT[:, l, c * 512:(c + 1) * 512], bank(1), AF.Copy), reads=[PB[1]], writes=[dst_b[l][c]])


        def st0(t):
            tb = slice(t * 128, (t + 1) * 128)
            c = t // 4
            par = t % 2
            pa = PA_[par]
            P.op("pe", [MM(bank(pa)[:, l * 128:(l + 1) * 128], kgT[:, l, tb], qgT[:, l, tb]) for l in range(2)],
                 reads=[kg_b[0][c], kg_b[1][c], qg_b[0][c], qg_b[1][c]], writes=[PB[pa]])
            if t < NT - 1:
                P.op("dve", TT(kstT[par], kgT[:, :, tb], dec[:, :, t:t + 1].to_broadcast([128, 2, 128]), ALU.mult),
                     reads=[kg_b[0][c], kg_b[1][c], dec_b[0][c], dec_b[1][c]], writes=[kstT_b[par]])
                P.op("pe", [TR(bank_b(PKT_)[:, l * 128:(l + 1) * 128], kstT[par][:, l, :], ident_b) for l in range(2)], reads=[kstT_b[par], cst], writes=[PB[PKT_]])
            P.op("dve", TT(AmT[par], r3(bank(pa)[:, 0:256], 2), mask01.unsqueeze(1).to_broadcast([128, 2, 128]), ALU.mult), reads=[PB[pa], cst], writes=[AmT_b[par]])
            if t < NT - 1:
                P.op("act", ACTV(kstt[par], r3(bank_b(PKT_)[:, 0:256], 2), AF.Copy), reads=[PB[PKT_]], writes=[kstt_b[par]])

        def st1(t):
            tb = slice(t * 128, (t + 1) * 128)
            c = t // 4
            par = t % 2
            po = PO_[par]
            fns = []
            for l in range(2):
                o_ap = bank(po)[:, l * 256:(l + 1) * 256]
                fns.append(MM(o_ap, AmT[par][:, l, :], vg[:, t, l * 256:(l + 1) * 256], start=True, stop=(t == 0)))
                if t > 0:
                    fns.append(MM(o_ap, qgT[:, l, tb], Sbf[par][:, l, :], start=False, stop=True))
            P.op("pe", fns, reads=[AmT_b[par], vg_b[t], Sbf_b[par], qg_b[0][c], qg_b[1][c]], writes=[PB[po]])
            if t < NT - 1:
                P.op("pe", [MM(bank(PS_)[:, l * 256:(l + 1) * 256], kstt[par][:, l, :], vg[:, t, l * 256:(l + 1) * 256]) for l in range(2)],
                     reads=[kstt_b[par], vg_b[t]], writes=[PB[PS_]])
                for l in range(2):
                    s_ap = bank(PS_)[:, l * 256:(l + 1) * 256]
                    if t == 0:
                        P.op("dve", CP(Sst[:, l, :], s_ap), reads=[PB[PS_]], writes=[Sst_b[l]])
                    else:
                        P.op("dve", STT(Sst[:, l, :], Sst[:, l, :], dec[:, l, t:t + 1], s_ap, ALU.mult, ALU.add),
                             reads=[PB[PS_], dec_b[l][c], Sst_b[l]], writes=[Sst_b[l]])
                P.op("act", ACTV(Sbf[1 - par].rearrange("p l v -> p (l v)"), Sst.rearrange("p l v -> p (l v)"), AF.Copy), reads=Sst_b, writes=[Sbf_b[1 - par]])
            for l in range(2):
                o_ap = bank(po)[:, l * 256:(l + 1) * 256]
                P.op("act", ACTV(junk2[l], o_ap, AF.Square, accum_out=ss4[par][:, l:l + 1]), reads=[PB[po]], writes=[junk2_b[l], ss4_b[par]])
            P.op("pool", TS(ss4[par][:, 2:4], ss4[par][:, 0:2], 1.0 / 256, ALU.mult, EPS, ALU.add), reads=[ss4_b[par]], writes=[ss4_b[par]])
            P.op("pool", TT(ss4[par][:, 4:6], ss4[par][:, 2:4], mhalf[:, 0:2], ALU.pow), reads=[ss4_b[par], cst], writes=[ss4_b[par]])

        def st2a(t):
            par = t % 2
            po = PO_[par]
            for l in range(2):
                P.op("dve", TS(ogn[par][:, l * 256:(l + 1) * 256], bank(po)[:, l * 256:(l + 1) * 256], ss4[par][:, 4 + l:5 + l], ALU.mult),
                     reads=[PB[po], ss4_b[par]], writes=[ogn_b[par]])

        def st2b(t):
            tb = slice(t * 128, (t + 1) * 128)
            par = t % 2
            ptb = PT_[par]
            P.op("pe", [TR(bank_b(ptb)[:, kc * 128:(kc + 1) * 128], ogn[par][:, kc * 128:(kc + 1) * 128], ident_b) for kc in range(4)],
                 reads=[ogn_b[par], cst], writes=[PB[ptb]])
            zsl = zgT[:, 4 * p:4 * p + 4, tb]
            P.op("dve", TT(zsl, r3(bank_b(ptb)[:, 0:512], 4), zsl, ALU.mult), reads=[PB[ptb]] + [zg_b[4 * p + m][t] for m in range(4)],
                 writes=[zg_b[4 * p + m][t] for m in range(4)])

        if p == 0:
            for blk in range(2):
                zgw[blk] = stream(win_d, C_ZG + 512 * p + blk * 256, 256)
        if p == 0:
            for w_ in range(2):
                fill_w[w_] = stream(win_d, (C_QG, C_KG)[w_] + 256, 256)
            PRE["vg1"] = [stream(win_d, C_VG + 512 + blk * 256, 256) for blk in range(2)]
        else:
            PRE["F0"] = (stream(wpm_d, 0, 256, kcn=4), stream(win_d, C_GM, 256))
        for i in range(NT + 3):
            if i < NT:
                zg_fill(i)
            if p == 0 and 1 <= i <= NT:
                filler(i - 1)
            if i < NT:
                st0(i)
            if 0 <= i - 1 < NT:
                st1(i - 1)
            if 0 <= i - 2 < NT:
                st2a(i - 2)
            if 0 <= i - 3 < NT:
                st2b(i - 3)

    gla_pass(0)
    if STOP_AFTER == "gla0":
        tap("zgT", zgT)
        return fin()
    gla_pass(1)
    tap("zgT", zgT)
    if STOP_AFTER == "gla":
        return fin()

    A.release(m_gla)
    ARENA_LOG.append(("end-gla", A.peak, A.top))
    ib_F0 = len(ALLBUFS)
    mT = r3(A.tile(BF16, 8 * S), 8)
    mT_b = [[Buf(f"m{m}_{c}") for c in range(NC4)] for m in range(8)]
    woutb = r3(A.tile(BF16, 8 * D), 8)
    woutb_b = Buf("wout")
    gfin_bc = A.tile(F32, D)
    ggl_t = A.tile(F32, 2)
    yt = [A.tile(F32, 512) for _ in range(2)]
    yt_b = [Buf("yt0"), Buf("yt1")]
    yu = [A.tile(BF16, 512) for _ in range(2)]
    yu_b = [Buf("yu0"), Buf("yu1")]
    xs = [A.tile(F32, D) for _ in range(NXB)]
    xs_b = [Buf(f"xsF{i}") for i in range(NXB)]
    oout = [A.tile(F32, D) for _ in range(2)]
    oout_b = [Buf("oout0"), Buf("oout1")]
    fst = A.tile(F32, 3 * NT)
    fst_b = [Buf(f"fst{t}") for t in range(NT)]

    gfin_b = Buf("gfin")
    inherit(ALLBUFS[ib_F0:], ALLBUFS[ib_glap:ib_F0])

    ARENA_LOG.append(("F", A.top))

    def phaseF():
        P.dma("sp", gfin_bc, gfin_d.to_broadcast([128, D]), "d_m5", writes=[gfin_b])
        P.dma("sp", ggl_t, ggla_d.rearrange("o (k p) -> p (o k)", p=128), "d_m6", writes=[gfin_b], allow_slow_non_contiguous=True)
        it = 0
        for blk in range(4):
            if blk == 0 and "F0" in PRE:
                (wm3, wm_b), (wgm3, wgm_b) = PRE["F0"]
            else:
                wm3, wm_b = stream(wpm_d, blk * 256, 256, kcn=4)
                wgm3, wgm_b = stream(win_d, C_GM + blk * 256, 256)
            def gla_weights(blk=blk):
                jg = wctr["w"] % NWB
                wctr["w"] += 1
                wg3 = r3(wbuf[jg][:, 0:8 * 256], 8)
                P.dma("pool", wg3, k3(wpg_d)[:, :, blk * 256:(blk + 1) * 256], "d_" + wbuf_b[jg].name, writes=[wbuf_b[jg]])
                for par in range(2):
                    dst4 = wg3.rearrange("p (k two) n -> p k two n", two=2)[:, :, par, :]
                    P.op("dve", TS(dst4, dst4, ggl_t[:, par:par + 1], ALU.mult), reads=[gfin_b], writes=[wbuf_b[jg]])
                wgg3, wgg_b = stream(win_d, C_GG + blk * 256, 256)
                return jg, wg3, wgg3, wgg_b

            if blk > 0:
                jg, wg3, wgg3, wgg_b = gla_weights()
            for mm in range(2):
                m = blk * 2 + mm
                for c in range(NC4):
                    sl = slice(c * 512, (c + 1) * 512)
                    py, pg = (it % 2) * 2, (it % 2) * 2 + 1
                    k = it % 2
                    it += 1
                    P.op("pe", [MM(bank(py), wm3[:, kc, mm * 128:(mm + 1) * 128], oT[:, kc, sl], start=(kc == 0), stop=(kc == 3)) for kc in range(4)],
                         reads=[wm_b] + [oT_b[h][c] for h in range(8)], writes=[PB[py]])
                    P.op("pe", proj_fns(lambda kc: wgm3[:, kc, mm * 128:(mm + 1) * 128], 128, c, pg), reads=[wgm_b] + hTc(c), writes=[PB[pg]])
                    P.op("act", ACTV(yt[k], bank(pg), AF.Tanh, scale=0.5), reads=[PB[pg]], writes=[yt_b[k]])
                    P.op("dve", STT(mT[:, m, sl], yt[k], 1.0, bank(py), ALU.add, ALU.mult), reads=[PB[py], yt_b[k]], writes=[mT_b[m][c]])
            if blk == 0:
                jg, wg3, wgg3, wgg_b = gla_weights()
            for mm in range(2):
                m = blk * 2 + mm
                for c in range(NC4):
                    sl = slice(c * 512, (c + 1) * 512)
                    py, pg = (it % 2) * 2, (it % 2) * 2 + 1
                    k = it % 2
                    it += 1
                    P.op("pe", [MM(bank(py), wg3[:, kc, mm * 128:(mm + 1) * 128], zgT[:, kc, sl], start=(kc == 0), stop=(kc == 7)) for kc in range(8)],
                         reads=[wbuf_b[jg]] + [zg_b[kc][tt] for kc in range(8) for tt in range(4 * c, 4 * c + 4)], writes=[PB[py]])
                    P.op("pe", proj_fns(lambda kc: wgg3[:, kc, mm * 128:(mm + 1) * 128], 128, c, pg), reads=[wgg_b] + hTc(c), writes=[PB[pg]])
                    P.op("act", ACTV(yt[k], bank(pg), AF.Tanh, scale=0.5), reads=[PB[pg]], writes=[yt_b[k]])
                    P.op("dve", STT(yu[k], yt[k], 1.0, bank(py), ALU.add, ALU.mult), reads=[PB[py], yt_b[k]], writes=[yu_b[k]])
                    P.op("dve", TT(mT[:, m, sl], mT[:, m, sl], yu[k], ALU.add), reads=[yu_b[k], mT_b[m][c]], writes=[mT_b[m][c]])
        for blk in range(4):
            load_block(k3(wout_d)[:, :, blk * 256:(blk + 1) * 256], 8, 256, woutb[:, :, blk * 256:(blk + 1) * 256], woutb_b)
        def f2_A(t):
            i = t % NXB
            P.dma("sp", xs[i], x_d[t * 128:(t + 1) * 128, :], f"d_xs{i}", writes=[xs_b[i]])
            for hf in range(2):
                pb = 4 + hf + 2 * (t % 2)
                P.op("pe", [MM(bank(pb), mT[:, kc, t * 128:(t + 1) * 128], woutb[:, kc, hf * 512:(hf + 1) * 512], start=(kc == 0), stop=(kc == 7)) for kc in range(8)],
                     reads=[woutb_b] + [mT_b[kc][t // 4] for kc in range(8)], writes=[PB[pb]])
                xh = xs[i][:, hf * 512:(hf + 1) * 512]
                P.op("dve", STT(xh, bank(pb), 0.5, xh, ALU.mult, ALU.add), reads=[PB[pb], xs_b[i]], writes=[xs_b[i]])
            ss = fst[:, 3 * t:3 * t + 1]
            ln = fst[:, 3 * t + 1:3 * t + 2]
            rs = fst[:, 3 * t + 2:3 * t + 3]
            P.op("act", ACTV(junk, xs[i], AF.Square, accum_out=ss), reads=[xs_b[i]], writes=[junk_b, fst_b[t]])
            P.op("act", ACTV(ln, ss, AF.Ln, scale=1.0 / D, bias=eps_t), reads=[fst_b[t], cst], writes=[fst_b[t]])
            P.op("act", ACTV(rs, ln, AF.Exp, scale=-0.5), reads=[fst_b[t]], writes=[fst_b[t]])

        def f2_B(t):
            i = t % NXB
            k = t % 2
            rs = fst[:, 3 * t + 2:3 * t + 3]
            P.op("dve", STT(oout[k], xs[i], rs, gfin_bc, ALU.mult, ALU.mult), reads=[xs_b[i], fst_b[t], gfin_b], writes=[oout_b[k]])
            P.dma("pool", out_d[t * 128:(t + 1) * 128, :], oout[k], f"d_out{k}", reads=[oout_b[k]])

        for t in range(NT + 1):
            if t < NT:
                f2_A(t)
            if t - 1 >= 0:
                f2_B(t - 1)

    phaseF()
    tap("mT", mT)

    return fin()


_NC_CACHE = {}


def kernel(x, positions, g_in, w_in, g_q, w_uq, g_kv, w_ukv, w_gla_gate, b_gla_gate,
           g_gla, w_proj_mla, w_proj_gla, w_out, g_final):
    f = lambda a: np.ascontiguousarray(np.asarray(a, dtype=np.float32))
    x = f(x)
    positions = np.ascontiguousarray(np.asarray(positions, dtype=np.int32))
    shared = {
        "g_in": f(g_in).reshape(1, D), "w_in": f(w_in).reshape(D, DIN),
        "g_q": f(g_q).reshape(1, 384), "w_uq": f(w_uq).reshape(384, 768),
        "g_kv": f(g_kv).reshape(1, 256), "w_ukv": f(w_ukv).reshape(256, 1024),
        "w_gla_gate": f(w_gla_gate).reshape(16, 512), "b_gla_gate": f(b_gla_gate).reshape(1, 512),
        "g_gla": f(g_gla).reshape(1, 256), "w_proj_mla": f(w_proj_mla).reshape(512, D),
        "w_proj_gla": f(w_proj_gla).reshape(D, D), "w_out": f(w_out).reshape(D, D),
        "g_final": f(g_final).reshape(1, D),
    }
    if "nc" not in _NC_CACHE:
        _NC_CACHE["nc"] = build_program()
    nc = _NC_CACHE["nc"]
    in_maps = []
    for b in range(8):
        m = dict(shared)
        m["x"] = x[b]
        m["pos"] = positions[b].reshape(1, S)
        in_maps.append(m)
    res = run_bass_kernel_spmd(nc, in_maps, core_ids=list(range(8)))
    _NC_CACHE["last"] = res
    return np.stack([np.asarray(r["out"], dtype=np.float32) for r in res.results], axis=0)
```
